# Optimizing a Trainium2 kernel written in Bass

```python
import math
import jax
import jax.numpy as jnp
from jax import lax
import numpy as np

D_MODEL = 1024
BATCH = 4
SEQ = 4096
DEPTH = 4
DEC_BATCH = 32
DEC_SEQ = 32
PAST_LEN = 1024

CHUNK = 64
N_META = 16
EPS = 1e-6
H_FOX = 6
HD_FOX = 64
D_FOX = H_FOX * HD_FOX
Q_BLOCK = 128
POOL_WINDOWS = (2, 4, 8, 16)
N_POOL_GROUPS = 4
POOL_GROUP_DIM = 64
D_POOL = N_POOL_GROUPS * POOL_GROUP_DIM
POOL_BUF = 15
H_SSD = 6
HD_SSD = 64
D_SSD = H_SSD * HD_SSD
N_BC_GROUPS = 2
D_STATE = 64
CONV_W = 4
D_CONV = D_SSD + 2 * N_BC_GROUPS * D_STATE
D_MIX = D_FOX + D_POOL + D_SSD
SPLIT_SIZES = (D_FOX, D_FOX, D_FOX, H_FOX, D_POOL, D_SSD, D_CONV, H_SSD)
N_IN = D_FOX * 3 + H_FOX + D_POOL + D_SSD + D_CONV + H_SSD
D_FF = ((8 * D_MODEL + 3 * 256 - 1) // (3 * 256)) * 256

kernel_name = "hymba_fox_pool_ssd_streaming_step"


def rmsnorm(x, g):
    xf = x.astype(jnp.float32)
    y = xf * lax.rsqrt(jnp.mean(xf * xf, axis=-1, keepdims=True) + EPS)
    return (y * g.astype(jnp.float32)).astype(x.dtype)


def split_points():
    pts, acc = [], 0
    for s in SPLIT_SIZES[:-1]:
        acc += s
        pts.append(acc)
    return pts


def fox_block(q, k, v, c_q, c_k, q_pos, k_pos):
    s = jnp.einsum('bqhd,bkhd->bhqk', q, k).astype(jnp.float32) * (HD_FOX ** -0.5)
    s = s + jnp.transpose(c_q, (0, 2, 1))[:, :, :, None] - jnp.transpose(c_k, (0, 2, 1))[:, :, None, :]
    mask = (k_pos[None, :] <= q_pos[:, None])[None, None]
    p = jax.nn.softmax(jnp.where(mask, s, -jnp.inf), axis=-1)
    return jnp.einsum('bhqk,bkhd->bqhd', p.astype(v.dtype), v)


def fox_prompt(q, k, v, logf):
    b, t = q.shape[0], q.shape[1]
    c = jnp.cumsum(logf, axis=1)
    nb = -(-t // Q_BLOCK)
    tp = nb * Q_BLOCK
    pad = tp - t
    padt = lambda a: jnp.pad(a, [(0, 0), (0, pad)] + [(0, 0)] * (a.ndim - 2))
    qp, kp, vp, cp = padt(q), padt(k), padt(v), padt(c)
    k_pos = jnp.arange(tp)

    def one_block(i):
        s0 = i * Q_BLOCK
        qb = lax.dynamic_slice_in_dim(qp, s0, Q_BLOCK, axis=1)
        cb = lax.dynamic_slice_in_dim(cp, s0, Q_BLOCK, axis=1)
        return fox_block(qb, kp, vp, cb, cp, s0 + jnp.arange(Q_BLOCK), k_pos)

    out = lax.map(one_block, jnp.arange(nb))
    out = jnp.moveaxis(out, 0, 1).reshape(b, tp, H_FOX, HD_FOX)
    return out[:, :t]


def fox_sample(q, k, v, logf, ck, cv, cl):
    p_len, s_len = ck.shape[1], q.shape[1]
    k_all = jnp.concatenate([ck.astype(k.dtype), k], axis=1)
    v_all = jnp.concatenate([cv.astype(v.dtype), v], axis=1)
    c_all = jnp.cumsum(jnp.concatenate([cl.astype(jnp.float32), logf], axis=1), axis=1)
    return fox_block(q, k_all, v_all, c_all[:, p_len:], c_all,
                     p_len + jnp.arange(s_len), jnp.arange(p_len + s_len))


def pool_mixer(u, prefix, pos0, w_pool, scale):
    b, L, _ = u.shape
    ext = jnp.concatenate([prefix.astype(u.dtype), u], axis=1)
    csum = jnp.pad(jnp.cumsum(ext.astype(jnp.float32), axis=1), ((0, 0), (1, 0), (0, 0)))
    end = csum[:, POOL_BUF + 1:]
    pos = pos0 + jnp.arange(L)
    parts = []
    for g, w in enumerate(POOL_WINDOWS):
        lo, hi = g * POOL_GROUP_DIM, (g + 1) * POOL_GROUP_DIM
        start = csum[:, POOL_BUF + 1 - w:POOL_BUF + 1 - w + L, lo:hi]
        cnt = jnp.minimum(pos + 1, w).astype(jnp.float32)[None, :, None]
        parts.append((end[:, :, lo:hi] - start) / cnt)
    diff = jnp.concatenate(parts, axis=-1) - u.astype(jnp.float32)
    y = jnp.einsum('blgc,gce->blge', diff.reshape(b, L, N_POOL_GROUPS, POOL_GROUP_DIM),
                   w_pool.astype(jnp.float32)).reshape(b, L, D_POOL)
    y = y * scale.astype(jnp.float32)
    return y.astype(u.dtype), ext[:, -POOL_BUF:]


def causal_conv(xbc, prefix, w, bias):
    L = xbc.shape[1]
    ext = jnp.concatenate([prefix.astype(xbc.dtype), xbc], axis=1)
    out = bias
    for j in range(CONV_W):
        out = out + ext[:, j:j + L] * w[j]
    return jax.nn.silu(out), ext[:, -(CONV_W - 1):]


def ssd_scan(xs, dt, a, bm, cm, init, block):
    b, L, h, p = xs.shape
    n = bm.shape[-1]
    nc = L // block
    rep = h // bm.shape[2]
    bh = jnp.repeat(bm, rep, axis=2).reshape(b, nc, block, h, n)
    ch = jnp.repeat(cm, rep, axis=2).reshape(b, nc, block, h, n)
    la = (dt * a).reshape(b, nc, block, h)
    xd = (xs * dt[..., None]).reshape(b, nc, block, h, p)
    a_cs = jnp.cumsum(la, axis=2)
    seg = a_cs[:, :, :, None, :] - a_cs[:, :, None, :, :]
    causal = jnp.tril(jnp.ones((block, block), dtype=bool))[None, None, :, :, None]
    lmat = jnp.exp(jnp.where(causal, seg, -jnp.inf))
    gmat = jnp.einsum('bcqhn,bcshn->bcqsh', ch, bh) * lmat
    y_diag = jnp.einsum('bcqsh,bcshp->bcqhp', gmat, xd)
    decay_end = jnp.exp(a_cs[:, :, -1:, :] - a_cs)
    chunk_states = jnp.einsum('bcshn,bcsh,bcshp->bchpn', bh, decay_end, xd)
    chunk_decay = jnp.exp(a_cs[:, :, -1, :])

    def step(state, inp):
        dec, st = inp
        return dec[:, :, None, None] * state + st, state

    final, prev = lax.scan(step, init, (jnp.moveaxis(chunk_decay, 1, 0), jnp.moveaxis(chunk_states, 1, 0)))
    prev = jnp.moveaxis(prev, 0, 1)
    y_off = jnp.einsum('bcqhn,bchpn->bcqhp', ch, prev) * jnp.exp(a_cs)[..., None]
    return (y_diag + y_off).reshape(b, L, h, p), final


def trunk_layer(x, params, state, pos0):
    (ln_pre_mix, ln_post_mix, ln_pre_ffn, ln_post_ffn, w_in, fox_f_bias, pool_w, pool_scale,
     conv_w, conv_b, dt_bias, a_log, d_skip, ssd_norm, w_out, w_gate, w_up, w_down) = params
    f32 = jnp.float32
    b, L, _ = x.shape
    h = rmsnorm(x, ln_pre_mix)
    proj = h @ w_in
    q, k, v, f_raw, u, z, xbc, dt_raw = jnp.split(proj, split_points(), axis=-1)
    q = q.reshape(b, L, H_FOX, HD_FOX)
    k = k.reshape(b, L, H_FOX, HD_FOX)
    v = v.reshape(b, L, H_FOX, HD_FOX)
    logf = jax.nn.log_sigmoid(f_raw.astype(f32) + fox_f_bias.astype(f32))
    if state is None:
        attn = fox_prompt(q, k, v, logf)
        pool_prefix = jnp.zeros((b, POOL_BUF, D_POOL), x.dtype)
        conv_prefix = jnp.zeros((b, CONV_W - 1, D_CONV), x.dtype)
        ssd_init = jnp.zeros((b, H_SSD, HD_SSD, D_STATE), f32)
        ssd_pad = (-L) % CHUNK
        block = CHUNK
    else:
        ck, cv, cl, pool_prefix, conv_prefix, ssd_init = state
        attn = fox_sample(q, k, v, logf, ck, cv, cl)
        ssd_pad = 0
        block = L
    pool_out, pool_new = pool_mixer(u, pool_prefix, pos0, pool_w, pool_scale)
    xbc_a, conv_new = causal_conv(xbc, conv_prefix, conv_w, conv_b)
    nbc = N_BC_GROUPS * D_STATE
    xs = xbc_a[..., :D_SSD].reshape(b, L, H_SSD, HD_SSD).astype(f32)
    bm = xbc_a[..., D_SSD:D_SSD + nbc].reshape(b, L, N_BC_GROUPS, D_STATE).astype(f32)
    cm = xbc_a[..., D_SSD + nbc:].reshape(b, L, N_BC_GROUPS, D_STATE).astype(f32)
    dt = jax.nn.softplus(dt_raw.astype(f32) + dt_bias.astype(f32))
    a = -jnp.exp(a_log.astype(f32))
    lpad = lambda t: jnp.pad(t, [(0, 0), (ssd_pad, 0)] + [(0, 0)] * (t.ndim - 2))
    y, ssd_new = ssd_scan(lpad(xs), lpad(dt), a, lpad(bm), lpad(cm), ssd_init.astype(f32), block)
    y = y[:, ssd_pad:] + d_skip.astype(f32)[:, None] * xs
    y = y.reshape(b, L, D_SSD) * jax.nn.silu(z.astype(f32))
    ssd_out = rmsnorm(y, ssd_norm).astype(x.dtype)
    mix = jnp.concatenate([attn.reshape(b, L, D_FOX).astype(x.dtype), pool_out.astype(x.dtype), ssd_out], axis=-1)
    x = x + rmsnorm(mix @ w_out, ln_post_mix)
    h = rmsnorm(x, ln_pre_ffn)
    ff = (jax.nn.silu(h @ w_gate) * (h @ w_up)) @ w_down
    x = x + rmsnorm(ff, ln_post_ffn)
    return x, (k, v, logf, pool_new, conv_new, ssd_new)


def setup_inputs(seed: int = 0) -> dict:
    key = jax.random.key(seed)
    ks = jax.random.split(key, 32)
    nrm = lambda i, shape, s=1.0: s * jax.random.normal(ks[i], shape, jnp.float32)
    u01 = jax.random.uniform(ks[20], (DEPTH, H_SSD), jnp.float32)
    dt0 = jnp.exp(u01 * (math.log(0.1) - math.log(0.001)) + math.log(0.001))
    return {
        'x_prompt': nrm(0, (BATCH, SEQ, D_MODEL)),
        'x_sample': nrm(1, (DEC_BATCH, DEC_SEQ, D_MODEL)),
        'cache_fox_k': nrm(2, (DEPTH, DEC_BATCH, PAST_LEN, H_FOX, HD_FOX)),
        'cache_fox_v': nrm(3, (DEPTH, DEC_BATCH, PAST_LEN, H_FOX, HD_FOX)),
        'cache_fox_logf': jax.nn.log_sigmoid(nrm(4, (DEPTH, DEC_BATCH, PAST_LEN, H_FOX)) + 2.0),
        'state_pool': nrm(5, (DEPTH, DEC_BATCH, POOL_BUF, D_POOL)),
        'state_conv': nrm(6, (DEPTH, DEC_BATCH, CONV_W - 1, D_CONV)),
        'state_ssd': nrm(7, (DEPTH, DEC_BATCH, H_SSD, HD_SSD, D_STATE), 0.1),
        'meta_tokens': nrm(8, (N_META, D_MODEL)),
        'ln_pre_mix': 1.0 + nrm(9, (DEPTH, D_MODEL), 0.05),
        'ln_post_mix': 1.0 + nrm(10, (DEPTH, D_MODEL), 0.05),
        'ln_pre_ffn': 1.0 + nrm(11, (DEPTH, D_MODEL), 0.05),
        'ln_post_ffn': 1.0 + nrm(12, (DEPTH, D_MODEL), 0.05),
        'w_in': nrm(13, (DEPTH, D_MODEL, N_IN), D_MODEL ** -0.5),
        'fox_f_bias': nrm(14, (DEPTH, H_FOX), 0.1),
        'pool_w': nrm(15, (DEPTH, N_POOL_GROUPS, POOL_GROUP_DIM, POOL_GROUP_DIM), POOL_GROUP_DIM ** -0.5),
        'pool_scale': 1.0 + nrm(16, (DEPTH, D_POOL), 0.05),
        'conv_w': nrm(17, (DEPTH, CONV_W, D_CONV), CONV_W ** -0.5),
        'conv_b': nrm(18, (DEPTH, D_CONV), 0.01),
        'dt_bias': dt0 + jnp.log(-jnp.expm1(-dt0)),
        'a_log': jnp.log(jax.random.uniform(ks[21], (DEPTH, H_SSD), jnp.float32, 1.0, 16.0)),
        'd_skip': 1.0 + nrm(22, (DEPTH, H_SSD), 0.05),
        'ssd_norm': 1.0 + nrm(23, (DEPTH, D_SSD), 0.05),
        'w_out': nrm(24, (DEPTH, D_MIX, D_MODEL), D_MIX ** -0.5),
        'w_gate': nrm(25, (DEPTH, D_MODEL, D_FF), D_MODEL ** -0.5),
        'w_up': nrm(26, (DEPTH, D_MODEL, D_FF), D_MODEL ** -0.5),
        'w_down': nrm(27, (DEPTH, D_FF, D_MODEL), D_FF ** -0.5),
    }


def reference(x_prompt, x_sample, cache_fox_k, cache_fox_v, cache_fox_logf, state_pool, state_conv,
              state_ssd, meta_tokens, ln_pre_mix, ln_post_mix, ln_pre_ffn, ln_post_ffn, w_in,
              fox_f_bias, pool_w, pool_scale, conv_w, conv_b, dt_bias, a_log, d_skip, ssd_norm,
              w_out, w_gate, w_up, w_down):
    weights = (ln_pre_mix, ln_post_mix, ln_pre_ffn, ln_post_ffn, w_in, fox_f_bias, pool_w, pool_scale,
               conv_w, conv_b, dt_bias, a_log, d_skip, ssd_norm, w_out, w_gate, w_up, w_down)
    past = cache_fox_k.shape[2]
    meta = jnp.broadcast_to(meta_tokens.astype(x_prompt.dtype)[None],
                            (x_prompt.shape[0], N_META, D_MODEL))
    xp = jnp.concatenate([meta, x_prompt], axis=1)
    xs = x_sample
    new_p, new_s = [], []
    for l in range(DEPTH):
        p_l = tuple(w[l] for w in weights)
        xp, st_p = trunk_layer(xp, p_l, None, 0)
        xs, st_s = trunk_layer(xs, p_l, (cache_fox_k[l], cache_fox_v[l], cache_fox_logf[l],
                                         state_pool[l], state_conv[l], state_ssd[l]), past)
        new_p.append(st_p)
        new_s.append(st_s)
    stk = lambda lst, i: jnp.stack([s[i] for s in lst], axis=0)
    y_prompt = xp[:, N_META:]
    return (y_prompt, xs,
            stk(new_p, 0), stk(new_p, 1), stk(new_p, 2), stk(new_p, 3), stk(new_p, 4), stk(new_p, 5),
            stk(new_s, 0), stk(new_s, 1), stk(new_s, 2), stk(new_s, 3), stk(new_s, 4), stk(new_s, 5))
```

```python
import numpy as np
import ml_dtypes
from contextlib import ExitStack
import concourse.bass as bass
import concourse.mybir as mybir
from concourse.bass_utils import run_bass_kernel_spmd
from concourse.alu_op_type import AluOpType as ALU

F32 = mybir.dt.float32
BF16 = mybir.dt.bfloat16
AF = mybir.ActivationFunctionType
NEG = -30000.0
D = 1024
NIN = 2444
DFF = 2816
NFC = DFF // 128
EPS = 1e-6
META = 16
DSEQ = 32
NSEQ = 4
OQ, OKK, OV, OF, ODT, OZ, OU, OX = 0, 384, 768, 1152, 1158, 1164, 1548, 1804
TM0 = 384
B_PM, B_PF, B_SN, B_FB, B_AL, B_DS, NBC = 0, 1024, 2048, 2432, 2444, 2450, 2456
C_GM, C_GF, C_PS, C_CW, C_CB, NCOL = 0, 8, 16, 18, 38, 43
K_ID, K_TRI, K_ONE, K_S16, K_S32, K_S128, K_MSK, K_RC, K_Z, K_1, K_EPS, NCF = 0, 128, 256, 384, 512, 640, 768, 896, 928, 929, 930, 931
KB_ID, KB_MSK, KB_TRI, KB_S16, KB_S32, KB_S128, NCB = 0, 128, 256, 384, 512, 640, 768
ENGS = ("pe", "act", "dve", "pool")


class U:
    __slots__ = ("name", "w", "r", "excl")

    def __init__(self, name, excl=False):
        self.name = name
        self.w = None
        self.r = {}
        self.excl = excl


class _Rec:
    def __init__(self):
        self.calls = []

    def __getattr__(self, name):
        def f(*a, **kw):
            self.calls.append((name, a, kw))
            return None
        return f


class Prog:
    def __init__(self):
        self.q = {e: [] for e in ENGS + ("sp",)}
        self.cnt = {e: 0 for e in ENGS}
        self.waited = {e: {} for e in ENGS + ("sp",)}
        self.dcnt = {}

    def _waits(self, eng, reads, writes):
        waits = []
        wd = self.waited[eng]

        def need(ev):
            if ev is None:
                return
            k, v = ev
            if wd.get(k, 0) >= v:
                return
            wd[k] = v
            waits.append((k, v))

        for u in reads:
            need(u.w)
        for u in writes:
            need(u.w)
            for k, v in u.r.items():
                need((k, v))
        return waits

    def op(self, eng, fn, reads=(), writes=()):
        ex = [u for u in reads if u.excl]
        if ex:
            reads = [u for u in reads if not u.excl]
            writes = list(writes) + ex
        if callable(fn):
            rec = _Rec()
            fn(rec)
            fn = rec.calls
        assert isinstance(fn, list) and len(fn) > 0, fn
        waits = self._waits(eng, reads, writes)
        self.cnt[eng] += 1
        ev = (eng, self.cnt[eng])
        self.q[eng].append((waits, fn, ev, 1))
        for u in reads:
            if u.r.get(eng, 0) < ev[1]:
                u.r[eng] = ev[1]
        for u in writes:
            u.w = ev
            u.r = {}

    def dma(self, fn, reads, writes, sem):
        waits = self._waits("sp", reads, writes)
        prev = self.dcnt.get(sem, 0)
        if prev > self.waited["sp"].get(sem, 0):
            self.waited["sp"][sem] = prev
            waits.append((sem, prev))
        self.dcnt[sem] = prev + 16
        ev = (sem, self.dcnt[sem])
        self.q["sp"].append((waits, fn, ev, 16))
        for u in reads:
            if u.r.get(sem, 0) < ev[1]:
                u.r[sem] = ev[1]
        for u in writes:
            u.w = ev
            u.r = {}

    def barrier(self):
        for e in ENGS + ("sp",):
            waits = []
            wd = self.waited[e]
            for f in ENGS:
                if f != e and self.cnt[f] > wd.get(f, 0):
                    wd[f] = self.cnt[f]
                    waits.append((f, self.cnt[f]))
            for s, v in self.dcnt.items():
                if v > wd.get(s, 0):
                    wd[s] = v
                    waits.append((s, v))
            if waits:
                self.q[e].append((waits, None, None, 0))


class Buf:
    __slots__ = ("v", "u", "name")

    def __init__(self, v, name):
        self.v = v
        self.u = U(name)
        self.name = name


class Arena:
    def __init__(self, ap, nwords):
        self.ap = ap
        self.nwords = nwords
        self.off = 0
        self.peak = 0

    def alloc(self, name, shape, dt):
        n = 1
        for s in shape:
            n *= s
        words = n if dt == F32 else (n + 1) // 2
        assert self.off + words <= self.nwords, f"arena overflow at {name}: {self.off + words} > {self.nwords}"
        v = self.ap[:, self.off:self.off + words]
        self.off += words
        self.peak = max(self.peak, self.off)
        if dt == BF16:
            v = v.bitcast(BF16)
            if 2 * words != n:
                v = v[:, 0:n]
        if len(shape) == 2:
            v = v.rearrange("p (a b) -> p a b", a=shape[0])
        elif len(shape) == 3:
            v = v.rearrange("p (a b c) -> p a b c", a=shape[0], b=shape[1])
        return Buf(v, name)


class _Stop(Exception):
    pass


def build(SEQ, PAST, DEPTH, stop_at=None):
    T = META + SEQ
    stage = [0]

    def mark_stage(name):
        stage[0] += 1
        if stop_at is not None:
            print("stage", stage[0], name, flush=True)
            if stage[0] >= stop_at:
                raise _Stop()

    NPT = SEQ // 256
    NKB_P = 1 + SEQ // 128
    NCB_S = PAST // 128
    NKB = max(NKB_P, NCB_S + 1)
    NROW = T + NSEQ * DSEQ
    nc = bass.Bass("TRN2", target_bir_lowering=False)

    def din(name, shape, dt=F32):
        return nc.dram_tensor(name, list(shape), dt, kind="ExternalInput").ap()

    def dout(name, shape):
        return nc.dram_tensor(name, list(shape), F32, kind="ExternalOutput").ap()

    def dscr(name, shape, dt=F32):
        return nc.dram_tensor(name, list(shape), dt, kind="Internal").ap()

    I = dict(
        xp=din("xp", [SEQ, D]), meta=din("meta", [META, D]), xs=din("xs", [NSEQ * DSEQ, D]),
        ck=din("ck", [DEPTH, NSEQ, PAST, 384]), cv=din("cv", [DEPTH, NSEQ, PAST, 384]),
        cl=din("cl", [DEPTH, NSEQ, PAST, 6]), spool=din("spool", [DEPTH, NSEQ, 15, 256]),
        sconv=din("sconv", [DEPTH, NSEQ, 3, 640]), sssd=din("sssd", [DEPTH, NSEQ, 6, 64, 64]),
        w_in=din("w_in", [DEPTH, D, NIN]), w_out=din("w_out", [DEPTH, D, D]),
        w_gate=din("w_gate", [DEPTH, D, DFF]), w_up=din("w_up", [DEPTH, D, DFF]),
        w_down=din("w_down", [DEPTH, DFF, D]),
        pbc=din("pbc", [DEPTH, 128, NBC]), pcol=din("pcol", [DEPTH, 128, NCOL]),
        poolw=din("poolw", [DEPTH, 128, 256]),
        cstf=din("cstf", [128, NCF]), cstb=din("cstb", [128, NCB], BF16),
    )
    O = dict(
        y_p=dout("y_p", [SEQ, D]), y_s=dout("y_s", [NSEQ * DSEQ, D]),
        k_p=dout("k_p", [DEPTH, T, 384]), v_p=dout("v_p", [DEPTH, T, 384]), l_p=dout("l_p", [DEPTH, T, 6]),
        pool_p=dout("pool_p", [DEPTH, 15, 256]), conv_p=dout("conv_p", [DEPTH, 3, 640]),
        ssd_p=dout("ssd_p", [DEPTH, 6, 64, 64]),
        k_s=dout("k_s", [DEPTH, NSEQ, DSEQ, 384]), v_s=dout("v_s", [DEPTH, NSEQ, DSEQ, 384]),
        l_s=dout("l_s", [DEPTH, NSEQ, DSEQ, 6]), pool_s=dout("pool_s", [DEPTH, NSEQ, 15, 256]),
        conv_s=dout("conv_s", [DEPTH, NSEQ, 3, 640]), ssd_s=dout("ssd_s", [DEPTH, NSEQ, 6, 64, 64]),
    )
    XA = dscr("xa", [NROW, D])
    XB = dscr("xb", [NROW, D])
    KSC = dscr("ksc", [NKB_P, 128, 768], BF16)
    VSC = dscr("vsc", [NKB_P, 128, 576], BF16)
    uXA = [U(f"xa{i}") for i in range((NROW + 127) // 128 + 1)]
    uXB = [U(f"xb{i}") for i in range((NROW + 127) // 128 + 1)]
    uKSC = [U(f"ksc{i}") for i in range(NKB_P)]
    uIN = U("inputs")
    uOUT = U("outputs")

    P = Prog()
    es = ExitStack()
    ARW = 53100
    arena_t = es.enter_context(nc.sbuf_tensor("arena", [128, ARW], F32))
    AR = Arena(arena_t[:, :], ARW)
    pb0 = es.enter_context(nc.psum_tensor("pb0", [128, 1024], BF16))
    pbs = [pb0] + [es.enter_context(nc.psum_tensor(f"pb{i}", [128, 512], F32)) for i in range(1, 8)]
    pb0f = pb0[:, :].bitcast(F32)
    pu = []
    for i in range(8):
        _u = U(f"ps{i}", excl=True)
        pu.append([_u, _u])

    def psu(bank, c0, c1):
        half = 512 if bank == 0 else 256
        us = []
        if c0 < half:
            us.append(pu[bank][0])
        if c1 > half:
            us.append(pu[bank][1])
        return us

    CF = AR.alloc("cstf", [NCF], F32)
    CB = AR.alloc("cstb", [NCB], BF16)
    PBC = AR.alloc("pbc", [NBC], F32)
    PCOL = AR.alloc("pcol", [NCOL], F32)
    PWF = AR.alloc("pwf", [256], F32)
    PWB = AR.alloc("pwb", [2, 128], BF16)
    ABC = AR.alloc("abc", [6], F32)
    NST = 2
    WCH = 512
    WST = [AR.alloc(f"wst{i}", [WCH], F32) for i in range(NST)]
    XIN = [AR.alloc(f"xin{i}", [D], F32) for i in range(4)]
    XOUT = [AR.alloc(f"xout{i}", [D], F32) for i in range(2)]
    HBF = [AR.alloc(f"hbf{i}", [D], BF16) for i in range(2)]
    HT = AR.alloc("hT", [8, 256], BF16)
    _t1off = AR.off
    T1 = AR.alloc("t1", [D], F32)
    JUNK = Buf(AR.ap[:, _t1off:_t1off + D // 2].bitcast(BF16), "junk")
    JUNK.u = T1.u
    SM = AR.alloc("small", [64], F32)
    smu = [U(f"sm{i}") for i in range(8)]
    mark = AR.off

    ident_f = CF.v[:, K_ID:K_ID + 128]
    tri_f = CF.v[:, K_TRI:K_TRI + 128]
    ones_f = CF.v[:, K_ONE:K_ONE + 128]
    mask_f = CF.v[:, K_MSK:K_MSK + 128]
    zcol = CF.v[:, K_Z:K_Z + 1]
    ecol = CF.v[:, K_EPS:K_EPS + 1]
    ident_b = CB.v[:, KB_ID:KB_ID + 128]
    mask_b = CB.v[:, KB_MSK:KB_MSK + 128]
    tri_b = CB.v[:, KB_TRI:KB_TRI + 128]

    def sel_f(n):
        o = {16: K_S16, 32: K_S32, 128: K_S128}[n]
        return CF.v[:, o:o + 128]

    def sel_b(n):
        o = {16: KB_S16, 32: KB_S32, 128: KB_S128}[n]
        return CB.v[:, o:o + 128]

    dma_q = [0]

    def DMA(out, in_, reads, writes, sem):
        P.dma([("dma_start", (), dict(out=out, in_=in_))], reads, writes, sem)

    DMA(CF.v, I["cstf"][:, :], [uIN], [CF.u], "c_cf")
    DMA(CB.v, I["cstb"][:, :], [uIN], [CB.u], "c_cb")

    wst_ctr = [0]
    STG = list(WST)
    for i_ in range(4):
        STG.append(Buf(XIN[i_].v[:, 0:WCH], f"xstg{i_}a"))
        STG.append(Buf(XIN[i_].v[:, WCH:2 * WCH], f"xstg{i_}b"))
    CAST_ENG = ("pool", "dve", "act")

    def load_cast(dst_view, dst_u, src_ap, ncols, gcol=None):
        st = STG[wst_ctr[0] % len(STG)]
        eng = CAST_ENG[wst_ctr[0] % 3]
        dst_u = U("wchunk")
        wst_ctr[0] += 1
        DMA(st.v[:, 0:ncols], src_ap, [uIN], [st.u], st.name)
        if eng == "act":
            if gcol is None:
                P.op("act", lambda e: e.activation(out=dst_view, in_=st.v[:, 0:ncols], func=AF.Copy), [st.u], [dst_u])
            else:
                P.op("act", lambda e: e.activation(out=dst_view, in_=st.v[:, 0:ncols], func=AF.Copy, scale=gcol), [st.u, PCOL.u], [dst_u])
        elif gcol is None:
            P.op(eng, lambda e: e.tensor_copy(out=dst_view, in_=st.v[:, 0:ncols]), [st.u], [dst_u])
        else:
            P.op(eng, lambda e: e.tensor_scalar(out=dst_view, in0=st.v[:, 0:ncols], scalar1=gcol, scalar2=zcol, op0=ALU.mult, op1=ALU.add),
                 [st.u, PCOL.u, CF.u], [dst_u])

    class Seq:
        pass

    seqs = []
    sp = Seq()
    sp.kind = "p"
    sp.idx = 0
    sp.tiles = [(0, META)] + [(META + 256 * j, 256) for j in range(NPT)]
    sp.npast = 0
    sp.row0 = 0
    sp.kblocks = [(0, META)] + [(META + 128 * m, 128) for m in range(SEQ // 128)]
    seqs.append(sp)
    for s in range(NSEQ):
        q = Seq()
        q.kind = "s"
        q.idx = s
        q.tiles = [(0, DSEQ)]
        q.npast = PAST
        q.row0 = T + DSEQ * s
        q.kblocks = [(128 * m, 128) for m in range(NCB_S)] + [(PAST, DSEQ)]
        seqs.append(q)

    def xsrc(L, sq, t, nb):
        if L == 0:
            if sq.kind == "p":
                if t < META:
                    return I["meta"][t:t + nb, :], uIN
                return I["xp"][t - META:t - META + nb, :], uIN
            return I["xs"][DSEQ * sq.idx + t:DSEQ * sq.idx + t + nb, :], uIN
        r = sq.row0 + t
        return XB[r:r + nb, :], uXB[r // 128]

    try:
      for L in range(DEPTH):
          P.barrier()
          AR.off = mark
          WIN = AR.alloc("win", [8, NIN], BF16)
          WOUT = AR.alloc("wout", [8, D], BF16)
          NRING = 3
          KVRk = [AR.alloc(f"kvrk{i}", [6, 128], BF16) for i in range(NRING)]
          KVRv = [AR.alloc(f"kvrv{i}", [576], BF16) for i in range(NRING)]
          QT = AR.alloc("qT", [6, 256], BF16)
          KTC = AR.alloc("kTc", [6, 256], BF16)
          VCUR = AR.alloc("vcur", [2, 576], BF16)
          KVST = [AR.alloc(f"kvst{i}", [768], F32) for i in range(2)]
          LFST = [AR.alloc(f"lfst{i}", [6], F32) for i in range(2)]
          FD = AR.alloc("fd", [12], F32)
          E12 = AR.alloc("e12", [12], F32)
          L12 = [AR.alloc(f"l12{i}", [12], F32) for i in range(2)]
          LA_ = AR.alloc("la", [6], F32)
          NACS = AR.alloc("nacs", [6], F32)
          EA = AR.alloc("ea", [6], F32)
          DEC = AR.alloc("dec", [6], F32)
          NEGC = AR.alloc("negc", [NKB, 6], F32)
          STA = AR.alloc("sta", [4, 128], BF16)
          CLT = AR.alloc("clt", [6], F32)
          ZS = AR.alloc("zs", [2, 384], F32)
          ZE = AR.alloc("ze", [384], F32)
          UT = AR.alloc("uT", [2, 15 + 256], F32)
          PS2 = AR.alloc("ps2", [15 + 256], F32)
          PS4 = AR.alloc("ps4", [15 + 256], F32)
          DIFF = AR.alloc("diff", [2, 256], BF16)
          XBC = AR.alloc("xbcT", [5, 3 + 256], F32)
          ACC = AR.alloc("acc", [256], F32)
          CE = AR.alloc("ce", [256], F32)
          XS = AR.alloc("xsT", [3, 256], F32)
          BT = AR.alloc("bT", [256], BF16)
          CT = AR.alloc("cT", [256], BF16)
          XX = AR.alloc("xx", [6, 128], F32)
          LT = AR.alloc("lt", [6, 128], F32)
          MT = AR.alloc("mt", [6, 128], BF16)
          XSTOK = AR.alloc("xstok", [384], F32)
          XD = AR.alloc("xd", [6, 64], BF16)
          BW = AR.alloc("bw", [6, 128], BF16)
          Y1 = AR.alloc("y1", [384], F32)
          Y2 = AR.alloc("y2", [384], F32)
          YN = AR.alloc("yn", [384], BF16)
          SST = AR.alloc("sst", [192], F32)
          SBF = AR.alloc("sbf", [192], BF16)
          PT = [AR.alloc(f"pt{i}", [256], BF16) for i in range(4)]
          REC = [AR.alloc(f"rec{i}", [256], F32) for i in range(2)]
          MIXT = AR.alloc("mixT", [8, 256], BF16)
          CKS = [AR.alloc(f"cks{i}", [384], F32) for i in range(2)]
          CVS = [AR.alloc(f"cvs{i}", [384], F32) for i in range(2)]
          KB16 = AR.alloc("kb16", [384], BF16)
          STT = AR.alloc("stt", [256], F32)
          SIN = AR.alloc("sin", [3, 128], F32)

          DMA(PBC.v, I["pbc"][L], [uIN], [PBC.u], "c_pbc")
          DMA(PCOL.v, I["pcol"][L], [uIN], [PCOL.u], "c_pcol")
          DMA(PWF.v, I["poolw"][L], [uIN], [PWF.u], "c_pwf")
          P.op("pool", lambda e: e.tensor_copy(out=PWB.v.rearrange("p a b -> p (a b)"), in_=PWF.v), [PWF.u], [PWB.u])
          P.op("act", lambda e: e.activation(out=ABC.v, in_=PBC.v[:, B_AL:B_AL + 6], func=AF.Exp), [PBC.u], [ABC.u])
          P.op("act", lambda e: e.activation(out=ABC.v, in_=ABC.v, func=AF.Copy, scale=-1.0), [ABC.u], [ABC.u])
          for kc in range(8):
              for c0 in range(0, NIN, WCH):
                  c1 = min(NIN, c0 + WCH)
                  load_cast(WIN.v[:, kc, c0:c1], WIN.u, I["w_in"][L, kc * 128:(kc + 1) * 128, c0:c1], c1 - c0,
                            PCOL.v[:, C_GM + kc:C_GM + kc + 1])
          for kc in range(8):
              for c0 in range(0, D, WCH):
                  load_cast(WOUT.v[:, kc, c0:c0 + WCH], WOUT.u, I["w_out"][L, kc * 128:(kc + 1) * 128, c0:c0 + WCH], WCH)
          P.barrier()
          for b_ in [QT, KTC] + KVRk:
              P.op("pool", lambda e, b_=b_: e.memset(b_.v, 0.0), [], [b_.u])
          for b_ in [KTC] + KVRk:
              P.op("pool", lambda e, b_=b_: e.memset(b_.v[64:65, :, :], 1.0), [], [b_.u])
          P.op("pool", lambda e: e.memset(VCUR.v, 1.0), [], [VCUR.u])
          for b_ in KVRv:
              P.op("pool", lambda e, b_=b_: e.memset(b_.v, 1.0), [], [b_.u])
          P.op("pool", lambda e: e.memset(STA.v, 0.0), [], [STA.u])
          P.op("pool", lambda e: e.memset(BW.v, 0.0), [], [BW.u])
          mark_stage('A-weights')

          xin_ctr = [0]
          kvst_ctr = [0]
          ring_ctr = [0]
          pt_ctr = [0]
          st_ctr = [0]
          fm_ctr = [0]
          xo_ctr = [0]
          cks_ctr = [0]

          def do_step1(sq, t0, n):
              blocks = [(o, min(128, n - o)) for o in range(0, n, 128)]
              xins = []
              for bi, (o, nb) in enumerate(blocks):
                  xin = XIN[xin_ctr[0] % 4]
                  xin_ctr[0] += 1
                  xins.append(xin)
                  src, su = xsrc(L, sq, t0 + o, nb)
                  DMA(xin.v[0:nb, :], src, [su], [xin.u], xin.name)
                  hb = HBF[bi % 2]
                  ssq = SM.v[0:nb, 0:1]
                  rs = SM.v[0:nb, 1:2]
                  P.op("act", lambda e, xin=xin, nb=nb, ssq=ssq: e.activation(out=JUNK.v[0:nb, :], in_=xin.v[0:nb, :], func=AF.Square, accum_out=ssq),
                       [xin.u], [JUNK.u, smu[0]])
                  P.op("act", lambda e, nb=nb, ssq=ssq, rs=rs: e.activation(out=rs, in_=ssq, func=AF.Ln, bias=CF.v[0:nb, K_EPS:K_EPS + 1], scale=1.0 / D), [smu[0], CF.u], [smu[0]])
                  P.op("act", lambda e, rs=rs: e.activation(out=rs, in_=rs, func=AF.Exp, scale=-0.5), [smu[0]], [smu[0]])
                  P.op("dve", lambda e, hb=hb, xin=xin, nb=nb, rs=rs: e.tensor_scalar(out=hb.v[0:nb, :], in0=xin.v[0:nb, :], scalar1=rs, scalar2=CF.v[0:nb, K_Z:K_Z + 1], op0=ALU.mult, op1=ALU.add),
                       [xin.u, smu[0], CF.u], [hb.u])

                  def tr(e, hb=hb, nb=nb):
                      r = None
                      for kc in range(8):
                          r = e.transpose(out=pbs[0][:, kc * 128:kc * 128 + nb], in_=hb.v[0:nb, kc * 128:(kc + 1) * 128], identity=ident_b[0:nb, 0:nb])
                      return r
                  P.op("pe", tr, [hb.u, CB.u], pu[0])
                  P.op("act", lambda e, o=o, nb=nb: e.activation(out=HT.v[:, :, o:o + nb], in_=pbs[0][:, :].rearrange("p (a b) -> p a b", a=8)[:, :, 0:nb], func=AF.Copy),
                       pu[0], [HT.u])

              return xins

          s1cache = {}

          for sq in seqs:
              isP = sq.kind == "p"
              nkb = len(sq.kblocks)
              if isP:
                  P.op("pool", lambda e: e.memset(UT.v[:, :, 0:15], 0.0), [], [UT.u])
                  P.op("pool", lambda e: e.memset(XBC.v[:, :, 0:3], 0.0), [], [XBC.u])
                  P.op("pool", lambda e: e.memset(SST.v, 0.0), [], [SST.u])
                  P.op("pool", lambda e: e.memset(SBF.v, 0.0), [], [SBF.u])
              else:
                  s = sq.idx
                  DMA(STT.v[0:15, 0:256], I["spool"][L, s], [uIN], [STT.u], "c_stt")
                  for pt in range(2):
                      P.op("pe", lambda e, pt=pt: e.transpose(out=pbs[7][:, 256 + 16 * pt:256 + 16 * pt + 15], in_=STT.v[0:15, 128 * pt:128 * pt + 128], identity=ident_f[0:15, 0:15]),
                           [STT.u, CF.u], psu(7, 256, 512))
                  P.op("act", lambda e: e.activation(out=UT.v[:, 0, 0:15], in_=pbs[7][:, 256:271], func=AF.Copy), psu(7, 256, 512), [UT.u])
                  P.op("act", lambda e: e.activation(out=UT.v[:, 1, 0:15], in_=pbs[7][:, 272:287], func=AF.Copy), psu(7, 256, 512), [UT.u])
                  DMA(T1.v[0:3, 0:640], I["sconv"][L, s], [uIN], [T1.u], "c_t1")
                  for ct in range(5):
                      P.op("pe", lambda e, ct=ct: e.transpose(out=pbs[7][:, 4 * ct:4 * ct + 3], in_=T1.v[0:3, 128 * ct:128 * ct + 128], identity=ident_f[0:3, 0:3]),
                           [T1.u, CF.u], psu(7, 0, 256))
                  for ct in range(5):
                      P.op("act", lambda e, ct=ct: e.activation(out=XBC.v[:, ct, 0:3], in_=pbs[7][:, 4 * ct:4 * ct + 3], func=AF.Copy), psu(7, 0, 256), [XBC.u])
                  for j in range(3):
                      DMA(SIN.v[0:64, j, 0:64], I["sssd"][L, s, j], [uIN], [SIN.u], f"c_sin{j}a")
                      DMA(SIN.v[0:64, j, 64:128], I["sssd"][L, s, j + 3], [uIN], [SIN.u], f"c_sin{j}b")
                  for j in range(3):
                      P.op("pe", lambda e, j=j: e.transpose(out=pbs[6][:, 64 * j:64 * j + 64], in_=SIN.v[0:64, j, :], identity=ident_f[0:64, 0:64]),
                           [SIN.u, CF.u], psu(6, 0, 256))
                  P.op("act", lambda e: e.activation(out=SST.v, in_=pbs[6][:, 0:192], func=AF.Copy), psu(6, 0, 256), [SST.u])
                  P.op("dve", lambda e: e.tensor_copy(out=SBF.v, in_=SST.v), [SST.u], [SBF.u])

              if not isP:
                  for m in range(NCB_S):
                      DMA(CLT.v, I["cl"][L, s, 128 * m:128 * m + 128, :], [uIN], [CLT.u], "c_clt")
                      P.op("dve", lambda e: e.tensor_single_scalar(out=CLT.v, in_=CLT.v, scalar=-1.0, op=ALU.mult), [CLT.u], [CLT.u])

                      def cs_mm(e, m=m):
                          r = e.matmul(pbs[6][:, 384:390], lhsT=tri_f, rhs=CLT.v, start=True, stop=(m == 0))
                          if m > 0:
                              r = e.matmul(pbs[6][:, 384:390], lhsT=sel_f(128), rhs=NEGC.v[:, m - 1, :], start=False, stop=True)
                          return r
                      P.op("pe", cs_mm, [CLT.u, CF.u, NEGC.u], psu(6, 256, 512))
                      P.op("act", lambda e, m=m: e.activation(out=NEGC.v[:, m, :], in_=pbs[6][:, 384:390], func=AF.Copy), psu(6, 256, 512), [NEGC.u])

              for ti, (t0, n) in enumerate(sq.tiles):
                  blocks = [(o, min(128, n - o)) for o in range(0, n, 128)]
                  kb_first = (sq.npast // 128) + (0 if (not isP) else (0 if ti == 0 else 1 + 2 * (ti - 1)))
                  if isP:
                      kbs = [0] if ti == 0 else [1 + 2 * (ti - 1), 2 + 2 * (ti - 1)]
                  else:
                      kbs = [NCB_S]
                  key_ = (sq.kind, sq.idx, ti)
                  xins = s1cache.pop(key_) if key_ in s1cache else do_step1(sq, t0, n)

                  mark_stage('step1')
                  l12s = []
                  for bi, (o, nb) in enumerate(blocks):
                      kb = kbs[bi]

                      def tm(e, o=o, nb=nb):
                          r = None
                          for (c0, c1, bank, pc) in ((0, 384, 3, 0), (384, 768, 4, 0), (768, 780, 5, 0), (780, 1164, 5, 128)):
                              for kc in range(8):
                                  r = e.matmul(pbs[bank][0:nb, pc:pc + c1 - c0], lhsT=HT.v[:, kc, o:o + nb], rhs=WIN.v[:, kc, TM0 + c0:TM0 + c1], start=(kc == 0), stop=(kc == 7))
                          return r
                      P.op("pe", tm, [HT.u, WIN.u], pu[3] + pu[4] + pu[5])
                      kv = KVST[kvst_ctr[0] % 2]
                      lf = LFST[kvst_ctr[0] % 2]
                      l12 = L12[kvst_ctr[0] % 2]
                      kvst_ctr[0] += 1
                      l12s.append(l12)
                      P.op("act", lambda e, kv=kv, nb=nb: e.activation(out=kv.v[0:nb, 0:384], in_=pbs[3][0:nb, 0:384], func=AF.Copy), pu[3], [kv.u])
                      P.op("dve", lambda e, kv=kv, nb=nb: e.tensor_copy(out=kv.v[0:nb, 384:768], in_=pbs[4][0:nb, 0:384]), pu[4], [kv.u])
                      vsrc = kv.v[0:nb, 384:768].rearrange("p (j t d) -> p j t d", j=3, t=2)
                      vdst = VCUR.v[0:nb, bi, :].rearrange("p (j x) -> p j x", j=3)
                      P.op("pool", lambda e, vsrc=vsrc, vdst=vdst: e.tensor_copy(out=vdst[:, :, 0:64], in_=vsrc[:, :, 0, :]), [kv.u], [VCUR.u])
                      P.op("pool", lambda e, vsrc=vsrc, vdst=vdst: e.tensor_copy(out=vdst[:, :, 128:192], in_=vsrc[:, :, 1, :]), [kv.u], [VCUR.u])
                      if isP:
                          ko, vo, lo = O["k_p"][L, t0 + o:t0 + o + nb, :], O["v_p"][L, t0 + o:t0 + o + nb, :], O["l_p"][L, t0 + o:t0 + o + nb, :]
                      else:
                          ko, vo, lo = O["k_s"][L, sq.idx, o:o + nb, :], O["v_s"][L, sq.idx, o:o + nb, :], O["l_s"][L, sq.idx, o:o + nb, :]
                      DMA(ko, kv.v[0:nb, 0:384], [kv.u], [uOUT], kv.name + "_k")
                      DMA(vo, kv.v[0:nb, 384:768], [kv.u], [uOUT], kv.name + "_v")
                      P.op("dve", lambda e, nb=nb: e.tensor_tensor(out=FD.v[0:nb, :], in0=pbs[5][0:nb, 0:12], in1=PBC.v[0:nb, B_FB:B_FB + 12], op=ALU.add), psu(5, 0, 12) + [PBC.u], [FD.u])
                      P.op("act", lambda e, nb=nb: e.activation(out=E12.v[0:nb, 0:6], in_=FD.v[0:nb, 0:6], func=AF.Exp, scale=-1.0), [FD.u], [E12.u])
                      P.op("act", lambda e, nb=nb: e.activation(out=E12.v[0:nb, 6:12], in_=FD.v[0:nb, 6:12], func=AF.Exp), [FD.u], [E12.u])
                      P.op("act", lambda e, nb=nb, l12=l12: e.activation(out=l12.v[0:nb, :], in_=E12.v[0:nb, :], func=AF.Ln, bias=1.0), [E12.u], [l12.u])
                      P.op("dve", lambda e, nb=nb, l12=l12, lf=lf: e.tensor_single_scalar(out=lf.v[0:nb, :], in_=l12.v[0:nb, 0:6], scalar=-1.0, op=ALU.mult), [l12.u], [lf.u])
                      DMA(lo, lf.v[0:nb, :], [lf.u], [uOUT], lf.name)
                      P.op("act", lambda e, nb=nb: e.activation(out=ZE.v[0:nb, :], in_=pbs[5][0:nb, 128:512], func=AF.Exp, scale=-1.0), pu[5], [ZE.u])
                      P.op("act", lambda e, nb=nb: e.activation(out=ZE.v[0:nb, :], in_=ZE.v[0:nb, :], func=AF.Ln, bias=1.0), [ZE.u], [ZE.u])
                      P.op("act", lambda e, nb=nb: e.activation(out=ZE.v[0:nb, :], in_=ZE.v[0:nb, :], func=AF.Exp, scale=-1.0), [ZE.u], [ZE.u])
                      P.op("dve", lambda e, nb=nb, bi=bi: e.tensor_tensor(out=ZS.v[0:nb, bi, :], in0=pbs[5][0:nb, 128:512], in1=ZE.v[0:nb, :], op=ALU.mult), pu[5] + [ZE.u], [ZS.u])

                      if isP:
                          prev = None if kb == 0 else (kb - 1, sq.kblocks[kb - 1][1])
                      else:
                          prev = (kb - 1, 128)

                      def cmm(e, nb=nb, l12=l12, prev=prev):
                          r = e.matmul(pbs[6][0:nb, 384:390], lhsT=tri_f[0:nb, 0:nb], rhs=l12.v[0:nb, 0:6], start=True, stop=(prev is None))
                          if prev is not None:
                              pk, pn = prev
                              r = e.matmul(pbs[6][0:nb, 384:390], lhsT=sel_f(pn)[0:pn, 0:nb], rhs=NEGC.v[0:pn, pk, :], start=False, stop=True)
                          return r
                      P.op("pe", cmm, [l12.u, CF.u, NEGC.u], psu(6, 256, 512))
                      P.op("act", lambda e, nb=nb, kb=kb: e.activation(out=NEGC.v[0:nb, kb, :], in_=pbs[6][0:nb, 384:390], func=AF.Copy), psu(6, 256, 512), [NEGC.u])
                      P.op("dve", lambda e, nb=nb, l12=l12: e.tensor_single_scalar(out=STA.v[0:nb, 0, 0:128:32], in_=l12.v[0:nb, 0:4], scalar=-1.0, op=ALU.mult), [l12.u], [STA.u])
                      P.op("dve", lambda e, nb=nb, l12=l12: e.tensor_single_scalar(out=STA.v[0:nb, 1, 0:64:32], in_=l12.v[0:nb, 4:6], scalar=-1.0, op=ALU.mult), [l12.u], [STA.u])
                      if prev is not None:
                          pk, pn = prev
                          P.op("dve", lambda e, pk=pk, pn=pn: e.tensor_single_scalar(out=STA.v[0:pn, 2, 0:128:32], in_=NEGC.v[0:pn, pk, 0:4], scalar=-1.0, op=ALU.mult), [NEGC.u], [STA.u])
                          P.op("dve", lambda e, pk=pk, pn=pn: e.tensor_single_scalar(out=STA.v[0:pn, 3, 0:64:32], in_=NEGC.v[0:pn, pk, 4:6], scalar=-1.0, op=ALU.mult), [NEGC.u], [STA.u])

                      def smm(e, nb=nb, prev=prev):
                          r = None
                          for a in range(2):
                              r = e.matmul(pbs[6][:, 128 * a:128 * a + nb], lhsT=STA.v[0:nb, a, :], rhs=tri_b[0:nb, 0:nb], start=True, stop=(prev is None))
                              if prev is not None:
                                  pn = prev[1]
                                  r = e.matmul(pbs[6][:, 128 * a:128 * a + nb], lhsT=STA.v[0:pn, 2 + a, :], rhs=sel_b(pn)[0:pn, 0:nb], start=False, stop=True)
                          return r
                      P.op("pe", smm, [STA.u, CB.u], psu(6, 0, 256))
                      for h in range(6):
                          a, i = (0, h) if h < 4 else (1, h - 4)
                          P.op("act", lambda e, h=h, a=a, i=i, o=o, nb=nb: e.activation(out=QT.v[64:65, h, o:o + nb], in_=pbs[6][32 * i:32 * i + 1, 128 * a:128 * a + nb], func=AF.Copy),
                               psu(6, 0, 256), [QT.u])

                  mark_stage('step2a+fox')
                  def fm(c0, ncols, evac):
                      slot = fm_ctr[0] % 3
                      fm_ctr[0] += 1
                      bank, cc = (1, 2, 7)[slot], 0
                      us = [pu[bank][0]]

                      def mm(e):
                          r = None
                          for kc in range(8):
                              r = e.matmul(pbs[bank][0:ncols, cc:cc + n], lhsT=WIN.v[:, kc, c0:c0 + ncols], rhs=HT.v[:, kc, 0:n], start=(kc == 0), stop=(kc == 7))
                          return r
                      P.op("pe", mm, [WIN.u, HT.u], us)
                      evac(pbs[bank][:, cc:cc + n], us)

                  for j in range(3):
                      def evq(ps, us, j=j):
                          P.op("act", lambda e: e.activation(out=QT.v[0:64, 2 * j, 0:n], in_=ps[0:64, :], func=AF.Copy, scale=0.125), us, [QT.u])
                          P.op("act", lambda e: e.activation(out=QT.v[0:64, 2 * j + 1, 0:n], in_=ps[64:128, :], func=AF.Copy, scale=0.125), us, [QT.u])
                      fm(OQ + 128 * j, 128, evq)
                  for j in range(3):
                      def evk(ps, us, j=j):
                          P.op("act", lambda e: e.activation(out=KTC.v[0:64, 2 * j, 0:n], in_=ps[0:64, :], func=AF.Copy), us, [KTC.u])
                          P.op("dve", lambda e: e.tensor_copy(out=KTC.v[0:64, 2 * j + 1, 0:n], in_=ps[64:128, :]), us, [KTC.u])
                      fm(OKK + 128 * j, 128, evk)
                  for pt in range(2):
                      def evu(ps, us, pt=pt):
                          P.op("dve", lambda e: e.tensor_copy(out=UT.v[:, pt, 15:15 + n], in_=ps), us, [UT.u])
                      fm(OU + 128 * pt, 128, evu)
                  for ct in range(5):
                      def evx(ps, us, ct=ct):
                          P.op("act", lambda e: e.activation(out=XBC.v[:, ct, 3:3 + n], in_=ps, func=AF.Copy), us, [XBC.u])
                      fm(OX + 128 * ct, 128, evx)

                  mark_stage('step2b')
                  if isP and ti < len(sq.tiles) - 1:
                      for bi, (o, nb) in enumerate(blocks):
                          kb = kbs[bi]
                          DMA(KSC[kb].rearrange("p (h k) -> p h k", h=6)[:, :, 0:nb], KTC.v[:, :, o:o + nb], [KTC.u], [uKSC[kb]], f"c_ktc{bi}")
                          DMA(VSC[kb][0:nb, :], VCUR.v[0:nb, bi, :], [VCUR.u], [uKSC[kb]], f"c_vcur{bi}")

                  def gen_mix():
                      ln_ = 15 + n
                      for pt in range(2):
                          E_ = UT.v[:, pt, :]
                          P.op("dve", lambda e, E_=E_: e.tensor_tensor(out=PS2.v[:, 1:ln_], in0=E_[:, 1:ln_], in1=E_[:, 0:ln_ - 1], op=ALU.add), [UT.u], [PS2.u])
                          P.op("dve", lambda e: e.tensor_tensor(out=PS4.v[:, 3:ln_], in0=PS2.v[:, 3:ln_], in1=PS2.v[:, 1:ln_ - 2], op=ALU.add), [PS2.u], [PS4.u])
                          if pt == 0:
                              lo, hi = PS2, PS4
                          else:
                              P.op("dve", lambda e: e.tensor_tensor(out=PS2.v[:, 7:ln_], in0=PS4.v[:, 7:ln_], in1=PS4.v[:, 3:ln_ - 4], op=ALU.add), [PS4.u], [PS2.u])
                              P.op("dve", lambda e: e.tensor_tensor(out=PS4.v[:, 15:ln_], in0=PS2.v[:, 15:ln_], in1=PS2.v[:, 7:ln_ - 8], op=ALU.add), [PS2.u], [PS4.u])
                              lo, hi = PS2, PS4
                          for (r0, sb) in ((0, lo), (64, hi)):
                              if isP and ti == 0:
                                  P.op("dve", lambda e, r0=r0, sb=sb, pt=pt: e.tensor_tensor(out=ACC.v[r0:r0 + 64, 0:n], in0=sb.v[r0:r0 + 64, 15:15 + n], in1=CF.v[r0:r0 + 64, K_RC + 16 * pt:K_RC + 16 * pt + n], op=ALU.mult),
                                       [sb.u, CF.u], [ACC.u])
                                  P.op("dve", lambda e, r0=r0, pt=pt, E_=E_: e.tensor_tensor(out=DIFF.v[r0:r0 + 64, pt, 0:n], in0=ACC.v[r0:r0 + 64, 0:n], in1=E_[r0:r0 + 64, 15:15 + n], op=ALU.subtract),
                                       [ACC.u, UT.u], [DIFF.u])
                              else:
                                  P.op("dve", lambda e, r0=r0, sb=sb, pt=pt, E_=E_: e.scalar_tensor_tensor(out=DIFF.v[r0:r0 + 64, pt, 0:n], in0=sb.v[r0:r0 + 64, 15:15 + n],
                                                                                                   scalar=CF.v[r0:r0 + 64, K_RC + 16 * pt + 15:K_RC + 16 * pt + 16], in1=E_[r0:r0 + 64, 15:15 + n], op0=ALU.mult, op1=ALU.subtract),
                                       [sb.u, CF.u, UT.u], [DIFF.u])
                          P.op("pe", lambda e, pt=pt: e.matmul(pbs[7][:, 0:n], lhsT=PWB.v[:, pt, :], rhs=DIFF.v[:, pt, 0:n], start=True, stop=True), [PWB.u, DIFF.u], psu(7, 0, 256))
                          yield
                          P.op("act", lambda e, pt=pt: e.activation(out=MIXT.v[:, 3 + pt, 0:n], in_=pbs[7][:, 0:n], func=AF.Copy, scale=PCOL.v[:, C_PS + pt:C_PS + pt + 1]), psu(7, 0, 256) + [PCOL.u], [MIXT.u])
                      P.op("pool", lambda e: e.tensor_copy(out=UT.v[:, :, 0:15], in_=UT.v[:, :, n:n + 15]), [UT.u], [UT.u])

                      for ct in range(5):
                          cw = PCOL.v[:, C_CW + 4 * ct:C_CW + 4 * ct + 4]
                          cb_ = PCOL.v[:, C_CB + ct:C_CB + ct + 1]
                          P.op("pool", lambda e, ct=ct, cw=cw, cb_=cb_: e.tensor_scalar(out=ACC.v[:, 0:n], in0=XBC.v[:, ct, 0:n], scalar1=cw[:, 0:1], scalar2=cb_, op0=ALU.mult, op1=ALU.add),
                               [XBC.u, PCOL.u], [ACC.u])
                          for j in range(1, 4):
                              P.op("dve", lambda e, ct=ct, cw=cw, j=j: e.scalar_tensor_tensor(out=ACC.v[:, 0:n], in0=XBC.v[:, ct, j:j + n], scalar=cw[:, j:j + 1], in1=ACC.v[:, 0:n], op0=ALU.mult, op1=ALU.add),
                                   [XBC.u, PCOL.u], [ACC.u])
                          P.op("act", lambda e: e.activation(out=CE.v[:, 0:n], in_=ACC.v[:, 0:n], func=AF.Exp, scale=-1.0), [ACC.u], [CE.u])
                          P.op("act", lambda e: e.activation(out=CE.v[:, 0:n], in_=CE.v[:, 0:n], func=AF.Ln, bias=1.0), [CE.u], [CE.u])
                          P.op("act", lambda e: e.activation(out=CE.v[:, 0:n], in_=CE.v[:, 0:n], func=AF.Exp, scale=-1.0), [CE.u], [CE.u])
                          if ct < 3:
                              dst, du = XS.v[:, ct, 0:n], XS.u
                          elif ct == 3:
                              dst, du = BT.v[:, 0:n], BT.u
                          else:
                              dst, du = CT.v[:, 0:n], CT.u
                          P.op("dve", lambda e, dst=dst: e.tensor_tensor(out=dst, in0=ACC.v[:, 0:n], in1=CE.v[:, 0:n], op=ALU.mult), [ACC.u, CE.u], [du])
                          yield
                      P.op("pool", lambda e: e.tensor_copy(out=XBC.v[:, :, 0:3], in_=XBC.v[:, :, n:n + 3]), [XBC.u], [XBC.u])

                      for bi, (o, nb) in enumerate(blocks):
                          l12 = l12s[bi]
                          P.op("dve", lambda e, nb=nb, l12=l12: e.tensor_tensor(out=LA_.v[0:nb, :], in0=l12.v[0:nb, 6:12], in1=ABC.v[0:nb, :], op=ALU.mult), [l12.u, ABC.u], [LA_.u])
                          P.op("pe", lambda e, nb=nb: e.matmul(pbs[6][0:nb, 400:406], lhsT=tri_f[0:nb, 0:nb], rhs=LA_.v[0:nb, :], start=True, stop=True), [LA_.u, CF.u], psu(6, 256, 512))
                          P.op("act", lambda e, nb=nb: e.activation(out=NACS.v[0:nb, :], in_=pbs[6][0:nb, 400:406], func=AF.Copy, scale=-1.0), psu(6, 256, 512), [NACS.u])
                          P.op("act", lambda e, nb=nb: e.activation(out=EA.v[0:nb, :], in_=pbs[6][0:nb, 400:406], func=AF.Exp), psu(6, 256, 512), [EA.u])
                          P.op("dve", lambda e, nb=nb: e.tensor_tensor(out=XX.v[0:nb, :, 0:nb], in0=tri_f[0:nb, 0:nb].unsqueeze(1).to_broadcast([nb, 6, nb]),
                                                                      in1=LA_.v[0:nb, :].unsqueeze(2).to_broadcast([nb, 6, nb]), op=ALU.mult), [LA_.u, CF.u], [XX.u])
                          abank = lambda h: ((5, 7)[h // 4], 128 * (h % 4))

                          def amm(e, nb=nb):
                              r = None
                              for h in range(6):
                                  b_, c_ = abank(h)
                                  e.matmul(pbs[b_][:, c_:c_ + nb], lhsT=ones_f[0:nb, :], rhs=XX.v[0:nb, h, 0:nb], start=True, stop=False)
                                  r = e.matmul(pbs[b_][:, c_:c_ + nb], lhsT=ident_f[0:nb, :], rhs=mask_f[0:nb, 0:nb], start=False, stop=True)
                              return r
                          P.op("pe", amm, [XX.u, CF.u], [pu[5][0], pu[7][0]])
                          yield
                          for h in range(6):
                              b_, c_ = abank(h)
                              P.op("act", lambda e, h=h, b_=b_, c_=c_, nb=nb: e.activation(out=LT.v[0:nb, h, 0:nb], in_=pbs[b_][0:nb, c_:c_ + nb], func=AF.Exp, bias=NACS.v[0:nb, h:h + 1]),
                                   [pu[b_][0], NACS.u], [LT.u])

                          P.op("act", lambda e, nb=nb: e.activation(out=DEC.v[:, 0:4], in_=pbs[5][:, nb - 1:512:128], func=AF.Exp), [pu[5][0]], [DEC.u])
                          P.op("act", lambda e, nb=nb: e.activation(out=DEC.v[:, 4:6], in_=pbs[7][:, nb - 1:256:128], func=AF.Exp), [pu[7][0]], [DEC.u])

                          def gmm(e, o=o, nb=nb):
                              r = None
                              for g in range(2):
                                  r = e.matmul(pbs[(5, 7)[g]][0:nb, 0:nb], lhsT=BT.v[64 * g:64 * g + 64, o:o + nb], rhs=CT.v[64 * g:64 * g + 64, o:o + nb], start=True, stop=True)
                              return r
                          P.op("pe", gmm, [BT.u, CT.u], [pu[5][0], pu[7][0]])
                          yield
                          for h in range(6):
                              g = h // 3
                              P.op("dve", lambda e, h=h, g=g, nb=nb: e.tensor_tensor(out=MT.v[0:nb, h, 0:nb], in0=pbs[(5, 7)[g]][0:nb, 0:nb], in1=LT.v[0:nb, h, 0:nb], op=ALU.mult),
                                   [pu[(5, 7)[g]][0], LT.u], [MT.u])

                          def xtr(e, o=o, nb=nb):
                              r = None
                              for j in range(3):
                                  r = e.transpose(out=pbs[7][0:nb, 128 * j:128 * j + 128], in_=XS.v[:, j, o:o + nb], identity=ident_f)
                              return r
                          P.op("pe", xtr, [XS.u, CF.u], pu[7])
                          yield
                          P.op("pe", lambda e, o=o, nb=nb: e.transpose(out=pbs[0][0:nb, 0:128], in_=BT.v[:, o:o + nb], identity=ident_b), [BT.u, CB.u], pu[0])
                          P.op("act", lambda e, nb=nb: e.activation(out=XSTOK.v[0:nb, :], in_=pbs[7][0:nb, 0:384], func=AF.Copy), pu[7], [XSTOK.u])
                          P.op("dve", lambda e, nb=nb, l12=l12: e.tensor_tensor(out=XD.v[0:nb, :, :], in0=XSTOK.v[0:nb, :].rearrange("p (h d) -> p h d", h=6),
                                                                               in1=l12.v[0:nb, 6:12].unsqueeze(2).to_broadcast([nb, 6, 64]), op=ALU.mult), [XSTOK.u, l12.u], [XD.u])
                          for h in range(6):
                              g = h // 3
                              P.op("act", lambda e, h=h, g=g, nb=nb: e.activation(out=BW.v[0:nb, h, 64 * g:64 * g + 64], in_=pbs[0][0:nb, 64 * g:64 * g + 64], func=AF.Copy, scale=LT.v[0:nb, h, nb - 1:nb]),
                                   pu[0] + [LT.u], [BW.u])

                          def ymm(e, o=o, nb=nb):
                              r = None
                              for h in range(6):
                                  r = e.matmul(pbs[6][0:nb, 64 * h:64 * h + 64], lhsT=MT.v[0:nb, h, 0:nb], rhs=XD.v[0:nb, h, :], start=True, stop=True)
                              for g in range(2):
                                  r = e.matmul(pbs[(5, 7)[g]][0:nb, 0:192], lhsT=CT.v[64 * g:64 * g + 64, o:o + nb], rhs=SBF.v[64 * g:64 * g + 64, :], start=True, stop=True)
                              for h in range(6):
                                  r = e.matmul(pb0f[:, 64 * h:64 * h + 64], lhsT=BW.v[0:nb, h, :], rhs=XD.v[0:nb, h, :], start=True, stop=True)
                              return r
                          P.op("pe", ymm, [MT.u, XD.u, CT.u, SBF.u, BW.u], [pu[6][0], pu[5][0], pu[7][0], pu[0][0]])
                          yield
                          for h in range(6):
                              g, j = h // 3, h % 3
                              P.op("dve", lambda e, h=h, g=g, j=j: e.scalar_tensor_tensor(out=SST.v[64 * g:64 * g + 64, 64 * j:64 * j + 64], in0=SST.v[64 * g:64 * g + 64, 64 * j:64 * j + 64],
                                                                                       scalar=DEC.v[64 * g:64 * g + 64, h:h + 1], in1=pb0f[64 * g:64 * g + 64, 64 * h:64 * h + 64], op0=ALU.mult, op1=ALU.add),
                                   [SST.u, DEC.u] + pu[0], [SST.u])
                          P.op("dve", lambda e: e.tensor_copy(out=SBF.v, in_=SST.v), [SST.u], [SBF.u])
                          yield
                          for g in range(2):
                              P.op("dve", lambda e, nb=nb, g=g: e.tensor_tensor(out=Y1.v[0:nb, 192 * g:192 * g + 192].rearrange("p (h d) -> p h d", h=3), in0=pbs[(5, 7)[g]][0:nb, 0:192].rearrange("p (h d) -> p h d", h=3),
                                                                               in1=EA.v[0:nb, 3 * g:3 * g + 3].unsqueeze(2).to_broadcast([nb, 3, 64]), op=ALU.mult), [pu[(5, 7)[g]][0], EA.u], [Y1.u])
                          P.op("dve", lambda e, nb=nb: e.tensor_tensor(out=Y1.v[0:nb, :], in0=pbs[6][0:nb, 0:384], in1=Y1.v[0:nb, :], op=ALU.add), pu[6] + [Y1.u], [Y1.u])
                          P.op("pool", lambda e, nb=nb: e.tensor_tensor(out=Y2.v[0:nb, :].rearrange("p (h d) -> p h d", h=6), in0=XSTOK.v[0:nb, :].rearrange("p (h d) -> p h d", h=6),
                                                                       in1=PBC.v[0:nb, B_DS:B_DS + 6].unsqueeze(2).to_broadcast([nb, 6, 64]), op=ALU.mult), [XSTOK.u, PBC.u], [Y2.u])
                          P.op("dve", lambda e, nb=nb: e.tensor_tensor(out=Y1.v[0:nb, :], in0=Y1.v[0:nb, :], in1=Y2.v[0:nb, :], op=ALU.add), [Y1.u, Y2.u], [Y1.u])
                          P.op("dve", lambda e, nb=nb, bi=bi: e.tensor_tensor(out=Y1.v[0:nb, :], in0=Y1.v[0:nb, :], in1=ZS.v[0:nb, bi, :], op=ALU.mult), [Y1.u, ZS.u], [Y1.u])
                          ssq = SM.v[0:nb, 2:3]
                          rs = SM.v[0:nb, 3:4]
                          P.op("act", lambda e, nb=nb, ssq=ssq: e.activation(out=Y2.v[0:nb, :], in_=Y1.v[0:nb, :], func=AF.Square, accum_out=ssq), [Y1.u], [Y2.u, smu[1]])
                          P.op("act", lambda e, nb=nb, ssq=ssq, rs=rs: e.activation(out=rs, in_=ssq, func=AF.Ln, bias=CF.v[0:nb, K_EPS:K_EPS + 1], scale=1.0 / 384), [smu[1], CF.u], [smu[1]])
                          P.op("act", lambda e, rs=rs: e.activation(out=rs, in_=rs, func=AF.Exp, scale=-0.5), [smu[1]], [smu[1]])
                          P.op("dve", lambda e, nb=nb, rs=rs: e.scalar_tensor_tensor(out=YN.v[0:nb, :], in0=Y1.v[0:nb, :], scalar=rs, in1=PBC.v[0:nb, B_SN:B_SN + 384], op0=ALU.mult, op1=ALU.mult),
                               [Y1.u, smu[1], PBC.u], [YN.u])
                          yield

                          def ytr(e, nb=nb):
                              r = None
                              for j in range(3):
                                  r = e.transpose(out=pbs[0][:, 512 + 128 * j:512 + 128 * j + nb], in_=YN.v[0:nb, 128 * j:128 * j + 128], identity=ident_b[0:nb, 0:nb])
                              return r
                          P.op("pe", ytr, [YN.u, CB.u], pu[0])
                          yield
                          P.op("act", lambda e, o=o, nb=nb: e.activation(out=MIXT.v[:, 5:8, o:o + nb], in_=pbs[0][:, 512:896].rearrange("p (a b) -> p a b", a=3)[:, :, 0:nb], func=AF.Copy), pu[0], [MIXT.u])


                  def gen_att():
                      q0 = sq.npast + t0
                      vis = []
                      for kb, (ks, kn) in enumerate(sq.kblocks):
                          if ks > q0 + n - 1:
                              break
                          i0 = max(0, ks - q0)
                          diag = ks + kn - 1 > q0
                          vis.append((kb, ks, kn, i0, diag))
                      otu = lambda h: (5 + h // 2, 256 * (h % 2))
                      pend = []
                      SKEW = 3
                      for hp in range(1):
                          heads = list(range(6))
                          for vi, (kb, ks, kn, i0, diag) in enumerate(vis):
                              nq = n - i0
                              cur = ks >= q0
                              if cur:
                                  bi = (ks - q0) // 128
                                  kt_ap = KTC.v[:, :, ks - q0:ks - q0 + kn]
                                  v_ap = VCUR.v[0:kn, bi, :]
                                  kvu = [KTC.u, VCUR.u]
                              else:
                                  r_ = ring_ctr[0] % NRING
                                  ring_ctr[0] += 1
                                  rk, rv = KVRk[r_], KVRv[r_]
                                  kt_ap = rk.v[:, :, 0:kn]
                                  v_ap = rv.v[0:kn, :]
                                  kvu = [rk.u, rv.u]
                                  if isP:
                                      DMA(rk.v[:, :, 0:kn], KSC[kb].rearrange("p (h k) -> p h k", h=6)[:, :, 0:kn], [uKSC[kb]], [rk.u], rk.name)
                                      DMA(rv.v[0:kn, :], VSC[kb][0:kn, :], [uKSC[kb]], [rv.u], rv.name)
                                  else:
                                      ck_, cv_ = CKS[cks_ctr[0] % 2], CVS[cks_ctr[0] % 2]
                                      cks_ctr[0] += 1
                                      DMA(ck_.v, I["ck"][L, sq.idx, ks:ks + 128, :], [uIN], [ck_.u], ck_.name)
                                      DMA(cv_.v, I["cv"][L, sq.idx, ks:ks + 128, :], [uIN], [cv_.u], cv_.name)
                                      P.op("pool", lambda e, ck_=ck_: e.tensor_copy(out=KB16.v, in_=ck_.v), [ck_.u], [KB16.u])

                                      def ktr(e):
                                          r = None
                                          for j in range(3):
                                              r = e.transpose(out=pbs[0][:, 128 * j:128 * j + 128], in_=KB16.v[:, 128 * j:128 * j + 128], identity=ident_b)
                                          return r
                                      P.op("pe", ktr, [KB16.u, CB.u], pu[0])
                                      for h in range(6):
                                          P.op("act", lambda e, h=h, rk=rk: e.activation(out=rk.v[0:64, h, :], in_=pbs[0][64 * (h % 2):64 * (h % 2) + 64, 128 * (h // 2):128 * (h // 2) + 128], func=AF.Copy),
                                               pu[0], [rk.u])
                                      vsrc = cv_.v.rearrange("p (j t d) -> p j t d", j=3, t=2)
                                      vdst = rv.v.rearrange("p (j x) -> p j x", j=3)
                                      P.op("pool", lambda e, vsrc=vsrc, vdst=vdst: e.tensor_copy(out=vdst[:, :, 0:64], in_=vsrc[:, :, 0, :]), [cv_.u], [rv.u])
                                      P.op("pool", lambda e, vsrc=vsrc, vdst=vdst: e.tensor_copy(out=vdst[:, :, 128:192], in_=vsrc[:, :, 1, :]), [cv_.u], [rv.u])
                              if vi == 0:
                                  for h in heads:
                                      ob, oc = otu(h)
                                      P.op("dve", lambda e, ob=ob, oc=oc: e.memset(pbs[ob][:, oc:oc + n], 0.0), [], [pu[ob][0]])
                              for h in heads:
                                  sl = st_ctr[0] % 4
                                  st_ctr[0] += 1
                                  sb_, sc_ = 1 + sl, 0
                                  sus = [pu[sb_][0]]
                                  mw = min(kn, nq)

                                  def smm2(e, h=h, kt_ap=kt_ap, sb_=sb_, sc_=sc_, kn=kn, nq=nq, i0=i0, diag=diag, mw=mw):
                                      r = e.matmul(pbs[sb_][0:kn, sc_:sc_ + nq], lhsT=kt_ap[:, h, :], rhs=QT.v[:, h, i0:n], start=True, stop=not diag)
                                      if diag:
                                          r = e.matmul(pbs[sb_][0:kn, sc_:sc_ + mw], lhsT=ident_b[0:kn, 0:kn], rhs=mask_b[0:kn, 0:mw], start=False, stop=True)
                                      return r
                                  P.op("pe", smm2, kvu + [QT.u, CB.u], sus)
                                  pt_ = PT[pt_ctr[0] % 4]
                                  pt_ctr[0] += 1
                                  P.op("act", lambda e, h=h, pt_=pt_, sb_=sb_, sc_=sc_, kn=kn, nq=nq, kb=kb: e.activation(out=pt_.v[0:kn, 0:nq], in_=pbs[sb_][0:kn, sc_:sc_ + nq], func=AF.Exp, bias=NEGC.v[0:kn, kb, h:h + 1]),
                                       sus + [NEGC.u], [pt_.u])
                                  ob, oc = otu(h)
                                  vc0 = 192 * (h // 2) + 64 * (h % 2)
                                  pend.append((lambda e, pt_=pt_, ob=ob, oc=oc, v_ap=v_ap, vc0=vc0, kn=kn, nq=nq, i0=i0: e.matmul(pbs[ob][:, oc + i0:oc + n], lhsT=v_ap[:, vc0:vc0 + 128], rhs=pt_.v[0:kn, 0:nq],
                                                                                                                      start=False, stop=False, skip_group_check=True),
                                               kvu + [pt_.u], [pu[ob][0]]))
                                  if len(pend) > SKEW:
                                      f_, r_s, w_s = pend.pop(0)
                                      P.op("pe", f_, r_s, w_s)
                                  yield
                          while pend:
                              f_, r_s, w_s = pend.pop(0)
                              P.op("pe", f_, r_s, w_s)
                          for h in heads:
                              ob, oc = otu(h)
                              rc_ = REC[h % 2]
                              nr, dr = (0, 64) if h % 2 == 0 else (64, 0)
                              P.op("act", lambda e, ob=ob, oc=oc, rc_=rc_, dr=dr: e.activation(out=rc_.v[dr:dr + 64, 0:n], in_=pbs[ob][dr:dr + 64, oc:oc + n], func=AF.Ln), [pu[ob][0]], [rc_.u])
                              P.op("act", lambda e, rc_=rc_, dr=dr: e.activation(out=rc_.v[dr:dr + 64, 0:n], in_=rc_.v[dr:dr + 64, 0:n], func=AF.Exp, scale=-1.0), [rc_.u], [rc_.u])
                              P.op("dve", lambda e, h=h, ob=ob, oc=oc, rc_=rc_, nr=nr, dr=dr: e.tensor_tensor(out=MIXT.v[nr:nr + 64, h // 2, 0:n], in0=pbs[ob][nr:nr + 64, oc:oc + n], in1=rc_.v[dr:dr + 64, 0:n], op=ALU.mult),
                                   [pu[ob][0], rc_.u], [MIXT.u])
                          yield

                  if ti + 1 < len(sq.tiles):
                      nsq, nti = sq, ti + 1
                  else:
                      k_ = seqs.index(sq)
                      nsq, nti = (seqs[k_ + 1], 0) if k_ + 1 < len(seqs) else (None, 0)
                  if nsq is not None:
                      s1cache[(nsq.kind, nsq.idx, nti)] = do_step1(nsq, nsq.tiles[nti][0], nsq.tiles[nti][1])
                  for _ in gen_mix():
                      pass
                  for _ in gen_att():
                      pass


                  mark_stage('attn')
                  for bi, (o, nb) in enumerate(blocks):
                      def omm(e, o=o, nb=nb):
                          r = None
                          for c in range(2):
                              for kt in range(8):
                                  r = e.matmul(pbs[6 + c][0:nb, :], lhsT=MIXT.v[:, kt, o:o + nb], rhs=WOUT.v[:, kt, 512 * c:512 * c + 512], start=(kt == 0), stop=(kt == 7))
                          return r
                      P.op("pe", omm, [MIXT.u, WOUT.u], pu[6] + pu[7])
                      xo = XOUT[xo_ctr[0] % 2]
                      xo_ctr[0] += 1
                      post_norm_res(P, pbs, pu, 6, 7, nb, xins[bi], xo, B_PM, JUNK, SM, smu, CF, PBC, T1)
                      r = sq.row0 + t0 + o
                      DMA(XA[r:r + nb, :], xo.v[0:nb, :], [xo.u], [uXA[r // 128]], xo.name)

              mark_stage('seq-tiles-done')
              s = sq.idx
              for pt in range(2):
                  P.op("pe", lambda e, pt=pt: e.transpose(out=pbs[7][0:15, 128 * pt:128 * pt + 128], in_=UT.v[:, pt, 0:15], identity=ident_f), [UT.u, CF.u], pu[7])
              P.op("act", lambda e: e.activation(out=STT.v[0:15, 0:256], in_=pbs[7][0:15, 0:256], func=AF.Copy), pu[7], [STT.u])
              DMA(O["pool_p"][L] if isP else O["pool_s"][L, s], STT.v[0:15, 0:256], [STT.u], [uOUT], "c_stt")
              for ct in range(5):
                  P.op("pe", lambda e, ct=ct: e.transpose(out=pbs[3 + ct // 4][0:3, 128 * (ct % 4):128 * (ct % 4) + 128], in_=XBC.v[:, ct, 0:3], identity=ident_f), [XBC.u, CF.u], pu[3] + pu[4])
              P.op("act", lambda e: e.activation(out=T1.v[0:3, 0:512], in_=pbs[3][0:3, 0:512], func=AF.Copy), pu[3], [T1.u])
              P.op("act", lambda e: e.activation(out=T1.v[0:3, 512:640], in_=pbs[4][0:3, 0:128], func=AF.Copy), pu[4], [T1.u])
              DMA(O["conv_p"][L] if isP else O["conv_s"][L, s], T1.v[0:3, 0:640], [T1.u], [uOUT], "c_t1")
              for j in range(3):
                  P.op("pe", lambda e, j=j: e.transpose(out=pbs[5][0:64, 128 * j:128 * j + 128], in_=SST.v[:, 64 * j:64 * j + 64], identity=ident_f), [SST.u, CF.u], pu[5])
              P.op("act", lambda e: e.activation(out=SIN.v[0:64, :, :], in_=pbs[5][0:64, 0:384].rearrange("p (a b) -> p a b", a=3), func=AF.Copy), pu[5], [SIN.u])
              od = O["ssd_p"][L] if isP else O["ssd_s"][L, s]
              for j in range(3):
                  DMA(od[j], SIN.v[0:64, j, 0:64], [SIN.u], [uOUT], f"c_so{j}a")
                  DMA(od[j + 3], SIN.v[0:64, j, 64:128], [SIN.u], [uOUT], f"c_so{j}b")

          mark_stage('phaseA-done')
          P.barrier()
          AR.off = mark
          WG = AR.alloc("wg", [8, DFF], BF16)
          WU = AR.alloc("wu", [8, DFF], BF16)
          WD = AR.alloc("wd", [NFC, D], BF16)
          H1 = AR.alloc("h1", [NFC, 256], BF16)
          GE = [AR.alloc(f"ge{i}", [256], F32) for i in range(3)]
          for kc in range(8):
              for c0 in range(0, DFF, WCH):
                  c1 = min(DFF, c0 + WCH)
                  load_cast(WG.v[:, kc, c0:c1], WG.u, I["w_gate"][L, kc * 128:(kc + 1) * 128, c0:c1], c1 - c0, PCOL.v[:, C_GF + kc:C_GF + kc + 1])
                  load_cast(WU.v[:, kc, c0:c1], WU.u, I["w_up"][L, kc * 128:(kc + 1) * 128, c0:c1], c1 - c0, PCOL.v[:, C_GF + kc:C_GF + kc + 1])
          for fc in range(NFC):
              for c0 in range(0, D, WCH):
                  load_cast(WD.v[:, fc, c0:c0 + WCH], WD.u, I["w_down"][L, fc * 128:(fc + 1) * 128, c0:c0 + WCH], WCH)

          P.barrier()
          btiles = [(0, META)] + [(META + 256 * j, 256) for j in range(NPT)] + [(T, NSEQ * DSEQ)]
          xin_ctr = [0]
          xo_ctr = [0]
          gu_ctr = [0]
          HT2 = AR.alloc("hT2", [8, 256], BF16)
          HTS = [HT, HT2]
          bstate = {}

          def b_front(ti):
              r0, n = btiles[ti]
              HTc = HTS[ti % 2]
              blocks = [(o, min(128, n - o)) for o in range(0, n, 128)]
              xins = []
              for bi, (o, nb) in enumerate(blocks):
                  xin = XIN[xin_ctr[0] % 4]
                  xin_ctr[0] += 1
                  xins.append(xin)
                  r = r0 + o
                  DMA(xin.v[0:nb, :], XA[r:r + nb, :], [uXA[r // 128]] + ([uXA[(r + nb - 1) // 128]] if (r + nb - 1) // 128 != r // 128 else []), [xin.u], xin.name)
                  hb = HBF[bi % 2]
                  ssq = SM.v[0:nb, 0:1]
                  rs = SM.v[0:nb, 1:2]
                  P.op("act", lambda e, xin=xin, nb=nb, ssq=ssq: e.activation(out=JUNK.v[0:nb, :], in_=xin.v[0:nb, :], func=AF.Square, accum_out=ssq), [xin.u], [JUNK.u, smu[0]])
                  P.op("act", lambda e, nb=nb, ssq=ssq, rs=rs: e.activation(out=rs, in_=ssq, func=AF.Ln, bias=CF.v[0:nb, K_EPS:K_EPS + 1], scale=1.0 / D), [smu[0], CF.u], [smu[0]])
                  P.op("act", lambda e, rs=rs: e.activation(out=rs, in_=rs, func=AF.Exp, scale=-0.5), [smu[0]], [smu[0]])
                  P.op("dve", lambda e, hb=hb, xin=xin, nb=nb, rs=rs: e.tensor_scalar(out=hb.v[0:nb, :], in0=xin.v[0:nb, :], scalar1=rs, scalar2=CF.v[0:nb, K_Z:K_Z + 1], op0=ALU.mult, op1=ALU.add),
                       [xin.u, smu[0], CF.u], [hb.u])

                  def tr(e, hb=hb, nb=nb):
                      r_ = None
                      for kc in range(8):
                          r_ = e.transpose(out=pbs[0][:, kc * 128:kc * 128 + nb], in_=hb.v[0:nb, kc * 128:(kc + 1) * 128], identity=ident_b[0:nb, 0:nb])
                      return r_
                  P.op("pe", tr, [hb.u, CB.u], pu[0])
                  P.op("act", lambda e, o=o, nb=nb: e.activation(out=HTc.v[:, :, o:o + nb], in_=pbs[0][:, :].rearrange("p (a b) -> p a b", a=8)[:, :, 0:nb], func=AF.Copy), pu[0], [HTc.u])
              bstate[ti] = (blocks, xins)

          def b_gu(ti):
              r0, n = btiles[ti]
              HTc = HTS[ti % 2]
              for fc in range(NFC):
                  sl = gu_ctr[0] % 3
                  gu_ctr[0] += 1
                  (gb_, gc_), (ub_, uc_) = (1 + sl, 0), (1 + sl, 256)
                  gus = [pu[gb_][0], pu[ub_][0]]

                  def gmm2(e, fc=fc, gb_=gb_, gc_=gc_, ub_=ub_, uc_=uc_):
                      r_ = None
                      for kc in range(8):
                          r_ = e.matmul(pbs[gb_][:, gc_:gc_ + n], lhsT=WG.v[:, kc, 128 * fc:128 * fc + 128], rhs=HTc.v[:, kc, 0:n], start=(kc == 0), stop=(kc == 7))
                      for kc in range(8):
                          r_ = e.matmul(pbs[ub_][:, uc_:uc_ + n], lhsT=WU.v[:, kc, 128 * fc:128 * fc + 128], rhs=HTc.v[:, kc, 0:n], start=(kc == 0), stop=(kc == 7))
                      return r_
                  P.op("pe", gmm2, [WG.u, WU.u, HTc.u], gus)
                  ge = GE[sl]
                  P.op("act", lambda e, ge=ge, gb_=gb_, gc_=gc_: e.activation(out=ge.v[:, 0:n], in_=pbs[gb_][:, gc_:gc_ + n], func=AF.Exp, scale=-1.0), [gus[0]], [ge.u])
                  P.op("act", lambda e, ge=ge: e.activation(out=ge.v[:, 0:n], in_=ge.v[:, 0:n], func=AF.Ln, bias=1.0), [ge.u], [ge.u])
                  P.op("act", lambda e, ge=ge: e.activation(out=ge.v[:, 0:n], in_=ge.v[:, 0:n], func=AF.Exp, scale=-1.0), [ge.u], [ge.u])
                  P.op("dve", lambda e, ge=ge, gb_=gb_, gc_=gc_: e.tensor_tensor(out=ge.v[:, 0:n], in0=pbs[gb_][:, gc_:gc_ + n], in1=ge.v[:, 0:n], op=ALU.mult), [gus[0], ge.u], [ge.u])
                  P.op("dve", lambda e, ge=ge, ub_=ub_, uc_=uc_, fc=fc: e.tensor_tensor(out=H1.v[:, fc, 0:n], in0=pbs[ub_][:, uc_:uc_ + n], in1=ge.v[:, 0:n], op=ALU.mult), [gus[1], ge.u], [H1.u])

          def b_down(ti):
              r0, n = btiles[ti]
              blocks, xins = bstate[ti]
              for bi, (o, nb) in enumerate(blocks):
                  ba, bb = (4, 5) if bi % 2 == 0 else (6, 7)

                  def dmm(e, o=o, nb=nb, ba=ba, bb=bb):
                      r_ = None
                      for c, bk in ((0, ba), (1, bb)):
                          for fc in range(NFC):
                              r_ = e.matmul(pbs[bk][0:nb, :], lhsT=H1.v[:, fc, o:o + nb], rhs=WD.v[:, fc, 512 * c:512 * c + 512], start=(fc == 0), stop=(fc == NFC - 1))
                      return r_
                  P.op("pe", dmm, [H1.u, WD.u], pu[ba] + pu[bb])
                  xo = XOUT[xo_ctr[0] % 2]
                  xo_ctr[0] += 1
                  post_norm_res(P, pbs, pu, ba, bb, nb, xins[bi], xo, B_PF, JUNK, SM, smu, CF, PBC, T1)
                  r = r0 + o
                  if L < DEPTH - 1:
                      DMA(XB[r:r + nb, :], xo.v[0:nb, :], [xo.u], [uXB[r // 128], uXB[(r + nb - 1) // 128]], xo.name)
                  else:
                      if r0 == 0:
                          pass
                      elif r0 < T:
                          DMA(O["y_p"][r - META:r - META + nb, :], xo.v[0:nb, :], [xo.u], [uOUT], xo.name)
                      else:
                          DMA(O["y_s"][r - T:r - T + nb, :], xo.v[0:nb, :], [xo.u], [uOUT], xo.name)


          b_front(0)
          for ti in range(len(btiles)):
              b_gu(ti)
              if ti + 1 < len(btiles):
                  b_front(ti + 1)
              b_down(ti)
    except _Stop:
        pass
    P.barrier()
    semkeys = list(ENGS) + sorted(P.dcnt.keys())
    sems = {k: es.enter_context(nc.semaphore(k)) for k in semkeys}

    def replay(qn, e):
        for waits, fn, ev, inc in P.q[qn]:
            for k, v in waits:
                e.wait_ge(sems[k], v)
            if fn is None:
                continue
            ins = None
            for m_, a_, kw_ in fn:
                ins = getattr(e, m_)(*a_, **kw_)
            ins.then_inc(sems[ev[0]], inc)

    with nc.Block() as block:
        @block.sync
        def _(e):
            replay("sp", e)

        @block.tensor
        def _(e):
            replay("pe", e)

        @block.scalar
        def _(e):
            replay("act", e)

        @block.vector
        def _(e):
            replay("dve", e)

        @block.gpsimd
        def _(e):
            replay("pool", e)
    es.close()
    return nc


def post_norm_res(P, pbs, pu, ba, bb, nb, xin, xo, goff, JUNK, SM, smu, CF, PBC, T1):
    s0 = SM.v[0:nb, 4:5]
    s1 = SM.v[0:nb, 5:6]
    rs = SM.v[0:nb, 6:7]
    P.op("act", lambda e: e.activation(out=JUNK.v[0:nb, 0:512], in_=pbs[ba][0:nb, :], func=AF.Square, accum_out=s0), pu[ba], [JUNK.u, smu[2]])
    P.op("act", lambda e: e.activation(out=JUNK.v[0:nb, 512:1024], in_=pbs[bb][0:nb, :], func=AF.Square, accum_out=s1), pu[bb], [JUNK.u, smu[2]])
    P.op("dve", lambda e: e.tensor_tensor(out=s0, in0=s0, in1=s1, op=ALU.add), [smu[2]], [smu[2]])
    P.op("act", lambda e: e.activation(out=rs, in_=s0, func=AF.Ln, bias=CF.v[0:nb, K_EPS:K_EPS + 1], scale=1.0 / D), [smu[2], CF.u], [smu[2]])
    P.op("act", lambda e: e.activation(out=rs, in_=rs, func=AF.Exp, scale=-0.5), [smu[2]], [smu[2]])
    P.op("dve", lambda e: e.scalar_tensor_tensor(out=T1.v[0:nb, 0:512], in0=pbs[ba][0:nb, :], scalar=rs, in1=PBC.v[0:nb, goff:goff + 512], op0=ALU.mult, op1=ALU.mult),
         pu[ba] + [smu[2], PBC.u], [T1.u])
    P.op("dve", lambda e: e.scalar_tensor_tensor(out=T1.v[0:nb, 512:1024], in0=pbs[bb][0:nb, :], scalar=rs, in1=PBC.v[0:nb, goff + 512:goff + 1024], op0=ALU.mult, op1=ALU.mult),
         pu[bb] + [smu[2], PBC.u], [T1.u])
    P.op("dve", lambda e: e.tensor_tensor(out=xo.v[0:nb, :], in0=T1.v[0:nb, :], in1=xin.v[0:nb, :], op=ALU.add), [T1.u, xin.u], [xo.u])


def _consts():
    cf = np.zeros((128, NCF), np.float32)
    p = np.arange(128)
    cf[:, K_ID:K_ID + 128] = np.eye(128, dtype=np.float32)
    cf[:, K_TRI:K_TRI + 128] = (p[:, None] <= p[None, :]).astype(np.float32)
    cf[:, K_ONE:K_ONE + 128] = 1.0
    cf[15, K_S16:K_S16 + 128] = 1.0
    cf[31, K_S32:K_S32 + 128] = 1.0
    cf[127, K_S128:K_S128 + 128] = 1.0
    cf[:, K_MSK:K_MSK + 128] = np.where(p[:, None] <= p[None, :], 0.0, NEG).astype(np.float32)
    for pt in range(2):
        for pp in range(128):
            g = 2 * pt + (1 if pp >= 64 else 0)
            w = 2 ** (g + 1)
            for pos in range(16):
                cf[pp, K_RC + 16 * pt + pos] = 1.0 / min(pos + 1, w)
    cf[:, K_Z] = 0.0
    cf[:, K_1] = 1.0
    cf[:, K_EPS] = EPS
    cb = np.zeros((128, NCB), np.float32)
    cb[:, KB_ID:KB_ID + 128] = cf[:, K_ID:K_ID + 128]
    cb[:, KB_MSK:KB_MSK + 128] = cf[:, K_MSK:K_MSK + 128]
    cb[:, KB_TRI:KB_TRI + 128] = cf[:, K_TRI:K_TRI + 128]
    cb[:, KB_S16:KB_S16 + 128] = cf[:, K_S16:K_S16 + 128]
    cb[:, KB_S32:KB_S32 + 128] = cf[:, K_S32:K_S32 + 128]
    cb[:, KB_S128:KB_S128 + 128] = cf[:, K_S128:K_S128 + 128]
    return cf, cb.astype(ml_dtypes.bfloat16)


_CACHE = {}


def kernel(x_prompt, x_sample, cache_fox_k, cache_fox_v, cache_fox_logf, state_pool, state_conv,
           state_ssd, meta_tokens, ln_pre_mix, ln_post_mix, ln_pre_ffn, ln_post_ffn, w_in,
           fox_f_bias, pool_w, pool_scale, conv_w, conv_b, dt_bias, a_log, d_skip, ssd_norm,
           w_out, w_gate, w_up, w_down):
    f = lambda a: np.ascontiguousarray(np.asarray(a, dtype=np.float32))
    x_prompt, x_sample = f(x_prompt), f(x_sample)
    BATCH, SEQ, _ = x_prompt.shape
    DEPTH, DB, PAST = cache_fox_k.shape[0], cache_fox_k.shape[1], cache_fox_k.shape[2]
    NCORE = 8
    assert DB == NCORE * NSEQ and x_sample.shape[1] == DSEQ
    T = META + SEQ
    key = (SEQ, PAST, DEPTH)
    if key not in _CACHE:
        _CACHE[key] = build(SEQ, PAST, DEPTH)
    nc = _CACHE[key]
    perm = np.concatenate([np.arange(0, 384), np.arange(384, 768), np.arange(768, 1152), np.arange(1152, 1158),
                           np.arange(2438, 2444), np.arange(1414, 1798), np.arange(1158, 1414), np.arange(1798, 2438)])
    w_in_r = np.ascontiguousarray(f(w_in)[:, :, perm])
    pbc = np.zeros((DEPTH, 128, NBC), np.float32)
    pcol = np.zeros((DEPTH, 128, NCOL), np.float32)
    poolw = np.zeros((DEPTH, 128, 256), np.float32)
    for l in range(DEPTH):
        pbc[l, :, B_PM:B_PM + 1024] = f(ln_post_mix)[l][None]
        pbc[l, :, B_PF:B_PF + 1024] = f(ln_post_ffn)[l][None]
        pbc[l, :, B_SN:B_SN + 384] = f(ssd_norm)[l][None]
        pbc[l, :, B_FB:B_FB + 6] = f(fox_f_bias)[l][None]
        pbc[l, :, B_FB + 6:B_FB + 12] = f(dt_bias)[l][None]
        pbc[l, :, B_AL:B_AL + 6] = f(a_log)[l][None]
        pbc[l, :, B_DS:B_DS + 6] = f(d_skip)[l][None]
        pcol[l, :, C_GM:C_GM + 8] = f(ln_pre_mix)[l].reshape(8, 128).T
        pcol[l, :, C_GF:C_GF + 8] = f(ln_pre_ffn)[l].reshape(8, 128).T
        pcol[l, :, C_PS:C_PS + 2] = f(pool_scale)[l].reshape(2, 128).T
        cw = f(conv_w)[l]
        for ct in range(5):
            pcol[l, :, C_CW + 4 * ct:C_CW + 4 * ct + 4] = cw[:, 128 * ct:128 * ct + 128].T
        pcol[l, :, C_CB:C_CB + 5] = f(conv_b)[l].reshape(5, 128).T
        pw = f(pool_w)[l]
        for g in range(4):
            pt, hf = g // 2, g % 2
            poolw[l, 64 * hf:64 * hf + 64, 128 * pt + 64 * hf:128 * pt + 64 * hf + 64] = pw[g]
    cf, cb = _consts()
    shared = dict(w_in=w_in_r, w_out=f(w_out), w_gate=f(w_gate), w_up=f(w_up), w_down=f(w_down),
                  pbc=pbc, pcol=pcol, poolw=poolw, cstf=cf, cstb=cb, meta=f(meta_tokens))
    ck, cv, cl = f(cache_fox_k), f(cache_fox_v), f(cache_fox_logf)
    sp_, sc_, ss_ = f(state_pool), f(state_conv), f(state_ssd)
    in_maps = []
    for c in range(NCORE):
        sl = slice(NSEQ * c, NSEQ * c + NSEQ)
        m = dict(shared)
        m["xp"] = x_prompt[c % BATCH]
        m["xs"] = np.ascontiguousarray(x_sample[sl].reshape(NSEQ * DSEQ, D))
        m["ck"] = np.ascontiguousarray(ck[:, sl].reshape(DEPTH, NSEQ, PAST, 384))
        m["cv"] = np.ascontiguousarray(cv[:, sl].reshape(DEPTH, NSEQ, PAST, 384))
        m["cl"] = np.ascontiguousarray(cl[:, sl])
        m["spool"] = np.ascontiguousarray(sp_[:, sl])
        m["sconv"] = np.ascontiguousarray(sc_[:, sl])
        m["sssd"] = np.ascontiguousarray(ss_[:, sl])
        in_maps.append(m)
    res = run_bass_kernel_spmd(nc, in_maps, core_ids=list(range(NCORE)))
    R = res.results
    y_prompt = np.stack([R[b]["y_p"] for b in range(BATCH)], 0)
    y_sample = np.concatenate([R[c]["y_s"].reshape(NSEQ, DSEQ, D) for c in range(NCORE)], 0)
    stp = lambda k, shp: np.stack([R[b][k].reshape(shp) for b in range(BATCH)], 1)
    sts = lambda k, shp: np.concatenate([R[c][k].reshape(shp) for c in range(NCORE)], 1)
    outs = (
        y_prompt, y_sample,
        stp("k_p", (DEPTH, T, 6, 64)), stp("v_p", (DEPTH, T, 6, 64)), stp("l_p", (DEPTH, T, 6)),
        stp("pool_p", (DEPTH, 15, 256)), stp("conv_p", (DEPTH, 3, 640)), stp("ssd_p", (DEPTH, 6, 64, 64)),
        sts("k_s", (DEPTH, NSEQ, DSEQ, 6, 64)), sts("v_s", (DEPTH, NSEQ, DSEQ, 6, 64)), sts("l_s", (DEPTH, NSEQ, DSEQ, 6)),
        sts("pool_s", (DEPTH, NSEQ, 15, 256)), sts("conv_s", (DEPTH, NSEQ, 3, 640)), sts("ssd_s", (DEPTH, NSEQ, 6, 64, 64)),
    )
    return tuple(np.ascontiguousarray(o, dtype=np.float32) for o in outs)
```

```python
import numpy as np
import ml_dtypes
from contextlib import ExitStack
import concourse.bass as bass
import concourse.mybir as mybir
from concourse.bass_utils import run_bass_kernel_spmd
from concourse.alu_op_type import AluOpType as ALU

F32 = mybir.dt.float32
BF16 = mybir.dt.bfloat16
AF = mybir.ActivationFunctionType
NEG = -30000.0
D = 1024
NIN = 2444
DFF = 2816
NFC = DFF // 128
EPS = 1e-6
META = 16
DSEQ = 32
NSEQ = 4
OQ, OKK, OV, OF, ODT, OZ, OU, OX = 0, 384, 768, 1152, 1158, 1164, 1548, 1804
TM0 = 384
B_PM, B_PF, B_SN, B_FB, B_AL, B_DS, NBC = 0, 1024, 2048, 2432, 2444, 2450, 2456
C_GM, C_GF, C_PS, C_CW, C_CB, NCOL = 0, 8, 16, 18, 38, 43
K_ID, K_TRI, K_ONE, K_S16, K_S32, K_S128, K_MSK, K_RC, K_Z, K_1, K_EPS, NCF = 0, 128, 256, 384, 512, 640, 768, 896, 928, 929, 930, 931
KB_ID, KB_MSK, KB_TRI, KB_S16, KB_S32, KB_S128, NCB = 0, 128, 256, 384, 512, 640, 768
ENGS = ("pe", "act", "dve", "pool")


class U:
    __slots__ = ("name", "w", "r", "excl")

    def __init__(self, name, excl=False):
        self.name = name
        self.w = None
        self.r = {}
        self.excl = excl


class _Rec:
    def __init__(self):
        self.calls = []

    def __getattr__(self, name):
        def f(*a, **kw):
            self.calls.append((name, a, kw))
            return None
        return f


class Prog:
    def __init__(self):
        self.q = {e: [] for e in ENGS + ("sp",)}
        self.cnt = {e: 0 for e in ENGS}
        self.waited = {e: {} for e in ENGS + ("sp",)}
        self.dcnt = {}

    def _waits(self, eng, reads, writes):
        waits = []
        wd = self.waited[eng]

        def need(ev):
            if ev is None:
                return
            k, v = ev
            if wd.get(k, 0) >= v:
                return
            wd[k] = v
            waits.append((k, v))

        for u in reads:
            need(u.w)
        for u in writes:
            need(u.w)
            for k, v in u.r.items():
                need((k, v))
        return waits

    def op(self, eng, fn, reads=(), writes=()):
        ex = [u for u in reads if u.excl]
        if ex:
            reads = [u for u in reads if not u.excl]
            writes = list(writes) + ex
        if callable(fn):
            rec = _Rec()
            fn(rec)
            fn = rec.calls
        assert isinstance(fn, list) and len(fn) > 0, fn
        waits = self._waits(eng, reads, writes)
        self.cnt[eng] += 1
        ev = (eng, self.cnt[eng])
        self.q[eng].append((waits, fn, ev, 1))
        for u in reads:
            if u.r.get(eng, 0) < ev[1]:
                u.r[eng] = ev[1]
        for u in writes:
            u.w = ev
            u.r = {}

    def dma(self, fn, reads, writes, sem):
        waits = self._waits("sp", reads, writes)
        prev = self.dcnt.get(sem, 0)
        if prev > self.waited["sp"].get(sem, 0):
            self.waited["sp"][sem] = prev
            waits.append((sem, prev))
        self.dcnt[sem] = prev + 16
        ev = (sem, self.dcnt[sem])
        self.q["sp"].append((waits, fn, ev, 16))
        for u in reads:
            if u.r.get(sem, 0) < ev[1]:
                u.r[sem] = ev[1]
        for u in writes:
            u.w = ev
            u.r = {}

    def barrier(self):
        for e in ENGS + ("sp",):
            waits = []
            wd = self.waited[e]
            for f in ENGS:
                if f != e and self.cnt[f] > wd.get(f, 0):
                    wd[f] = self.cnt[f]
                    waits.append((f, self.cnt[f]))
            for s, v in self.dcnt.items():
                if v > wd.get(s, 0):
                    wd[s] = v
                    waits.append((s, v))
            if waits:
                self.q[e].append((waits, None, None, 0))


class Buf:
    __slots__ = ("v", "u", "name")

    def __init__(self, v, name):
        self.v = v
        self.u = U(name)
        self.name = name


class Arena:
    def __init__(self, ap, nwords):
        self.ap = ap
        self.nwords = nwords
        self.off = 0
        self.peak = 0

    def alloc(self, name, shape, dt):
        n = 1
        for s in shape:
            n *= s
        words = n if dt == F32 else (n + 1) // 2
        assert self.off + words <= self.nwords, f"arena overflow at {name}: {self.off + words} > {self.nwords}"
        v = self.ap[:, self.off:self.off + words]
        self.off += words
        self.peak = max(self.peak, self.off)
        if dt == BF16:
            v = v.bitcast(BF16)
            if 2 * words != n:
                v = v[:, 0:n]
        if len(shape) == 2:
            v = v.rearrange("p (a b) -> p a b", a=shape[0])
        elif len(shape) == 3:
            v = v.rearrange("p (a b c) -> p a b c", a=shape[0], b=shape[1])
        return Buf(v, name)


class _Stop(Exception):
    pass


def build(SEQ, PAST, DEPTH, stop_at=None):
    T = META + SEQ
    stage = [0]

    def mark_stage(name):
        stage[0] += 1
        if stop_at is not None:
            print("stage", stage[0], name, flush=True)
            if stage[0] >= stop_at:
                raise _Stop()

    NPT = SEQ // 256
    NKB_P = 1 + SEQ // 128
    NCB_S = PAST // 128
    NKB = max(NKB_P, NCB_S + 1)
    NROW = T + NSEQ * DSEQ
    nc = bass.Bass("TRN2", target_bir_lowering=False)

    def din(name, shape, dt=F32):
        return nc.dram_tensor(name, list(shape), dt, kind="ExternalInput").ap()

    def dout(name, shape):
        return nc.dram_tensor(name, list(shape), F32, kind="ExternalOutput").ap()

    def dscr(name, shape, dt=F32):
        return nc.dram_tensor(name, list(shape), dt, kind="Internal").ap()

    I = dict(
        xp=din("xp", [SEQ, D]), meta=din("meta", [META, D]), xs=din("xs", [NSEQ * DSEQ, D]),
        ck=din("ck", [DEPTH, NSEQ, PAST, 384]), cv=din("cv", [DEPTH, NSEQ, PAST, 384]),
        cl=din("cl", [DEPTH, NSEQ, PAST, 6]), spool=din("spool", [DEPTH, NSEQ, 15, 256]),
        sconv=din("sconv", [DEPTH, NSEQ, 3, 640]), sssd=din("sssd", [DEPTH, NSEQ, 6, 64, 64]),
        w_in=din("w_in", [DEPTH, D, NIN]), w_out=din("w_out", [DEPTH, D, D]),
        w_gate=din("w_gate", [DEPTH, D, DFF]), w_up=din("w_up", [DEPTH, D, DFF]),
        w_down=din("w_down", [DEPTH, DFF, D]),
        pbc=din("pbc", [DEPTH, 128, NBC]), pcol=din("pcol", [DEPTH, 128, NCOL]),
        poolw=din("poolw", [DEPTH, 128, 256]),
        cstf=din("cstf", [128, NCF]), cstb=din("cstb", [128, NCB], BF16),
    )
    O = dict(
        y_p=dout("y_p", [SEQ, D]), y_s=dout("y_s", [NSEQ * DSEQ, D]),
        k_p=dout("k_p", [DEPTH, T, 384]), v_p=dout("v_p", [DEPTH, T, 384]), l_p=dout("l_p", [DEPTH, T, 6]),
        pool_p=dout("pool_p", [DEPTH, 15, 256]), conv_p=dout("conv_p", [DEPTH, 3, 640]),
        ssd_p=dout("ssd_p", [DEPTH, 6, 64, 64]),
        k_s=dout("k_s", [DEPTH, NSEQ, DSEQ, 384]), v_s=dout("v_s", [DEPTH, NSEQ, DSEQ, 384]),
        l_s=dout("l_s", [DEPTH, NSEQ, DSEQ, 6]), pool_s=dout("pool_s", [DEPTH, NSEQ, 15, 256]),
        conv_s=dout("conv_s", [DEPTH, NSEQ, 3, 640]), ssd_s=dout("ssd_s", [DEPTH, NSEQ, 6, 64, 64]),
    )
    XA = dscr("xa", [NROW, D])
    XB = dscr("xb", [NROW, D])
    KSC = dscr("ksc", [NKB_P, 128, 768], BF16)
    VSC = dscr("vsc", [NKB_P, 128, 576], BF16)
    uXA = [U(f"xa{i}") for i in range((NROW + 127) // 128 + 1)]
    uXB = [U(f"xb{i}") for i in range((NROW + 127) // 128 + 1)]
    uKSC = [U(f"ksc{i}") for i in range(NKB_P)]
    uIN = U("inputs")
    uOUT = U("outputs")

    P = Prog()
    es = ExitStack()
    ARW = 53100
    arena_t = es.enter_context(nc.sbuf_tensor("arena", [128, ARW], F32))
    AR = Arena(arena_t[:, :], ARW)
    pb0 = es.enter_context(nc.psum_tensor("pb0", [128, 1024], BF16))
    pbs = [pb0] + [es.enter_context(nc.psum_tensor(f"pb{i}", [128, 512], F32)) for i in range(1, 8)]
    pb0f = pb0[:, :].bitcast(F32)
    pu = []
    for i in range(8):
        _u = U(f"ps{i}", excl=True)
        pu.append([_u, _u])

    def psu(bank, c0, c1):
        half = 512 if bank == 0 else 256
        us = []
        if c0 < half:
            us.append(pu[bank][0])
        if c1 > half:
            us.append(pu[bank][1])
        return us

    CF = AR.alloc("cstf", [NCF], F32)
    CB = AR.alloc("cstb", [NCB], BF16)
    PBC = AR.alloc("pbc", [NBC], F32)
    PCOL = AR.alloc("pcol", [NCOL], F32)
    PWF = AR.alloc("pwf", [256], F32)
    PWB = AR.alloc("pwb", [2, 128], BF16)
    ABC = AR.alloc("abc", [6], F32)
    NST = 2
    WCH = 512
    WST = [AR.alloc(f"wst{i}", [WCH], F32) for i in range(NST)]
    XIN = [AR.alloc(f"xin{i}", [D], F32) for i in range(4)]
    XOUT = [AR.alloc(f"xout{i}", [D], F32) for i in range(2)]
    HBF = [AR.alloc(f"hbf{i}", [D], BF16) for i in range(2)]
    HT = AR.alloc("hT", [8, 256], BF16)
    _t1off = AR.off
    T1 = AR.alloc("t1", [D], F32)
    JUNK = Buf(AR.ap[:, _t1off:_t1off + D // 2].bitcast(BF16), "junk")
    JUNK.u = T1.u
    SM = AR.alloc("small", [64], F32)
    smu = [U(f"sm{i}") for i in range(8)]
    mark = AR.off

    ident_f = CF.v[:, K_ID:K_ID + 128]
    tri_f = CF.v[:, K_TRI:K_TRI + 128]
    ones_f = CF.v[:, K_ONE:K_ONE + 128]
    mask_f = CF.v[:, K_MSK:K_MSK + 128]
    zcol = CF.v[:, K_Z:K_Z + 1]
    ecol = CF.v[:, K_EPS:K_EPS + 1]
    ident_b = CB.v[:, KB_ID:KB_ID + 128]
    mask_b = CB.v[:, KB_MSK:KB_MSK + 128]
    tri_b = CB.v[:, KB_TRI:KB_TRI + 128]

    def sel_f(n):
        o = {16: K_S16, 32: K_S32, 128: K_S128}[n]
        return CF.v[:, o:o + 128]

    def sel_b(n):
        o = {16: KB_S16, 32: KB_S32, 128: KB_S128}[n]
        return CB.v[:, o:o + 128]

    dma_q = [0]

    def DMA(out, in_, reads, writes, sem):
        P.dma([("dma_start", (), dict(out=out, in_=in_))], reads, writes, sem)

    DMA(CF.v, I["cstf"][:, :], [uIN], [CF.u], "c_cf")
    DMA(CB.v, I["cstb"][:, :], [uIN], [CB.u], "c_cb")

    wst_ctr = [0]
    STG = list(WST)
    for i_ in range(4):
        STG.append(Buf(XIN[i_].v[:, 0:WCH], f"xstg{i_}a"))
        STG.append(Buf(XIN[i_].v[:, WCH:2 * WCH], f"xstg{i_}b"))
    CAST_ENG = ("pool", "dve", "act")

    def load_cast(dst_view, dst_u, src_ap, ncols, gcol=None):
        st = STG[wst_ctr[0] % len(STG)]
        eng = CAST_ENG[wst_ctr[0] % 3]
        dst_u = U("wchunk")
        wst_ctr[0] += 1
        DMA(st.v[:, 0:ncols], src_ap, [uIN], [st.u], st.name)
        if eng == "act":
            if gcol is None:
                P.op("act", lambda e: e.activation(out=dst_view, in_=st.v[:, 0:ncols], func=AF.Copy), [st.u], [dst_u])
            else:
                P.op("act", lambda e: e.activation(out=dst_view, in_=st.v[:, 0:ncols], func=AF.Copy, scale=gcol), [st.u, PCOL.u], [dst_u])
        elif gcol is None:
            P.op(eng, lambda e: e.tensor_copy(out=dst_view, in_=st.v[:, 0:ncols]), [st.u], [dst_u])
        else:
            P.op(eng, lambda e: e.tensor_scalar(out=dst_view, in0=st.v[:, 0:ncols], scalar1=gcol, scalar2=zcol, op0=ALU.mult, op1=ALU.add),
                 [st.u, PCOL.u, CF.u], [dst_u])

    class Seq:
        pass

    seqs = []
    sp = Seq()
    sp.kind = "p"
    sp.idx = 0
    sp.tiles = [(0, META)] + [(META + 256 * j, 256) for j in range(NPT)]
    sp.npast = 0
    sp.row0 = 0
    sp.kblocks = [(0, META)] + [(META + 128 * m, 128) for m in range(SEQ // 128)]
    seqs.append(sp)
    for s in range(NSEQ):
        q = Seq()
        q.kind = "s"
        q.idx = s
        q.tiles = [(0, DSEQ)]
        q.npast = PAST
        q.row0 = T + DSEQ * s
        q.kblocks = [(128 * m, 128) for m in range(NCB_S)] + [(PAST, DSEQ)]
        seqs.append(q)

    def xsrc(L, sq, t, nb):
        if L == 0:
            if sq.kind == "p":
                if t < META:
                    return I["meta"][t:t + nb, :], uIN
                return I["xp"][t - META:t - META + nb, :], uIN
            return I["xs"][DSEQ * sq.idx + t:DSEQ * sq.idx + t + nb, :], uIN
        r = sq.row0 + t
        return XB[r:r + nb, :], uXB[r // 128]

    try:
      for L in range(DEPTH):
          P.barrier()
          AR.off = mark
          WIN = AR.alloc("win", [8, NIN], BF16)
          WOUT = AR.alloc("wout", [8, D], BF16)
          NRING = 3
          KVRk = [AR.alloc(f"kvrk{i}", [6, 128], BF16) for i in range(NRING)]
          KVRv = [AR.alloc(f"kvrv{i}", [576], BF16) for i in range(NRING)]
          QT = AR.alloc("qT", [6, 256], BF16)
          KTC = AR.alloc("kTc", [6, 256], BF16)
          VCUR = AR.alloc("vcur", [2, 576], BF16)
          KVST = [AR.alloc(f"kvst{i}", [768], F32) for i in range(2)]
          LFST = [AR.alloc(f"lfst{i}", [6], F32) for i in range(2)]
          FD = AR.alloc("fd", [12], F32)
          E12 = AR.alloc("e12", [12], F32)
          L12 = [AR.alloc(f"l12{i}", [12], F32) for i in range(2)]
          LA_ = AR.alloc("la", [6], F32)
          NACS = AR.alloc("nacs", [6], F32)
          EA = AR.alloc("ea", [6], F32)
          DEC = AR.alloc("dec", [6], F32)
          NEGC = AR.alloc("negc", [NKB, 6], F32)
          STA = AR.alloc("sta", [4, 128], BF16)
          CLT = AR.alloc("clt", [6], F32)
          CLTA = AR.alloc("clta", [max(NCB_S, 1), 6], F32)
          ZS = AR.alloc("zs", [2, 384], F32)
          ZE = AR.alloc("ze", [384], F32)
          UT = AR.alloc("uT", [2, 15 + 256], F32)
          PS2 = AR.alloc("ps2", [15 + 256], F32)
          PS4 = AR.alloc("ps4", [15 + 256], F32)
          DIFF = AR.alloc("diff", [2, 256], BF16)
          XBC = AR.alloc("xbcT", [5, 3 + 256], F32)
          ACC = AR.alloc("acc", [256], F32)
          CE = AR.alloc("ce", [256], F32)
          XS = AR.alloc("xsT", [3, 256], F32)
          BT = AR.alloc("bT", [256], BF16)
          CT = AR.alloc("cT", [256], BF16)
          XX = AR.alloc("xx", [6, 128], F32)
          LT = AR.alloc("lt", [6, 128], F32)
          MT = AR.alloc("mt", [6, 128], BF16)
          XSTOK = AR.alloc("xstok", [384], F32)
          XD = AR.alloc("xd", [6, 64], BF16)
          BW = AR.alloc("bw", [6, 128], BF16)
          Y1 = AR.alloc("y1", [384], F32)
          Y2 = AR.alloc("y2", [384], F32)
          YN = AR.alloc("yn", [384], BF16)
          SST = AR.alloc("sst", [192], F32)
          SBF = AR.alloc("sbf", [192], BF16)
          PT = [AR.alloc(f"pt{i}", [256], BF16) for i in range(4)]
          REC = [AR.alloc(f"rec{i}", [256], F32) for i in range(2)]
          MIXT = AR.alloc("mixT", [8, 256], BF16)
          CKS = [AR.alloc(f"cks{i}", [384], F32) for i in range(2)]
          CVS = [AR.alloc(f"cvs{i}", [384], F32) for i in range(2)]
          KB16 = AR.alloc("kb16", [384], BF16)
          STT = AR.alloc("stt", [256], F32)
          SIN = AR.alloc("sin", [3, 128], F32)

          DMA(PBC.v, I["pbc"][L], [uIN], [PBC.u], "c_pbc")
          DMA(PCOL.v, I["pcol"][L], [uIN], [PCOL.u], "c_pcol")
          DMA(PWF.v, I["poolw"][L], [uIN], [PWF.u], "c_pwf")
          P.op("pool", lambda e: e.tensor_copy(out=PWB.v.rearrange("p a b -> p (a b)"), in_=PWF.v), [PWF.u], [PWB.u])
          P.op("act", lambda e: e.activation(out=ABC.v, in_=PBC.v[:, B_AL:B_AL + 6], func=AF.Exp), [PBC.u], [ABC.u])
          P.op("act", lambda e: e.activation(out=ABC.v, in_=ABC.v, func=AF.Copy, scale=-1.0), [ABC.u], [ABC.u])
          for kc in range(8):
              for c0 in range(0, NIN, WCH):
                  c1 = min(NIN, c0 + WCH)
                  load_cast(WIN.v[:, kc, c0:c1], WIN.u, I["w_in"][L, kc * 128:(kc + 1) * 128, c0:c1], c1 - c0,
                            PCOL.v[:, C_GM + kc:C_GM + kc + 1])
          for kc in range(8):
              for c0 in range(0, D, WCH):
                  load_cast(WOUT.v[:, kc, c0:c0 + WCH], WOUT.u, I["w_out"][L, kc * 128:(kc + 1) * 128, c0:c0 + WCH], WCH)
          P.barrier()
          for b_ in [QT, KTC] + KVRk:
              P.op("pool", lambda e, b_=b_: e.memset(b_.v, 0.0), [], [b_.u])
          for b_ in [KTC] + KVRk:
              P.op("pool", lambda e, b_=b_: e.memset(b_.v[64:65, :, :], 1.0), [], [b_.u])
          P.op("pool", lambda e: e.memset(VCUR.v, 1.0), [], [VCUR.u])
          for b_ in KVRv:
              P.op("pool", lambda e, b_=b_: e.memset(b_.v, 1.0), [], [b_.u])
          P.op("pool", lambda e: e.memset(STA.v, 0.0), [], [STA.u])
          P.op("pool", lambda e: e.memset(BW.v, 0.0), [], [BW.u])
          mark_stage('A-weights')

          xin_ctr = [0]
          kvst_ctr = [0]
          ring_ctr = [0]
          pt_ctr = [0]
          st_ctr = [0]
          fm_ctr = [0]
          xo_ctr = [0]
          cks_ctr = [0]

          def do_step1(sq, t0, n):
              blocks = [(o, min(128, n - o)) for o in range(0, n, 128)]
              xins = []
              for bi, (o, nb) in enumerate(blocks):
                  xin = XIN[xin_ctr[0] % 4]
                  xin_ctr[0] += 1
                  xins.append(xin)
                  src, su = xsrc(L, sq, t0 + o, nb)
                  DMA(xin.v[0:nb, :], src, [su], [xin.u], xin.name)
                  hb = HBF[bi % 2]
                  ssq = SM.v[0:nb, 0:1]
                  rs = SM.v[0:nb, 1:2]
                  P.op("act", lambda e, xin=xin, nb=nb, ssq=ssq: e.activation(out=JUNK.v[0:nb, :], in_=xin.v[0:nb, :], func=AF.Square, accum_out=ssq),
                       [xin.u], [JUNK.u, smu[0]])
                  P.op("act", lambda e, nb=nb, ssq=ssq, rs=rs: e.activation(out=rs, in_=ssq, func=AF.Ln, bias=CF.v[0:nb, K_EPS:K_EPS + 1], scale=1.0 / D), [smu[0], CF.u], [smu[0]])
                  P.op("act", lambda e, rs=rs: e.activation(out=rs, in_=rs, func=AF.Exp, scale=-0.5), [smu[0]], [smu[0]])
                  P.op("dve", lambda e, hb=hb, xin=xin, nb=nb, rs=rs: e.tensor_scalar(out=hb.v[0:nb, :], in0=xin.v[0:nb, :], scalar1=rs, scalar2=CF.v[0:nb, K_Z:K_Z + 1], op0=ALU.mult, op1=ALU.add),
                       [xin.u, smu[0], CF.u], [hb.u])

                  def tr(e, hb=hb, nb=nb):
                      r = None
                      for kc in range(8):
                          r = e.transpose(out=pbs[0][:, kc * 128:kc * 128 + nb], in_=hb.v[0:nb, kc * 128:(kc + 1) * 128], identity=ident_b[0:nb, 0:nb])
                      return r
                  P.op("pe", tr, [hb.u, CB.u], pu[0])
                  P.op("act", lambda e, o=o, nb=nb: e.activation(out=HT.v[:, :, o:o + nb], in_=pbs[0][:, :].rearrange("p (a b) -> p a b", a=8)[:, :, 0:nb], func=AF.Copy),
                       pu[0], [HT.u])

              return xins

          s1cache = {}

          for sq in seqs:
              isP = sq.kind == "p"
              nkb = len(sq.kblocks)
              if isP:
                  P.op("pool", lambda e: e.memset(UT.v[:, :, 0:15], 0.0), [], [UT.u])
                  P.op("pool", lambda e: e.memset(XBC.v[:, :, 0:3], 0.0), [], [XBC.u])
                  P.op("pool", lambda e: e.memset(SST.v, 0.0), [], [SST.u])
                  P.op("pool", lambda e: e.memset(SBF.v, 0.0), [], [SBF.u])
              else:
                  s = sq.idx
                  DMA(STT.v[0:15, 0:256], I["spool"][L, s], [uIN], [STT.u], "c_stt")
                  for pt in range(2):
                      P.op("pe", lambda e, pt=pt: e.transpose(out=pbs[7][:, 256 + 16 * pt:256 + 16 * pt + 15], in_=STT.v[0:15, 128 * pt:128 * pt + 128], identity=ident_f[0:15, 0:15]),
                           [STT.u, CF.u], psu(7, 256, 512))
                  P.op("act", lambda e: e.activation(out=UT.v[:, 0, 0:15], in_=pbs[7][:, 256:271], func=AF.Copy), psu(7, 256, 512), [UT.u])
                  P.op("act", lambda e: e.activation(out=UT.v[:, 1, 0:15], in_=pbs[7][:, 272:287], func=AF.Copy), psu(7, 256, 512), [UT.u])
                  DMA(T1.v[0:3, 0:640], I["sconv"][L, s], [uIN], [T1.u], "c_t1")
                  for ct in range(5):
                      P.op("pe", lambda e, ct=ct: e.transpose(out=pbs[7][:, 4 * ct:4 * ct + 3], in_=T1.v[0:3, 128 * ct:128 * ct + 128], identity=ident_f[0:3, 0:3]),
                           [T1.u, CF.u], psu(7, 0, 256))
                  for ct in range(5):
                      P.op("act", lambda e, ct=ct: e.activation(out=XBC.v[:, ct, 0:3], in_=pbs[7][:, 4 * ct:4 * ct + 3], func=AF.Copy), psu(7, 0, 256), [XBC.u])
                  for j in range(3):
                      DMA(SIN.v[0:64, j, 0:64], I["sssd"][L, s, j], [uIN], [SIN.u], f"c_sin{j}a")
                      DMA(SIN.v[0:64, j, 64:128], I["sssd"][L, s, j + 3], [uIN], [SIN.u], f"c_sin{j}b")
                  for j in range(3):
                      P.op("pe", lambda e, j=j: e.transpose(out=pbs[6][:, 64 * j:64 * j + 64], in_=SIN.v[0:64, j, :], identity=ident_f[0:64, 0:64]),
                           [SIN.u, CF.u], psu(6, 0, 256))
                  P.op("act", lambda e: e.activation(out=SST.v, in_=pbs[6][:, 0:192], func=AF.Copy), psu(6, 0, 256), [SST.u])
                  P.op("dve", lambda e: e.tensor_copy(out=SBF.v, in_=SST.v), [SST.u], [SBF.u])

              if not isP:
                  DMA(CLTA.v, I["cl"][L, s].rearrange("(m p) h -> p m h", p=128), [uIN], [CLTA.u], "c_clta")
                  P.op("dve", lambda e: e.tensor_single_scalar(out=CLTA.v, in_=CLTA.v, scalar=-1.0, op=ALU.mult), [CLTA.u], [CLTA.u])

                  def cs_mm(e):
                      r = None
                      for m in range(NCB_S):
                          r = e.matmul(pbs[6][:, 6 * m:6 * m + 6], lhsT=tri_f, rhs=CLTA.v[:, m, :], start=True, stop=(m == 0))
                          for j in range(m):
                              r = e.matmul(pbs[6][:, 6 * m:6 * m + 6], lhsT=ones_f, rhs=CLTA.v[:, j, :], start=False, stop=(j == m - 1))
                      return r
                  P.op("pe", cs_mm, [CLTA.u, CF.u], pu[6])
                  P.op("act", lambda e: e.activation(out=NEGC.v[:, 0:NCB_S, :], in_=pbs[6][:, 0:6 * NCB_S].rearrange("p (m h) -> p m h", h=6), func=AF.Copy), pu[6], [NEGC.u])

              for ti, (t0, n) in enumerate(sq.tiles):
                  blocks = [(o, min(128, n - o)) for o in range(0, n, 128)]
                  kb_first = (sq.npast // 128) + (0 if (not isP) else (0 if ti == 0 else 1 + 2 * (ti - 1)))
                  if isP:
                      kbs = [0] if ti == 0 else [1 + 2 * (ti - 1), 2 + 2 * (ti - 1)]
                  else:
                      kbs = [NCB_S]
                  key_ = (sq.kind, sq.idx, ti)
                  xins = s1cache.pop(key_) if key_ in s1cache else do_step1(sq, t0, n)

                  mark_stage('step1')
                  l12s = []
                  for bi, (o, nb) in enumerate(blocks):
                      kb = kbs[bi]

                      def tm(e, o=o, nb=nb):
                          r = None
                          for (c0, c1, bank, pc) in ((0, 384, 3, 0), (384, 768, 4, 0), (768, 780, 5, 0), (780, 1164, 5, 128)):
                              for kc in range(8):
                                  r = e.matmul(pbs[bank][0:nb, pc:pc + c1 - c0], lhsT=HT.v[:, kc, o:o + nb], rhs=WIN.v[:, kc, TM0 + c0:TM0 + c1], start=(kc == 0), stop=(kc == 7))
                          return r
                      P.op("pe", tm, [HT.u, WIN.u], pu[3] + pu[4] + pu[5])
                      kv = KVST[kvst_ctr[0] % 2]
                      lf = LFST[kvst_ctr[0] % 2]
                      l12 = L12[kvst_ctr[0] % 2]
                      kvst_ctr[0] += 1
                      l12s.append(l12)
                      P.op("act", lambda e, kv=kv, nb=nb: e.activation(out=kv.v[0:nb, 0:384], in_=pbs[3][0:nb, 0:384], func=AF.Copy), pu[3], [kv.u])
                      P.op("dve", lambda e, kv=kv, nb=nb: e.tensor_copy(out=kv.v[0:nb, 384:768], in_=pbs[4][0:nb, 0:384]), pu[4], [kv.u])
                      vsrc = kv.v[0:nb, 384:768].rearrange("p (j t d) -> p j t d", j=3, t=2)
                      vdst = VCUR.v[0:nb, bi, :].rearrange("p (j x) -> p j x", j=3)
                      P.op("pool", lambda e, vsrc=vsrc, vdst=vdst: e.tensor_copy(out=vdst[:, :, 0:64], in_=vsrc[:, :, 0, :]), [kv.u], [VCUR.u])
                      P.op("pool", lambda e, vsrc=vsrc, vdst=vdst: e.tensor_copy(out=vdst[:, :, 128:192], in_=vsrc[:, :, 1, :]), [kv.u], [VCUR.u])
                      if isP:
                          ko, vo, lo = O["k_p"][L, t0 + o:t0 + o + nb, :], O["v_p"][L, t0 + o:t0 + o + nb, :], O["l_p"][L, t0 + o:t0 + o + nb, :]
                      else:
                          ko, vo, lo = O["k_s"][L, sq.idx, o:o + nb, :], O["v_s"][L, sq.idx, o:o + nb, :], O["l_s"][L, sq.idx, o:o + nb, :]
                      DMA(ko, kv.v[0:nb, 0:384], [kv.u], [uOUT], kv.name + "_k")
                      DMA(vo, kv.v[0:nb, 384:768], [kv.u], [uOUT], kv.name + "_v")
                      P.op("dve", lambda e, nb=nb: e.tensor_tensor(out=FD.v[0:nb, :], in0=pbs[5][0:nb, 0:12], in1=PBC.v[0:nb, B_FB:B_FB + 12], op=ALU.add), psu(5, 0, 12) + [PBC.u], [FD.u])
                      P.op("act", lambda e, nb=nb: e.activation(out=E12.v[0:nb, 0:6], in_=FD.v[0:nb, 0:6], func=AF.Exp, scale=-1.0), [FD.u], [E12.u])
                      P.op("act", lambda e, nb=nb: e.activation(out=E12.v[0:nb, 6:12], in_=FD.v[0:nb, 6:12], func=AF.Exp), [FD.u], [E12.u])
                      P.op("act", lambda e, nb=nb, l12=l12: e.activation(out=l12.v[0:nb, :], in_=E12.v[0:nb, :], func=AF.Ln, bias=1.0), [E12.u], [l12.u])
                      P.op("dve", lambda e, nb=nb, l12=l12, lf=lf: e.tensor_single_scalar(out=lf.v[0:nb, :], in_=l12.v[0:nb, 0:6], scalar=-1.0, op=ALU.mult), [l12.u], [lf.u])
                      DMA(lo, lf.v[0:nb, :], [lf.u], [uOUT], lf.name)
                      P.op("act", lambda e, nb=nb: e.activation(out=ZE.v[0:nb, :], in_=pbs[5][0:nb, 128:512], func=AF.Exp, scale=-1.0), pu[5], [ZE.u])
                      P.op("act", lambda e, nb=nb: e.activation(out=ZE.v[0:nb, :], in_=ZE.v[0:nb, :], func=AF.Ln, bias=1.0), [ZE.u], [ZE.u])
                      P.op("act", lambda e, nb=nb: e.activation(out=ZE.v[0:nb, :], in_=ZE.v[0:nb, :], func=AF.Exp, scale=-1.0), [ZE.u], [ZE.u])
                      P.op("dve", lambda e, nb=nb, bi=bi: e.tensor_tensor(out=ZS.v[0:nb, bi, :], in0=pbs[5][0:nb, 128:512], in1=ZE.v[0:nb, :], op=ALU.mult), pu[5] + [ZE.u], [ZS.u])

                      if isP:
                          prev = None if kb == 0 else (kb - 1, sq.kblocks[kb - 1][1])
                      else:
                          prev = (kb - 1, 128)

                      def cmm(e, nb=nb, l12=l12, prev=prev):
                          r = e.matmul(pbs[6][0:nb, 384:390], lhsT=tri_f[0:nb, 0:nb], rhs=l12.v[0:nb, 0:6], start=True, stop=(prev is None))
                          if prev is not None:
                              pk, pn = prev
                              r = e.matmul(pbs[6][0:nb, 384:390], lhsT=sel_f(pn)[0:pn, 0:nb], rhs=NEGC.v[0:pn, pk, :], start=False, stop=True)
                          return r
                      P.op("pe", cmm, [l12.u, CF.u, NEGC.u], psu(6, 256, 512))
                      P.op("act", lambda e, nb=nb, kb=kb: e.activation(out=NEGC.v[0:nb, kb, :], in_=pbs[6][0:nb, 384:390], func=AF.Copy), psu(6, 256, 512), [NEGC.u])
                      P.op("dve", lambda e, nb=nb, l12=l12: e.tensor_single_scalar(out=STA.v[0:nb, 0, 0:128:32], in_=l12.v[0:nb, 0:4], scalar=-1.0, op=ALU.mult), [l12.u], [STA.u])
                      P.op("dve", lambda e, nb=nb, l12=l12: e.tensor_single_scalar(out=STA.v[0:nb, 1, 0:64:32], in_=l12.v[0:nb, 4:6], scalar=-1.0, op=ALU.mult), [l12.u], [STA.u])
                      if prev is not None:
                          pk, pn = prev
                          P.op("dve", lambda e, pk=pk, pn=pn: e.tensor_single_scalar(out=STA.v[0:pn, 2, 0:128:32], in_=NEGC.v[0:pn, pk, 0:4], scalar=-1.0, op=ALU.mult), [NEGC.u], [STA.u])
                          P.op("dve", lambda e, pk=pk, pn=pn: e.tensor_single_scalar(out=STA.v[0:pn, 3, 0:64:32], in_=NEGC.v[0:pn, pk, 4:6], scalar=-1.0, op=ALU.mult), [NEGC.u], [STA.u])

                      def smm(e, nb=nb, prev=prev):
                          r = None
                          for a in range(2):
                              r = e.matmul(pbs[6][:, 128 * a:128 * a + nb], lhsT=STA.v[0:nb, a, :], rhs=tri_b[0:nb, 0:nb], start=True, stop=(prev is None))
                              if prev is not None:
                                  pn = prev[1]
                                  r = e.matmul(pbs[6][:, 128 * a:128 * a + nb], lhsT=STA.v[0:pn, 2 + a, :], rhs=sel_b(pn)[0:pn, 0:nb], start=False, stop=True)
                          return r
                      P.op("pe", smm, [STA.u, CB.u], psu(6, 0, 256))
                      for h in range(6):
                          a, i = (0, h) if h < 4 else (1, h - 4)
                          P.op("act", lambda e, h=h, a=a, i=i, o=o, nb=nb: e.activation(out=QT.v[64:65, h, o:o + nb], in_=pbs[6][32 * i:32 * i + 1, 128 * a:128 * a + nb], func=AF.Copy),
                               psu(6, 0, 256), [QT.u])

                  mark_stage('step2a+fox')
                  def fm(c0, ncols, evac):
                      slot = fm_ctr[0] % 3
                      fm_ctr[0] += 1
                      bank, cc = (1, 2, 7)[slot], 0
                      us = [pu[bank][0]]

                      def mm(e):
                          r = None
                          for kc in range(8):
                              r = e.matmul(pbs[bank][0:ncols, cc:cc + n], lhsT=WIN.v[:, kc, c0:c0 + ncols], rhs=HT.v[:, kc, 0:n], start=(kc == 0), stop=(kc == 7))
                          return r
                      P.op("pe", mm, [WIN.u, HT.u], us)
                      evac(pbs[bank][:, cc:cc + n], us)

                  for j in range(3):
                      def evq(ps, us, j=j):
                          P.op("act", lambda e: e.activation(out=QT.v[0:64, 2 * j, 0:n], in_=ps[0:64, :], func=AF.Copy, scale=0.125), us, [QT.u])
                          P.op("act", lambda e: e.activation(out=QT.v[0:64, 2 * j + 1, 0:n], in_=ps[64:128, :], func=AF.Copy, scale=0.125), us, [QT.u])
                      fm(OQ + 128 * j, 128, evq)
                  for j in range(3):
                      def evk(ps, us, j=j):
                          P.op("act", lambda e: e.activation(out=KTC.v[0:64, 2 * j, 0:n], in_=ps[0:64, :], func=AF.Copy), us, [KTC.u])
                          P.op("dve", lambda e: e.tensor_copy(out=KTC.v[0:64, 2 * j + 1, 0:n], in_=ps[64:128, :]), us, [KTC.u])
                      fm(OKK + 128 * j, 128, evk)
                  for pt in range(2):
                      def evu(ps, us, pt=pt):
                          P.op("dve", lambda e: e.tensor_copy(out=UT.v[:, pt, 15:15 + n], in_=ps), us, [UT.u])
                      fm(OU + 128 * pt, 128, evu)
                  for ct in range(5):
                      def evx(ps, us, ct=ct):
                          P.op("act", lambda e: e.activation(out=XBC.v[:, ct, 3:3 + n], in_=ps, func=AF.Copy), us, [XBC.u])
                      fm(OX + 128 * ct, 128, evx)

                  mark_stage('step2b')
                  if isP and ti < len(sq.tiles) - 1:
                      for bi, (o, nb) in enumerate(blocks):
                          kb = kbs[bi]
                          DMA(KSC[kb].rearrange("p (h k) -> p h k", h=6)[:, :, 0:nb], KTC.v[:, :, o:o + nb], [KTC.u], [uKSC[kb]], f"c_ktc{bi}")
                          DMA(VSC[kb][0:nb, :], VCUR.v[0:nb, bi, :], [VCUR.u], [uKSC[kb]], f"c_vcur{bi}")

                  def gen_mix():
                      ln_ = 15 + n
                      for pt in range(2):
                          E_ = UT.v[:, pt, :]
                          P.op("dve", lambda e, E_=E_: e.tensor_tensor(out=PS2.v[:, 1:ln_], in0=E_[:, 1:ln_], in1=E_[:, 0:ln_ - 1], op=ALU.add), [UT.u], [PS2.u])
                          P.op("dve", lambda e: e.tensor_tensor(out=PS4.v[:, 3:ln_], in0=PS2.v[:, 3:ln_], in1=PS2.v[:, 1:ln_ - 2], op=ALU.add), [PS2.u], [PS4.u])
                          if pt == 0:
                              lo, hi = PS2, PS4
                          else:
                              P.op("dve", lambda e: e.tensor_tensor(out=PS2.v[:, 7:ln_], in0=PS4.v[:, 7:ln_], in1=PS4.v[:, 3:ln_ - 4], op=ALU.add), [PS4.u], [PS2.u])
                              P.op("dve", lambda e: e.tensor_tensor(out=PS4.v[:, 15:ln_], in0=PS2.v[:, 15:ln_], in1=PS2.v[:, 7:ln_ - 8], op=ALU.add), [PS2.u], [PS4.u])
                              lo, hi = PS2, PS4
                          for (r0, sb) in ((0, lo), (64, hi)):
                              if isP and ti == 0:
                                  P.op("dve", lambda e, r0=r0, sb=sb, pt=pt: e.tensor_tensor(out=ACC.v[r0:r0 + 64, 0:n], in0=sb.v[r0:r0 + 64, 15:15 + n], in1=CF.v[r0:r0 + 64, K_RC + 16 * pt:K_RC + 16 * pt + n], op=ALU.mult),
                                       [sb.u, CF.u], [ACC.u])
                                  P.op("dve", lambda e, r0=r0, pt=pt, E_=E_: e.tensor_tensor(out=DIFF.v[r0:r0 + 64, pt, 0:n], in0=ACC.v[r0:r0 + 64, 0:n], in1=E_[r0:r0 + 64, 15:15 + n], op=ALU.subtract),
                                       [ACC.u, UT.u], [DIFF.u])
                              else:
                                  P.op("dve", lambda e, r0=r0, sb=sb, pt=pt, E_=E_: e.scalar_tensor_tensor(out=DIFF.v[r0:r0 + 64, pt, 0:n], in0=sb.v[r0:r0 + 64, 15:15 + n],
                                                                                                   scalar=CF.v[r0:r0 + 64, K_RC + 16 * pt + 15:K_RC + 16 * pt + 16], in1=E_[r0:r0 + 64, 15:15 + n], op0=ALU.mult, op1=ALU.subtract),
                                       [sb.u, CF.u, UT.u], [DIFF.u])
                          P.op("pe", lambda e, pt=pt: e.matmul(pbs[7][:, 0:n], lhsT=PWB.v[:, pt, :], rhs=DIFF.v[:, pt, 0:n], start=True, stop=True), [PWB.u, DIFF.u], psu(7, 0, 256))
                          yield
                          P.op("act", lambda e, pt=pt: e.activation(out=MIXT.v[:, 3 + pt, 0:n], in_=pbs[7][:, 0:n], func=AF.Copy, scale=PCOL.v[:, C_PS + pt:C_PS + pt + 1]), psu(7, 0, 256) + [PCOL.u], [MIXT.u])
                      P.op("pool", lambda e: e.tensor_copy(out=UT.v[:, :, 0:15], in_=UT.v[:, :, n:n + 15]), [UT.u], [UT.u])

                      for ct in range(5):
                          cw = PCOL.v[:, C_CW + 4 * ct:C_CW + 4 * ct + 4]
                          cb_ = PCOL.v[:, C_CB + ct:C_CB + ct + 1]
                          P.op("pool", lambda e, ct=ct, cw=cw, cb_=cb_: e.tensor_scalar(out=ACC.v[:, 0:n], in0=XBC.v[:, ct, 0:n], scalar1=cw[:, 0:1], scalar2=cb_, op0=ALU.mult, op1=ALU.add),
                               [XBC.u, PCOL.u], [ACC.u])
                          for j in range(1, 4):
                              P.op("dve", lambda e, ct=ct, cw=cw, j=j: e.scalar_tensor_tensor(out=ACC.v[:, 0:n], in0=XBC.v[:, ct, j:j + n], scalar=cw[:, j:j + 1], in1=ACC.v[:, 0:n], op0=ALU.mult, op1=ALU.add),
                                   [XBC.u, PCOL.u], [ACC.u])
                          P.op("act", lambda e: e.activation(out=CE.v[:, 0:n], in_=ACC.v[:, 0:n], func=AF.Exp, scale=-1.0), [ACC.u], [CE.u])
                          P.op("act", lambda e: e.activation(out=CE.v[:, 0:n], in_=CE.v[:, 0:n], func=AF.Ln, bias=1.0), [CE.u], [CE.u])
                          P.op("act", lambda e: e.activation(out=CE.v[:, 0:n], in_=CE.v[:, 0:n], func=AF.Exp, scale=-1.0), [CE.u], [CE.u])
                          if ct < 3:
                              dst, du = XS.v[:, ct, 0:n], XS.u
                          elif ct == 3:
                              dst, du = BT.v[:, 0:n], BT.u
                          else:
                              dst, du = CT.v[:, 0:n], CT.u
                          P.op("dve", lambda e, dst=dst: e.tensor_tensor(out=dst, in0=ACC.v[:, 0:n], in1=CE.v[:, 0:n], op=ALU.mult), [ACC.u, CE.u], [du])
                          yield
                      P.op("pool", lambda e: e.tensor_copy(out=XBC.v[:, :, 0:3], in_=XBC.v[:, :, n:n + 3]), [XBC.u], [XBC.u])

                      for bi, (o, nb) in enumerate(blocks):
                          l12 = l12s[bi]
                          P.op("dve", lambda e, nb=nb, l12=l12: e.tensor_tensor(out=LA_.v[0:nb, :], in0=l12.v[0:nb, 6:12], in1=ABC.v[0:nb, :], op=ALU.mult), [l12.u, ABC.u], [LA_.u])
                          P.op("pe", lambda e, nb=nb: e.matmul(pbs[6][0:nb, 400:406], lhsT=tri_f[0:nb, 0:nb], rhs=LA_.v[0:nb, :], start=True, stop=True), [LA_.u, CF.u], psu(6, 256, 512))
                          P.op("act", lambda e, nb=nb: e.activation(out=NACS.v[0:nb, :], in_=pbs[6][0:nb, 400:406], func=AF.Copy, scale=-1.0), psu(6, 256, 512), [NACS.u])
                          P.op("act", lambda e, nb=nb: e.activation(out=EA.v[0:nb, :], in_=pbs[6][0:nb, 400:406], func=AF.Exp), psu(6, 256, 512), [EA.u])
                          P.op("dve", lambda e, nb=nb: e.tensor_tensor(out=XX.v[0:nb, :, 0:nb], in0=tri_f[0:nb, 0:nb].unsqueeze(1).to_broadcast([nb, 6, nb]),
                                                                      in1=LA_.v[0:nb, :].unsqueeze(2).to_broadcast([nb, 6, nb]), op=ALU.mult), [LA_.u, CF.u], [XX.u])
                          abank = lambda h: ((5, 7)[h // 4], 128 * (h % 4))

                          def amm(e, nb=nb):
                              r = None
                              for h in range(6):
                                  b_, c_ = abank(h)
                                  e.matmul(pbs[b_][:, c_:c_ + nb], lhsT=ones_f[0:nb, :], rhs=XX.v[0:nb, h, 0:nb], start=True, stop=False)
                                  r = e.matmul(pbs[b_][:, c_:c_ + nb], lhsT=ident_f[0:nb, :], rhs=mask_f[0:nb, 0:nb], start=False, stop=True)
                              return r
                          P.op("pe", amm, [XX.u, CF.u], [pu[5][0], pu[7][0]])
                          yield
                          for h in range(6):
                              b_, c_ = abank(h)
                              P.op("act", lambda e, h=h, b_=b_, c_=c_, nb=nb: e.activation(out=LT.v[0:nb, h, 0:nb], in_=pbs[b_][0:nb, c_:c_ + nb], func=AF.Exp, bias=NACS.v[0:nb, h:h + 1]),
                                   [pu[b_][0], NACS.u], [LT.u])

                          P.op("act", lambda e, nb=nb: e.activation(out=DEC.v[:, 0:4], in_=pbs[5][:, nb - 1:512:128], func=AF.Exp), [pu[5][0]], [DEC.u])
                          P.op("act", lambda e, nb=nb: e.activation(out=DEC.v[:, 4:6], in_=pbs[7][:, nb - 1:256:128], func=AF.Exp), [pu[7][0]], [DEC.u])

                          def gmm(e, o=o, nb=nb):
                              r = None
                              for g in range(2):
                                  r = e.matmul(pbs[(5, 7)[g]][0:nb, 0:nb], lhsT=BT.v[64 * g:64 * g + 64, o:o + nb], rhs=CT.v[64 * g:64 * g + 64, o:o + nb], start=True, stop=True)
                              return r
                          P.op("pe", gmm, [BT.u, CT.u], [pu[5][0], pu[7][0]])
                          yield
                          for h in range(6):
                              g = h // 3
                              P.op("dve", lambda e, h=h, g=g, nb=nb: e.tensor_tensor(out=MT.v[0:nb, h, 0:nb], in0=pbs[(5, 7)[g]][0:nb, 0:nb], in1=LT.v[0:nb, h, 0:nb], op=ALU.mult),
                                   [pu[(5, 7)[g]][0], LT.u], [MT.u])

                          def xtr(e, o=o, nb=nb):
                              r = None
                              for j in range(3):
                                  r = e.transpose(out=pbs[7][0:nb, 128 * j:128 * j + 128], in_=XS.v[:, j, o:o + nb], identity=ident_f)
                              return r
                          P.op("pe", xtr, [XS.u, CF.u], pu[7])
                          yield
                          P.op("pe", lambda e, o=o, nb=nb: e.transpose(out=pbs[0][0:nb, 0:128], in_=BT.v[:, o:o + nb], identity=ident_b), [BT.u, CB.u], pu[0])
                          P.op("act", lambda e, nb=nb: e.activation(out=XSTOK.v[0:nb, :], in_=pbs[7][0:nb, 0:384], func=AF.Copy), pu[7], [XSTOK.u])
                          P.op("dve", lambda e, nb=nb, l12=l12: e.tensor_tensor(out=XD.v[0:nb, :, :], in0=XSTOK.v[0:nb, :].rearrange("p (h d) -> p h d", h=6),
                                                                               in1=l12.v[0:nb, 6:12].unsqueeze(2).to_broadcast([nb, 6, 64]), op=ALU.mult), [XSTOK.u, l12.u], [XD.u])
                          for h in range(6):
                              g = h // 3
                              P.op("act", lambda e, h=h, g=g, nb=nb: e.activation(out=BW.v[0:nb, h, 64 * g:64 * g + 64], in_=pbs[0][0:nb, 64 * g:64 * g + 64], func=AF.Copy, scale=LT.v[0:nb, h, nb - 1:nb]),
                                   pu[0] + [LT.u], [BW.u])

                          def ymm(e, o=o, nb=nb):
                              r = None
                              for h in range(6):
                                  r = e.matmul(pbs[6][0:nb, 64 * h:64 * h + 64], lhsT=MT.v[0:nb, h, 0:nb], rhs=XD.v[0:nb, h, :], start=True, stop=True)
                              for g in range(2):
                                  r = e.matmul(pbs[(5, 7)[g]][0:nb, 0:192], lhsT=CT.v[64 * g:64 * g + 64, o:o + nb], rhs=SBF.v[64 * g:64 * g + 64, :], start=True, stop=True)
                              for h in range(6):
                                  r = e.matmul(pb0f[:, 64 * h:64 * h + 64], lhsT=BW.v[0:nb, h, :], rhs=XD.v[0:nb, h, :], start=True, stop=True)
                              return r
                          P.op("pe", ymm, [MT.u, XD.u, CT.u, SBF.u, BW.u], [pu[6][0], pu[5][0], pu[7][0], pu[0][0]])
                          yield
                          for h in range(6):
                              g, j = h // 3, h % 3
                              P.op("dve", lambda e, h=h, g=g, j=j: e.scalar_tensor_tensor(out=SST.v[64 * g:64 * g + 64, 64 * j:64 * j + 64], in0=SST.v[64 * g:64 * g + 64, 64 * j:64 * j + 64],
                                                                                       scalar=DEC.v[64 * g:64 * g + 64, h:h + 1], in1=pb0f[64 * g:64 * g + 64, 64 * h:64 * h + 64], op0=ALU.mult, op1=ALU.add),
                                   [SST.u, DEC.u] + pu[0], [SST.u])
                          P.op("dve", lambda e: e.tensor_copy(out=SBF.v, in_=SST.v), [SST.u], [SBF.u])
                          yield
                          for g in range(2):
                              P.op("dve", lambda e, nb=nb, g=g: e.tensor_tensor(out=Y1.v[0:nb, 192 * g:192 * g + 192].rearrange("p (h d) -> p h d", h=3), in0=pbs[(5, 7)[g]][0:nb, 0:192].rearrange("p (h d) -> p h d", h=3),
                                                                               in1=EA.v[0:nb, 3 * g:3 * g + 3].unsqueeze(2).to_broadcast([nb, 3, 64]), op=ALU.mult), [pu[(5, 7)[g]][0], EA.u], [Y1.u])
                          P.op("dve", lambda e, nb=nb: e.tensor_tensor(out=Y1.v[0:nb, :], in0=pbs[6][0:nb, 0:384], in1=Y1.v[0:nb, :], op=ALU.add), pu[6] + [Y1.u], [Y1.u])
                          P.op("pool", lambda e, nb=nb: e.tensor_tensor(out=Y2.v[0:nb, :].rearrange("p (h d) -> p h d", h=6), in0=XSTOK.v[0:nb, :].rearrange("p (h d) -> p h d", h=6),
                                                                       in1=PBC.v[0:nb, B_DS:B_DS + 6].unsqueeze(2).to_broadcast([nb, 6, 64]), op=ALU.mult), [XSTOK.u, PBC.u], [Y2.u])
                          P.op("dve", lambda e, nb=nb: e.tensor_tensor(out=Y1.v[0:nb, :], in0=Y1.v[0:nb, :], in1=Y2.v[0:nb, :], op=ALU.add), [Y1.u, Y2.u], [Y1.u])
                          P.op("dve", lambda e, nb=nb, bi=bi: e.tensor_tensor(out=Y1.v[0:nb, :], in0=Y1.v[0:nb, :], in1=ZS.v[0:nb, bi, :], op=ALU.mult), [Y1.u, ZS.u], [Y1.u])
                          ssq = SM.v[0:nb, 2:3]
                          rs = SM.v[0:nb, 3:4]
                          P.op("act", lambda e, nb=nb, ssq=ssq: e.activation(out=Y2.v[0:nb, :], in_=Y1.v[0:nb, :], func=AF.Square, accum_out=ssq), [Y1.u], [Y2.u, smu[1]])
                          P.op("act", lambda e, nb=nb, ssq=ssq, rs=rs: e.activation(out=rs, in_=ssq, func=AF.Ln, bias=CF.v[0:nb, K_EPS:K_EPS + 1], scale=1.0 / 384), [smu[1], CF.u], [smu[1]])
                          P.op("act", lambda e, rs=rs: e.activation(out=rs, in_=rs, func=AF.Exp, scale=-0.5), [smu[1]], [smu[1]])
                          P.op("dve", lambda e, nb=nb, rs=rs: e.scalar_tensor_tensor(out=YN.v[0:nb, :], in0=Y1.v[0:nb, :], scalar=rs, in1=PBC.v[0:nb, B_SN:B_SN + 384], op0=ALU.mult, op1=ALU.mult),
                               [Y1.u, smu[1], PBC.u], [YN.u])
                          yield

                          def ytr(e, nb=nb):
                              r = None
                              for j in range(3):
                                  r = e.transpose(out=pbs[0][:, 512 + 128 * j:512 + 128 * j + nb], in_=YN.v[0:nb, 128 * j:128 * j + 128], identity=ident_b[0:nb, 0:nb])
                              return r
                          P.op("pe", ytr, [YN.u, CB.u], pu[0])
                          yield
                          P.op("act", lambda e, o=o, nb=nb: e.activation(out=MIXT.v[:, 5:8, o:o + nb], in_=pbs[0][:, 512:896].rearrange("p (a b) -> p a b", a=3)[:, :, 0:nb], func=AF.Copy), pu[0], [MIXT.u])


                  def gen_att():
                      q0 = sq.npast + t0
                      vis = []
                      for kb, (ks, kn) in enumerate(sq.kblocks):
                          if ks > q0 + n - 1:
                              break
                          i0 = max(0, ks - q0)
                          diag = ks + kn - 1 > q0
                          vis.append((kb, ks, kn, i0, diag))
                      otu = lambda h: (5 + h // 2, 256 * (h % 2))
                      pend = []
                      SKEW = 3
                      for hp in range(1):
                          heads = list(range(6))
                          for vi, (kb, ks, kn, i0, diag) in enumerate(vis):
                              nq = n - i0
                              cur = ks >= q0
                              if cur:
                                  bi = (ks - q0) // 128
                                  kt_ap = KTC.v[:, :, ks - q0:ks - q0 + kn]
                                  v_ap = VCUR.v[0:kn, bi, :]
                                  kvu = [KTC.u, VCUR.u]
                              else:
                                  r_ = ring_ctr[0] % NRING
                                  ring_ctr[0] += 1
                                  rk, rv = KVRk[r_], KVRv[r_]
                                  kt_ap = rk.v[:, :, 0:kn]
                                  v_ap = rv.v[0:kn, :]
                                  kvu = [rk.u, rv.u]
                                  if isP:
                                      DMA(rk.v[:, :, 0:kn], KSC[kb].rearrange("p (h k) -> p h k", h=6)[:, :, 0:kn], [uKSC[kb]], [rk.u], rk.name)
                                      DMA(rv.v[0:kn, :], VSC[kb][0:kn, :], [uKSC[kb]], [rv.u], rv.name)
                                  else:
                                      ck_, cv_ = CKS[cks_ctr[0] % 2], CVS[cks_ctr[0] % 2]
                                      cks_ctr[0] += 1
                                      DMA(ck_.v, I["ck"][L, sq.idx, ks:ks + 128, :], [uIN], [ck_.u], ck_.name)
                                      DMA(cv_.v, I["cv"][L, sq.idx, ks:ks + 128, :], [uIN], [cv_.u], cv_.name)
                                      P.op("dve", lambda e, ck_=ck_: e.tensor_copy(out=KB16.v, in_=ck_.v), [ck_.u], [KB16.u])

                                      def ktr(e):
                                          r = None
                                          for j in range(3):
                                              r = e.transpose(out=pbs[0][:, 128 * j:128 * j + 128], in_=KB16.v[:, 128 * j:128 * j + 128], identity=ident_b)
                                          return r
                                      P.op("pe", ktr, [KB16.u, CB.u], pu[0])
                                      for h in range(6):
                                          P.op("act", lambda e, h=h, rk=rk: e.activation(out=rk.v[0:64, h, :], in_=pbs[0][64 * (h % 2):64 * (h % 2) + 64, 128 * (h // 2):128 * (h // 2) + 128], func=AF.Copy),
                                               pu[0], [rk.u])
                                      vsrc = cv_.v.rearrange("p (j t d) -> p j t d", j=3, t=2)
                                      vdst = rv.v.rearrange("p (j x) -> p j x", j=3)
                                      P.op("dve", lambda e, vsrc=vsrc, vdst=vdst: e.tensor_copy(out=vdst[:, :, 0:64], in_=vsrc[:, :, 0, :]), [cv_.u], [rv.u])
                                      P.op("pool", lambda e, vsrc=vsrc, vdst=vdst: e.tensor_copy(out=vdst[:, :, 128:192], in_=vsrc[:, :, 1, :]), [cv_.u], [rv.u])
                              if vi == 0:
                                  for h in heads:
                                      ob, oc = otu(h)
                                      P.op("dve", lambda e, ob=ob, oc=oc: e.memset(pbs[ob][:, oc:oc + n], 0.0), [], [pu[ob][0]])
                              for h in heads:
                                  sl = st_ctr[0] % 4
                                  st_ctr[0] += 1
                                  sb_, sc_ = 1 + sl, 0
                                  sus = [pu[sb_][0]]
                                  mw = min(kn, nq)

                                  def smm2(e, h=h, kt_ap=kt_ap, sb_=sb_, sc_=sc_, kn=kn, nq=nq, i0=i0, diag=diag, mw=mw):
                                      r = e.matmul(pbs[sb_][0:kn, sc_:sc_ + nq], lhsT=kt_ap[:, h, :], rhs=QT.v[:, h, i0:n], start=True, stop=not diag)
                                      if diag:
                                          r = e.matmul(pbs[sb_][0:kn, sc_:sc_ + mw], lhsT=ident_b[0:kn, 0:kn], rhs=mask_b[0:kn, 0:mw], start=False, stop=True)
                                      return r
                                  P.op("pe", smm2, kvu + [QT.u, CB.u], sus)
                                  pt_ = PT[pt_ctr[0] % 4]
                                  pt_ctr[0] += 1
                                  P.op("act", lambda e, h=h, pt_=pt_, sb_=sb_, sc_=sc_, kn=kn, nq=nq, kb=kb: e.activation(out=pt_.v[0:kn, 0:nq], in_=pbs[sb_][0:kn, sc_:sc_ + nq], func=AF.Exp, bias=NEGC.v[0:kn, kb, h:h + 1]),
                                       sus + [NEGC.u], [pt_.u])
                                  ob, oc = otu(h)
                                  vc0 = 192 * (h // 2) + 64 * (h % 2)
                                  pend.append((lambda e, pt_=pt_, ob=ob, oc=oc, v_ap=v_ap, vc0=vc0, kn=kn, nq=nq, i0=i0: e.matmul(pbs[ob][:, oc + i0:oc + n], lhsT=v_ap[:, vc0:vc0 + 128], rhs=pt_.v[0:kn, 0:nq],
                                                                                                                      start=False, stop=False, skip_group_check=True),
                                               kvu + [pt_.u], [pu[ob][0]]))
                                  if len(pend) > SKEW:
                                      f_, r_s, w_s = pend.pop(0)
                                      P.op("pe", f_, r_s, w_s)
                                  yield
                          while pend:
                              f_, r_s, w_s = pend.pop(0)
                              P.op("pe", f_, r_s, w_s)
                          for h in heads:
                              ob, oc = otu(h)
                              rc_ = REC[h % 2]
                              nr, dr = (0, 64) if h % 2 == 0 else (64, 0)
                              P.op("act", lambda e, ob=ob, oc=oc, rc_=rc_, dr=dr: e.activation(out=rc_.v[dr:dr + 64, 0:n], in_=pbs[ob][dr:dr + 64, oc:oc + n], func=AF.Ln), [pu[ob][0]], [rc_.u])
                              P.op("act", lambda e, rc_=rc_, dr=dr: e.activation(out=rc_.v[dr:dr + 64, 0:n], in_=rc_.v[dr:dr + 64, 0:n], func=AF.Exp, scale=-1.0), [rc_.u], [rc_.u])
                              P.op("dve", lambda e, h=h, ob=ob, oc=oc, rc_=rc_, nr=nr, dr=dr: e.tensor_tensor(out=MIXT.v[nr:nr + 64, h // 2, 0:n], in0=pbs[ob][nr:nr + 64, oc:oc + n], in1=rc_.v[dr:dr + 64, 0:n], op=ALU.mult),
                                   [pu[ob][0], rc_.u], [MIXT.u])
                          yield

                  if ti + 1 < len(sq.tiles):
                      nsq, nti = sq, ti + 1
                  else:
                      k_ = seqs.index(sq)
                      nsq, nti = (seqs[k_ + 1], 0) if k_ + 1 < len(seqs) else (None, 0)
                  if nsq is not None:
                      s1cache[(nsq.kind, nsq.idx, nti)] = do_step1(nsq, nsq.tiles[nti][0], nsq.tiles[nti][1])
                  for _ in gen_mix():
                      pass
                  for _ in gen_att():
                      pass


                  mark_stage('attn')
                  for bi, (o, nb) in enumerate(blocks):
                      def omm(e, o=o, nb=nb):
                          r = None
                          for c in range(2):
                              for kt in range(8):
                                  r = e.matmul(pbs[6 + c][0:nb, :], lhsT=MIXT.v[:, kt, o:o + nb], rhs=WOUT.v[:, kt, 512 * c:512 * c + 512], start=(kt == 0), stop=(kt == 7))
                          return r
                      P.op("pe", omm, [MIXT.u, WOUT.u], pu[6] + pu[7])
                      xo = XOUT[xo_ctr[0] % 2]
                      xo_ctr[0] += 1
                      post_norm_res(P, pbs, pu, 6, 7, nb, xins[bi], xo, B_PM, JUNK, SM, smu, CF, PBC, T1)
                      r = sq.row0 + t0 + o
                      DMA(XA[r:r + nb, :], xo.v[0:nb, :], [xo.u], [uXA[r // 128]], xo.name)

              mark_stage('seq-tiles-done')
              s = sq.idx
              for pt in range(2):
                  P.op("pe", lambda e, pt=pt: e.transpose(out=pbs[7][0:15, 128 * pt:128 * pt + 128], in_=UT.v[:, pt, 0:15], identity=ident_f), [UT.u, CF.u], pu[7])
              P.op("act", lambda e: e.activation(out=STT.v[0:15, 0:256], in_=pbs[7][0:15, 0:256], func=AF.Copy), pu[7], [STT.u])
              DMA(O["pool_p"][L] if isP else O["pool_s"][L, s], STT.v[0:15, 0:256], [STT.u], [uOUT], "c_stt")
              for ct in range(5):
                  P.op("pe", lambda e, ct=ct: e.transpose(out=pbs[3 + ct // 4][0:3, 128 * (ct % 4):128 * (ct % 4) + 128], in_=XBC.v[:, ct, 0:3], identity=ident_f), [XBC.u, CF.u], pu[3] + pu[4])
              P.op("act", lambda e: e.activation(out=T1.v[0:3, 0:512], in_=pbs[3][0:3, 0:512], func=AF.Copy), pu[3], [T1.u])
              P.op("act", lambda e: e.activation(out=T1.v[0:3, 512:640], in_=pbs[4][0:3, 0:128], func=AF.Copy), pu[4], [T1.u])
              DMA(O["conv_p"][L] if isP else O["conv_s"][L, s], T1.v[0:3, 0:640], [T1.u], [uOUT], "c_t1")
              for j in range(3):
                  P.op("pe", lambda e, j=j: e.transpose(out=pbs[5][0:64, 128 * j:128 * j + 128], in_=SST.v[:, 64 * j:64 * j + 64], identity=ident_f), [SST.u, CF.u], pu[5])
              P.op("act", lambda e: e.activation(out=SIN.v[0:64, :, :], in_=pbs[5][0:64, 0:384].rearrange("p (a b) -> p a b", a=3), func=AF.Copy), pu[5], [SIN.u])
              od = O["ssd_p"][L] if isP else O["ssd_s"][L, s]
              for j in range(3):
                  DMA(od[j], SIN.v[0:64, j, 0:64], [SIN.u], [uOUT], f"c_so{j}a")
                  DMA(od[j + 3], SIN.v[0:64, j, 64:128], [SIN.u], [uOUT], f"c_so{j}b")

          mark_stage('phaseA-done')
          P.barrier()
          AR.off = mark
          WG = AR.alloc("wg", [8, DFF], BF16)
          WU = AR.alloc("wu", [8, DFF], BF16)
          WD = AR.alloc("wd", [NFC, D], BF16)
          H1 = AR.alloc("h1", [NFC, 256], BF16)
          GE = [AR.alloc(f"ge{i}", [256], F32) for i in range(3)]
          for kc in range(8):
              for c0 in range(0, DFF, WCH):
                  c1 = min(DFF, c0 + WCH)
                  load_cast(WG.v[:, kc, c0:c1], WG.u, I["w_gate"][L, kc * 128:(kc + 1) * 128, c0:c1], c1 - c0, PCOL.v[:, C_GF + kc:C_GF + kc + 1])
                  load_cast(WU.v[:, kc, c0:c1], WU.u, I["w_up"][L, kc * 128:(kc + 1) * 128, c0:c1], c1 - c0, PCOL.v[:, C_GF + kc:C_GF + kc + 1])
          for fc in range(NFC):
              for c0 in range(0, D, WCH):
                  load_cast(WD.v[:, fc, c0:c0 + WCH], WD.u, I["w_down"][L, fc * 128:(fc + 1) * 128, c0:c0 + WCH], WCH)

          P.barrier()
          btiles = [(0, META)] + [(META + 256 * j, 256) for j in range(NPT)] + [(T, NSEQ * DSEQ)]
          xin_ctr = [0]
          xo_ctr = [0]
          gu_ctr = [0]
          HT2 = AR.alloc("hT2", [8, 256], BF16)
          HTS = [HT, HT2]
          bstate = {}

          def b_front(ti):
              r0, n = btiles[ti]
              HTc = HTS[ti % 2]
              blocks = [(o, min(128, n - o)) for o in range(0, n, 128)]
              xins = []
              for bi, (o, nb) in enumerate(blocks):
                  xin = XIN[xin_ctr[0] % 4]
                  xin_ctr[0] += 1
                  xins.append(xin)
                  r = r0 + o
                  DMA(xin.v[0:nb, :], XA[r:r + nb, :], [uXA[r // 128]] + ([uXA[(r + nb - 1) // 128]] if (r + nb - 1) // 128 != r // 128 else []), [xin.u], xin.name)
                  hb = HBF[bi % 2]
                  ssq = SM.v[0:nb, 0:1]
                  rs = SM.v[0:nb, 1:2]
                  P.op("act", lambda e, xin=xin, nb=nb, ssq=ssq: e.activation(out=JUNK.v[0:nb, :], in_=xin.v[0:nb, :], func=AF.Square, accum_out=ssq), [xin.u], [JUNK.u, smu[0]])
                  P.op("act", lambda e, nb=nb, ssq=ssq, rs=rs: e.activation(out=rs, in_=ssq, func=AF.Ln, bias=CF.v[0:nb, K_EPS:K_EPS + 1], scale=1.0 / D), [smu[0], CF.u], [smu[0]])
                  P.op("act", lambda e, rs=rs: e.activation(out=rs, in_=rs, func=AF.Exp, scale=-0.5), [smu[0]], [smu[0]])
                  P.op("dve", lambda e, hb=hb, xin=xin, nb=nb, rs=rs: e.tensor_scalar(out=hb.v[0:nb, :], in0=xin.v[0:nb, :], scalar1=rs, scalar2=CF.v[0:nb, K_Z:K_Z + 1], op0=ALU.mult, op1=ALU.add),
                       [xin.u, smu[0], CF.u], [hb.u])

                  def tr(e, hb=hb, nb=nb):
                      r_ = None
                      for kc in range(8):
                          r_ = e.transpose(out=pbs[0][:, kc * 128:kc * 128 + nb], in_=hb.v[0:nb, kc * 128:(kc + 1) * 128], identity=ident_b[0:nb, 0:nb])
                      return r_
                  P.op("pe", tr, [hb.u, CB.u], pu[0])
                  P.op("act", lambda e, o=o, nb=nb: e.activation(out=HTc.v[:, :, o:o + nb], in_=pbs[0][:, :].rearrange("p (a b) -> p a b", a=8)[:, :, 0:nb], func=AF.Copy), pu[0], [HTc.u])
              bstate[ti] = (blocks, xins)

          def b_gu(ti):
              r0, n = btiles[ti]
              HTc = HTS[ti % 2]
              for fc in range(NFC):
                  sl = gu_ctr[0] % 3
                  gu_ctr[0] += 1
                  (gb_, gc_), (ub_, uc_) = (1 + sl, 0), (1 + sl, 256)
                  gus = [pu[gb_][0], pu[ub_][0]]

                  def gmm2(e, fc=fc, gb_=gb_, gc_=gc_, ub_=ub_, uc_=uc_):
                      r_ = None
                      for kc in range(8):
                          r_ = e.matmul(pbs[gb_][:, gc_:gc_ + n], lhsT=WG.v[:, kc, 128 * fc:128 * fc + 128], rhs=HTc.v[:, kc, 0:n], start=(kc == 0), stop=(kc == 7))
                      for kc in range(8):
                          r_ = e.matmul(pbs[ub_][:, uc_:uc_ + n], lhsT=WU.v[:, kc, 128 * fc:128 * fc + 128], rhs=HTc.v[:, kc, 0:n], start=(kc == 0), stop=(kc == 7))
                      return r_
                  P.op("pe", gmm2, [WG.u, WU.u, HTc.u], gus)
                  ge = GE[sl]
                  P.op("act", lambda e, ge=ge, gb_=gb_, gc_=gc_: e.activation(out=ge.v[:, 0:n], in_=pbs[gb_][:, gc_:gc_ + n], func=AF.Exp, scale=-1.0), [gus[0]], [ge.u])
                  P.op("act", lambda e, ge=ge: e.activation(out=ge.v[:, 0:n], in_=ge.v[:, 0:n], func=AF.Ln, bias=1.0), [ge.u], [ge.u])
                  P.op("act", lambda e, ge=ge: e.activation(out=ge.v[:, 0:n], in_=ge.v[:, 0:n], func=AF.Exp, scale=-1.0), [ge.u], [ge.u])
                  P.op("dve", lambda e, ge=ge, gb_=gb_, gc_=gc_: e.tensor_tensor(out=ge.v[:, 0:n], in0=pbs[gb_][:, gc_:gc_ + n], in1=ge.v[:, 0:n], op=ALU.mult), [gus[0], ge.u], [ge.u])
                  P.op("dve", lambda e, ge=ge, ub_=ub_, uc_=uc_, fc=fc: e.tensor_tensor(out=H1.v[:, fc, 0:n], in0=pbs[ub_][:, uc_:uc_ + n], in1=ge.v[:, 0:n], op=ALU.mult), [gus[1], ge.u], [H1.u])

          def b_down(ti):
              r0, n = btiles[ti]
              blocks, xins = bstate[ti]
              for bi, (o, nb) in enumerate(blocks):
                  ba, bb = (4, 5) if bi % 2 == 0 else (6, 7)

                  def dmm(e, o=o, nb=nb, ba=ba, bb=bb):
                      r_ = None
                      for c, bk in ((0, ba), (1, bb)):
                          for fc in range(NFC):
                              r_ = e.matmul(pbs[bk][0:nb, :], lhsT=H1.v[:, fc, o:o + nb], rhs=WD.v[:, fc, 512 * c:512 * c + 512], start=(fc == 0), stop=(fc == NFC - 1))
                      return r_
                  P.op("pe", dmm, [H1.u, WD.u], pu[ba] + pu[bb])
                  xo = XOUT[xo_ctr[0] % 2]
                  xo_ctr[0] += 1
                  post_norm_res(P, pbs, pu, ba, bb, nb, xins[bi], xo, B_PF, JUNK, SM, smu, CF, PBC, T1)
                  r = r0 + o
                  if L < DEPTH - 1:
                      DMA(XB[r:r + nb, :], xo.v[0:nb, :], [xo.u], [uXB[r // 128], uXB[(r + nb - 1) // 128]], xo.name)
                  else:
                      if r0 == 0:
                          pass
                      elif r0 < T:
                          DMA(O["y_p"][r - META:r - META + nb, :], xo.v[0:nb, :], [xo.u], [uOUT], xo.name)
                      else:
                          DMA(O["y_s"][r - T:r - T + nb, :], xo.v[0:nb, :], [xo.u], [uOUT], xo.name)


          b_front(0)
          for ti in range(len(btiles)):
              b_gu(ti)
              if ti + 1 < len(btiles):
                  b_front(ti + 1)
              b_down(ti)
    except _Stop:
        pass
    P.barrier()
    semkeys = list(ENGS) + sorted(P.dcnt.keys())
    sems = {k: es.enter_context(nc.semaphore(k)) for k in semkeys}

    def replay(qn, e):
        for waits, fn, ev, inc in P.q[qn]:
            for k, v in waits:
                e.wait_ge(sems[k], v)
            if fn is None:
                continue
            ins = None
            for m_, a_, kw_ in fn:
                ins = getattr(e, m_)(*a_, **kw_)
            ins.then_inc(sems[ev[0]], inc)

    with nc.Block() as block:
        @block.sync
        def _(e):
            replay("sp", e)

        @block.tensor
        def _(e):
            replay("pe", e)

        @block.scalar
        def _(e):
            replay("act", e)

        @block.vector
        def _(e):
            replay("dve", e)

        @block.gpsimd
        def _(e):
            replay("pool", e)
    es.close()
    return nc


def post_norm_res(P, pbs, pu, ba, bb, nb, xin, xo, goff, JUNK, SM, smu, CF, PBC, T1):
    s0 = SM.v[0:nb, 4:5]
    s1 = SM.v[0:nb, 5:6]
    rs = SM.v[0:nb, 6:7]
    P.op("act", lambda e: e.activation(out=JUNK.v[0:nb, 0:512], in_=pbs[ba][0:nb, :], func=AF.Square, accum_out=s0), pu[ba], [JUNK.u, smu[2]])
    P.op("act", lambda e: e.activation(out=JUNK.v[0:nb, 512:1024], in_=pbs[bb][0:nb, :], func=AF.Square, accum_out=s1), pu[bb], [JUNK.u, smu[2]])
    P.op("dve", lambda e: e.tensor_tensor(out=s0, in0=s0, in1=s1, op=ALU.add), [smu[2]], [smu[2]])
    P.op("act", lambda e: e.activation(out=rs, in_=s0, func=AF.Ln, bias=CF.v[0:nb, K_EPS:K_EPS + 1], scale=1.0 / D), [smu[2], CF.u], [smu[2]])
    P.op("act", lambda e: e.activation(out=rs, in_=rs, func=AF.Exp, scale=-0.5), [smu[2]], [smu[2]])
    P.op("dve", lambda e: e.scalar_tensor_tensor(out=T1.v[0:nb, 0:512], in0=pbs[ba][0:nb, :], scalar=rs, in1=PBC.v[0:nb, goff:goff + 512], op0=ALU.mult, op1=ALU.mult),
         pu[ba] + [smu[2], PBC.u], [T1.u])
    P.op("dve", lambda e: e.scalar_tensor_tensor(out=T1.v[0:nb, 512:1024], in0=pbs[bb][0:nb, :], scalar=rs, in1=PBC.v[0:nb, goff + 512:goff + 1024], op0=ALU.mult, op1=ALU.mult),
         pu[bb] + [smu[2], PBC.u], [T1.u])
    P.op("dve", lambda e: e.tensor_tensor(out=xo.v[0:nb, :], in0=T1.v[0:nb, :], in1=xin.v[0:nb, :], op=ALU.add), [T1.u, xin.u], [xo.u])


def _consts():
    cf = np.zeros((128, NCF), np.float32)
    p = np.arange(128)
    cf[:, K_ID:K_ID + 128] = np.eye(128, dtype=np.float32)
    cf[:, K_TRI:K_TRI + 128] = (p[:, None] <= p[None, :]).astype(np.float32)
    cf[:, K_ONE:K_ONE + 128] = 1.0
    cf[15, K_S16:K_S16 + 128] = 1.0
    cf[31, K_S32:K_S32 + 128] = 1.0
    cf[127, K_S128:K_S128 + 128] = 1.0
    cf[:, K_MSK:K_MSK + 128] = np.where(p[:, None] <= p[None, :], 0.0, NEG).astype(np.float32)
    for pt in range(2):
        for pp in range(128):
            g = 2 * pt + (1 if pp >= 64 else 0)
            w = 2 ** (g + 1)
            for pos in range(16):
                cf[pp, K_RC + 16 * pt + pos] = 1.0 / min(pos + 1, w)
    cf[:, K_Z] = 0.0
    cf[:, K_1] = 1.0
    cf[:, K_EPS] = EPS
    cb = np.zeros((128, NCB), np.float32)
    cb[:, KB_ID:KB_ID + 128] = cf[:, K_ID:K_ID + 128]
    cb[:, KB_MSK:KB_MSK + 128] = cf[:, K_MSK:K_MSK + 128]
    cb[:, KB_TRI:KB_TRI + 128] = cf[:, K_TRI:K_TRI + 128]
    cb[:, KB_S16:KB_S16 + 128] = cf[:, K_S16:K_S16 + 128]
    cb[:, KB_S32:KB_S32 + 128] = cf[:, K_S32:K_S32 + 128]
    cb[:, KB_S128:KB_S128 + 128] = cf[:, K_S128:K_S128 + 128]
    return cf, cb.astype(ml_dtypes.bfloat16)


_CACHE = {}


def kernel(x_prompt, x_sample, cache_fox_k, cache_fox_v, cache_fox_logf, state_pool, state_conv,
           state_ssd, meta_tokens, ln_pre_mix, ln_post_mix, ln_pre_ffn, ln_post_ffn, w_in,
           fox_f_bias, pool_w, pool_scale, conv_w, conv_b, dt_bias, a_log, d_skip, ssd_norm,
           w_out, w_gate, w_up, w_down):
    f = lambda a: np.ascontiguousarray(np.asarray(a, dtype=np.float32))
    x_prompt, x_sample = f(x_prompt), f(x_sample)
    BATCH, SEQ, _ = x_prompt.shape
    DEPTH, DB, PAST = cache_fox_k.shape[0], cache_fox_k.shape[1], cache_fox_k.shape[2]
    NCORE = 8
    assert DB == NCORE * NSEQ and x_sample.shape[1] == DSEQ
    T = META + SEQ
    key = (SEQ, PAST, DEPTH)
    if key not in _CACHE:
        _CACHE[key] = build(SEQ, PAST, DEPTH)
    nc = _CACHE[key]
    perm = np.concatenate([np.arange(0, 384), np.arange(384, 768), np.arange(768, 1152), np.arange(1152, 1158),
                           np.arange(2438, 2444), np.arange(1414, 1798), np.arange(1158, 1414), np.arange(1798, 2438)])
    w_in_r = np.ascontiguousarray(f(w_in)[:, :, perm])
    pbc = np.zeros((DEPTH, 128, NBC), np.float32)
    pcol = np.zeros((DEPTH, 128, NCOL), np.float32)
    poolw = np.zeros((DEPTH, 128, 256), np.float32)
    for l in range(DEPTH):
        pbc[l, :, B_PM:B_PM + 1024] = f(ln_post_mix)[l][None]
        pbc[l, :, B_PF:B_PF + 1024] = f(ln_post_ffn)[l][None]
        pbc[l, :, B_SN:B_SN + 384] = f(ssd_norm)[l][None]
        pbc[l, :, B_FB:B_FB + 6] = f(fox_f_bias)[l][None]
        pbc[l, :, B_FB + 6:B_FB + 12] = f(dt_bias)[l][None]
        pbc[l, :, B_AL:B_AL + 6] = f(a_log)[l][None]
        pbc[l, :, B_DS:B_DS + 6] = f(d_skip)[l][None]
        pcol[l, :, C_GM:C_GM + 8] = f(ln_pre_mix)[l].reshape(8, 128).T
        pcol[l, :, C_GF:C_GF + 8] = f(ln_pre_ffn)[l].reshape(8, 128).T
        pcol[l, :, C_PS:C_PS + 2] = f(pool_scale)[l].reshape(2, 128).T
        cw = f(conv_w)[l]
        for ct in range(5):
            pcol[l, :, C_CW + 4 * ct:C_CW + 4 * ct + 4] = cw[:, 128 * ct:128 * ct + 128].T
        pcol[l, :, C_CB:C_CB + 5] = f(conv_b)[l].reshape(5, 128).T
        pw = f(pool_w)[l]
        for g in range(4):
            pt, hf = g // 2, g % 2
            poolw[l, 64 * hf:64 * hf + 64, 128 * pt + 64 * hf:128 * pt + 64 * hf + 64] = pw[g]
    cf, cb = _consts()
    shared = dict(w_in=w_in_r, w_out=f(w_out), w_gate=f(w_gate), w_up=f(w_up), w_down=f(w_down),
                  pbc=pbc, pcol=pcol, poolw=poolw, cstf=cf, cstb=cb, meta=f(meta_tokens))
    ck, cv, cl = f(cache_fox_k), f(cache_fox_v), f(cache_fox_logf)
    sp_, sc_, ss_ = f(state_pool), f(state_conv), f(state_ssd)
    in_maps = []
    for c in range(NCORE):
        sl = slice(NSEQ * c, NSEQ * c + NSEQ)
        m = dict(shared)
        m["xp"] = x_prompt[c % BATCH]
        m["xs"] = np.ascontiguousarray(x_sample[sl].reshape(NSEQ * DSEQ, D))
        m["ck"] = np.ascontiguousarray(ck[:, sl].reshape(DEPTH, NSEQ, PAST, 384))
        m["cv"] = np.ascontiguousarray(cv[:, sl].reshape(DEPTH, NSEQ, PAST, 384))
        m["cl"] = np.ascontiguousarray(cl[:, sl])
        m["spool"] = np.ascontiguousarray(sp_[:, sl])
        m["sconv"] = np.ascontiguousarray(sc_[:, sl])
        m["sssd"] = np.ascontiguousarray(ss_[:, sl])
        in_maps.append(m)
    res = run_bass_kernel_spmd(nc, in_maps, core_ids=list(range(NCORE)))
    R = res.results
    y_prompt = np.stack([R[b]["y_p"] for b in range(BATCH)], 0)
    y_sample = np.concatenate([R[c]["y_s"].reshape(NSEQ, DSEQ, D) for c in range(NCORE)], 0)
    stp = lambda k, shp: np.stack([R[b][k].reshape(shp) for b in range(BATCH)], 1)
    sts = lambda k, shp: np.concatenate([R[c][k].reshape(shp) for c in range(NCORE)], 1)
    outs = (
        y_prompt, y_sample,
        stp("k_p", (DEPTH, T, 6, 64)), stp("v_p", (DEPTH, T, 6, 64)), stp("l_p", (DEPTH, T, 6)),
        stp("pool_p", (DEPTH, 15, 256)), stp("conv_p", (DEPTH, 3, 640)), stp("ssd_p", (DEPTH, 6, 64, 64)),
        sts("k_s", (DEPTH, NSEQ, DSEQ, 6, 64)), sts("v_s", (DEPTH, NSEQ, DSEQ, 6, 64)), sts("l_s", (DEPTH, NSEQ, DSEQ, 6)),
        sts("pool_s", (DEPTH, NSEQ, 15, 256)), sts("conv_s", (DEPTH, NSEQ, 3, 640)), sts("ssd_s", (DEPTH, NSEQ, 6, 64, 64)),
    )
    return tuple(np.ascontiguousarray(o, dtype=np.float32) for o in outs)
```

```python
import numpy as np
import ml_dtypes
from contextlib import ExitStack
import concourse.bass as bass
import concourse.mybir as mybir
from concourse.bass_utils import run_bass_kernel_spmd
from concourse.alu_op_type import AluOpType as ALU

F32 = mybir.dt.float32
BF16 = mybir.dt.bfloat16
AF = mybir.ActivationFunctionType
NEG = -30000.0
D = 1024
NIN = 2444
DFF = 2816
NFC = DFF // 128
EPS = 1e-6
META = 16
DSEQ = 32
NSEQ = 4
OQ, OKK, OV, OF, ODT, OZ, OU, OX = 0, 384, 768, 1152, 1158, 1164, 1548, 1804
TM0 = 384
B_PM, B_PF, B_SN, B_FB, B_AL, B_DS, NBC = 0, 1024, 2048, 2432, 2444, 2450, 2456
C_GM, C_GF, C_PS, C_CW, C_CB, NCOL = 0, 8, 16, 18, 38, 43
K_ID, K_TRI, K_ONE, K_S16, K_S32, K_S128, K_MSK, K_RC, K_Z, K_1, K_EPS, NCF = 0, 128, 256, 384, 512, 640, 768, 896, 928, 929, 930, 931
KB_ID, KB_MSK, KB_TRI, KB_S16, KB_S32, KB_S128, NCB = 0, 128, 256, 384, 512, 640, 768
ENGS = ("pe", "act", "dve", "pool")


class U:
    __slots__ = ("name", "w", "r", "excl")

    def __init__(self, name, excl=False):
        self.name = name
        self.w = None
        self.r = {}
        self.excl = excl


class _Rec:
    def __init__(self):
        self.calls = []

    def __getattr__(self, name):
        def f(*a, **kw):
            self.calls.append((name, a, kw))
            return None
        return f


class Prog:
    def __init__(self):
        self.q = {e: [] for e in ENGS + ("sp",)}
        self.cnt = {e: 0 for e in ENGS}
        self.waited = {e: {} for e in ENGS + ("sp",)}
        self.dcnt = {}

    def _waits(self, eng, reads, writes):
        waits = []
        wd = self.waited[eng]

        def need(ev):
            if ev is None:
                return
            k, v = ev
            if wd.get(k, 0) >= v:
                return
            wd[k] = v
            waits.append((k, v))

        for u in reads:
            need(u.w)
        for u in writes:
            need(u.w)
            for k, v in u.r.items():
                need((k, v))
        return waits

    def op(self, eng, fn, reads=(), writes=()):
        ex = [u for u in reads if u.excl]
        if ex:
            reads = [u for u in reads if not u.excl]
            writes = list(writes) + ex
        if callable(fn):
            rec = _Rec()
            fn(rec)
            fn = rec.calls
        assert isinstance(fn, list) and len(fn) > 0, fn
        waits = self._waits(eng, reads, writes)
        self.cnt[eng] += 1
        ev = (eng, self.cnt[eng])
        self.q[eng].append((waits, fn, ev, 1))
        for u in reads:
            if u.r.get(eng, 0) < ev[1]:
                u.r[eng] = ev[1]
        for u in writes:
            u.w = ev
            u.r = {}

    def dma(self, fn, reads, writes, sem):
        waits = self._waits("sp", reads, writes)
        prev = self.dcnt.get(sem, 0)
        if prev > self.waited["sp"].get(sem, 0):
            self.waited["sp"][sem] = prev
            waits.append((sem, prev))
        self.dcnt[sem] = prev + 16
        ev = (sem, self.dcnt[sem])
        self.q["sp"].append((waits, fn, ev, 16))
        for u in reads:
            if u.r.get(sem, 0) < ev[1]:
                u.r[sem] = ev[1]
        for u in writes:
            u.w = ev
            u.r = {}

    def barrier(self):
        for e in ENGS + ("sp",):
            waits = []
            wd = self.waited[e]
            for f in ENGS:
                if f != e and self.cnt[f] > wd.get(f, 0):
                    wd[f] = self.cnt[f]
                    waits.append((f, self.cnt[f]))
            for s, v in self.dcnt.items():
                if v > wd.get(s, 0):
                    wd[s] = v
                    waits.append((s, v))
            if waits:
                self.q[e].append((waits, None, None, 0))


class Buf:
    __slots__ = ("v", "u", "name")

    def __init__(self, v, name):
        self.v = v
        self.u = U(name)
        self.name = name


class Arena:
    def __init__(self, ap, nwords):
        self.ap = ap
        self.nwords = nwords
        self.off = 0
        self.peak = 0

    def alloc(self, name, shape, dt):
        n = 1
        for s in shape:
            n *= s
        words = n if dt == F32 else (n + 1) // 2
        assert self.off + words <= self.nwords, f"arena overflow at {name}: {self.off + words} > {self.nwords}"
        v = self.ap[:, self.off:self.off + words]
        self.off += words
        self.peak = max(self.peak, self.off)
        if dt == BF16:
            v = v.bitcast(BF16)
            if 2 * words != n:
                v = v[:, 0:n]
        if len(shape) == 2:
            v = v.rearrange("p (a b) -> p a b", a=shape[0])
        elif len(shape) == 3:
            v = v.rearrange("p (a b c) -> p a b c", a=shape[0], b=shape[1])
        return Buf(v, name)


class _Stop(Exception):
    pass


def build(SEQ, PAST, DEPTH, stop_at=None):
    T = META + SEQ
    stage = [0]

    def mark_stage(name):
        stage[0] += 1
        if stop_at is not None:
            print("stage", stage[0], name, flush=True)
            if stage[0] >= stop_at:
                raise _Stop()

    NPT = SEQ // 256
    NKB_P = 1 + SEQ // 128
    NCB_S = PAST // 128
    NKB = max(NKB_P, NCB_S + 1)
    NROW = T + NSEQ * DSEQ
    nc = bass.Bass("TRN2", target_bir_lowering=False)

    def din(name, shape, dt=F32):
        return nc.dram_tensor(name, list(shape), dt, kind="ExternalInput").ap()

    def dout(name, shape):
        return nc.dram_tensor(name, list(shape), F32, kind="ExternalOutput").ap()

    def dscr(name, shape, dt=F32):
        return nc.dram_tensor(name, list(shape), dt, kind="Internal").ap()

    I = dict(
        xp=din("xp", [SEQ, D]), meta=din("meta", [META, D]), xs=din("xs", [NSEQ * DSEQ, D]),
        ck=din("ck", [DEPTH, NSEQ, PAST, 384]), cv=din("cv", [DEPTH, NSEQ, PAST, 384]),
        cl=din("cl", [DEPTH, NSEQ, PAST, 6]), spool=din("spool", [DEPTH, NSEQ, 15, 256]),
        sconv=din("sconv", [DEPTH, NSEQ, 3, 640]), sssd=din("sssd", [DEPTH, NSEQ, 6, 64, 64]),
        w_in=din("w_in", [DEPTH, D, NIN]), w_out=din("w_out", [DEPTH, D, D]),
        w_gate=din("w_gate", [DEPTH, D, DFF]), w_up=din("w_up", [DEPTH, D, DFF]),
        w_down=din("w_down", [DEPTH, DFF, D]),
        pbc=din("pbc", [DEPTH, 128, NBC]), pcol=din("pcol", [DEPTH, 128, NCOL]),
        poolw=din("poolw", [DEPTH, 128, 256]),
        cstf=din("cstf", [128, NCF]), cstb=din("cstb", [128, NCB], BF16),
    )
    O = dict(
        y_p=dout("y_p", [SEQ, D]), y_s=dout("y_s", [NSEQ * DSEQ, D]),
        k_p=dout("k_p", [DEPTH, T, 384]), v_p=dout("v_p", [DEPTH, T, 384]), l_p=dout("l_p", [DEPTH, T, 6]),
        pool_p=dout("pool_p", [DEPTH, 15, 256]), conv_p=dout("conv_p", [DEPTH, 3, 640]),
        ssd_p=dout("ssd_p", [DEPTH, 6, 64, 64]),
        k_s=dout("k_s", [DEPTH, NSEQ, DSEQ, 384]), v_s=dout("v_s", [DEPTH, NSEQ, DSEQ, 384]),
        l_s=dout("l_s", [DEPTH, NSEQ, DSEQ, 6]), pool_s=dout("pool_s", [DEPTH, NSEQ, 15, 256]),
        conv_s=dout("conv_s", [DEPTH, NSEQ, 3, 640]), ssd_s=dout("ssd_s", [DEPTH, NSEQ, 6, 64, 64]),
    )
    XA = dscr("xa", [NROW, D])
    XB = dscr("xb", [NROW, D])
    KSC = dscr("ksc", [NKB_P, 128, 768], BF16)
    VSC = dscr("vsc", [NKB_P, 128, 576], BF16)
    uXA = [U(f"xa{i}") for i in range((NROW + 127) // 128 + 1)]
    uXB = [U(f"xb{i}") for i in range((NROW + 127) // 128 + 1)]
    uKSC = [U(f"ksc{i}") for i in range(NKB_P)]
    uIN = U("inputs")
    uOUT = U("outputs")

    P = Prog()
    es = ExitStack()
    ARW = 53100
    arena_t = es.enter_context(nc.sbuf_tensor("arena", [128, ARW], F32))
    AR = Arena(arena_t[:, :], ARW)
    pb0 = es.enter_context(nc.psum_tensor("pb0", [128, 1024], BF16))
    pbs = [pb0] + [es.enter_context(nc.psum_tensor(f"pb{i}", [128, 512], F32)) for i in range(1, 8)]
    pb0f = pb0[:, :].bitcast(F32)
    pu = []
    for i in range(8):
        _u = U(f"ps{i}", excl=True)
        pu.append([_u, _u])

    def psu(bank, c0, c1):
        half = 512 if bank == 0 else 256
        us = []
        if c0 < half:
            us.append(pu[bank][0])
        if c1 > half:
            us.append(pu[bank][1])
        return us

    CF = AR.alloc("cstf", [NCF], F32)
    CB = AR.alloc("cstb", [NCB], BF16)
    PBC = AR.alloc("pbc", [NBC], F32)
    PCOL = AR.alloc("pcol", [NCOL], F32)
    PWF = AR.alloc("pwf", [256], F32)
    PWB = AR.alloc("pwb", [2, 128], BF16)
    ABC = AR.alloc("abc", [6], F32)
    NST = 2
    WCH = 512
    WST = [AR.alloc(f"wst{i}", [WCH], F32) for i in range(NST)]
    XIN = [AR.alloc(f"xin{i}", [D], F32) for i in range(4)]
    XOUT = [AR.alloc(f"xout{i}", [D], F32) for i in range(2)]
    HBF = [AR.alloc(f"hbf{i}", [D], BF16) for i in range(2)]
    HT = AR.alloc("hT", [8, 256], BF16)
    _t1off = AR.off
    T1 = AR.alloc("t1", [D], F32)
    JUNK = Buf(AR.ap[:, _t1off:_t1off + D // 2].bitcast(BF16), "junk")
    JUNK.u = T1.u
    SM = AR.alloc("small", [64], F32)
    smu = [U(f"sm{i}") for i in range(8)]
    mark = AR.off

    ident_f = CF.v[:, K_ID:K_ID + 128]
    tri_f = CF.v[:, K_TRI:K_TRI + 128]
    ones_f = CF.v[:, K_ONE:K_ONE + 128]
    mask_f = CF.v[:, K_MSK:K_MSK + 128]
    zcol = CF.v[:, K_Z:K_Z + 1]
    ecol = CF.v[:, K_EPS:K_EPS + 1]
    ident_b = CB.v[:, KB_ID:KB_ID + 128]
    mask_b = CB.v[:, KB_MSK:KB_MSK + 128]
    tri_b = CB.v[:, KB_TRI:KB_TRI + 128]

    def sel_f(n):
        o = {16: K_S16, 32: K_S32, 128: K_S128}[n]
        return CF.v[:, o:o + 128]

    def sel_b(n):
        o = {16: KB_S16, 32: KB_S32, 128: KB_S128}[n]
        return CB.v[:, o:o + 128]

    dma_q = [0]

    def DMA(out, in_, reads, writes, sem):
        P.dma([("dma_start", (), dict(out=out, in_=in_))], reads, writes, sem)

    DMA(CF.v, I["cstf"][:, :], [uIN], [CF.u], "c_cf")
    DMA(CB.v, I["cstb"][:, :], [uIN], [CB.u], "c_cb")

    wst_ctr = [0]
    STG = list(WST)
    for i_ in range(4):
        STG.append(Buf(XIN[i_].v[:, 0:WCH], f"xstg{i_}a"))
        STG.append(Buf(XIN[i_].v[:, WCH:2 * WCH], f"xstg{i_}b"))
    CAST_ENG = ("pool", "dve", "act")

    def load_cast(dst_view, dst_u, src_ap, ncols, gcol=None):
        st = STG[wst_ctr[0] % len(STG)]
        eng = CAST_ENG[wst_ctr[0] % 3]
        dst_u = U("wchunk")
        wst_ctr[0] += 1
        DMA(st.v[:, 0:ncols], src_ap, [uIN], [st.u], st.name)
        if eng == "act":
            if gcol is None:
                P.op("act", lambda e: e.activation(out=dst_view, in_=st.v[:, 0:ncols], func=AF.Copy), [st.u], [dst_u])
            else:
                P.op("act", lambda e: e.activation(out=dst_view, in_=st.v[:, 0:ncols], func=AF.Copy, scale=gcol), [st.u, PCOL.u], [dst_u])
        elif gcol is None:
            P.op(eng, lambda e: e.tensor_copy(out=dst_view, in_=st.v[:, 0:ncols]), [st.u], [dst_u])
        else:
            P.op(eng, lambda e: e.tensor_scalar(out=dst_view, in0=st.v[:, 0:ncols], scalar1=gcol, scalar2=zcol, op0=ALU.mult, op1=ALU.add),
                 [st.u, PCOL.u, CF.u], [dst_u])

    class Seq:
        pass

    seqs = []
    sp = Seq()
    sp.kind = "p"
    sp.idx = 0
    sp.tiles = [(0, META)] + [(META + 256 * j, 256) for j in range(NPT)]
    sp.npast = 0
    sp.row0 = 0
    sp.kblocks = [(0, META)] + [(META + 128 * m, 128) for m in range(SEQ // 128)]
    seqs.append(sp)
    for s in range(NSEQ):
        q = Seq()
        q.kind = "s"
        q.idx = s
        q.tiles = [(0, DSEQ)]
        q.npast = PAST
        q.row0 = T + DSEQ * s
        q.kblocks = [(128 * m, 128) for m in range(NCB_S)] + [(PAST, DSEQ)]
        seqs.append(q)

    def xsrc(L, sq, t, nb):
        if L == 0:
            if sq.kind == "p":
                if t < META:
                    return I["meta"][t:t + nb, :], uIN
                return I["xp"][t - META:t - META + nb, :], uIN
            return I["xs"][DSEQ * sq.idx + t:DSEQ * sq.idx + t + nb, :], uIN
        r = sq.row0 + t
        return XB[r:r + nb, :], uXB[r // 128]

    try:
      for L in range(DEPTH):
          P.barrier()
          AR.off = mark
          WIN = AR.alloc("win", [8, NIN], BF16)
          WOUT = AR.alloc("wout", [8, D], BF16)
          NRING = 3
          KVRk = [AR.alloc(f"kvrk{i}", [6, 128], BF16) for i in range(NRING)]
          KVRv = [AR.alloc(f"kvrv{i}", [576], BF16) for i in range(NRING)]
          QT = AR.alloc("qT", [6, 256], BF16)
          KTC = AR.alloc("kTc", [6, 256], BF16)
          VCUR = AR.alloc("vcur", [2, 576], BF16)
          KVST = [AR.alloc(f"kvst{i}", [768], F32) for i in range(2)]
          LFST = [AR.alloc(f"lfst{i}", [6], F32) for i in range(2)]
          FD = AR.alloc("fd", [12], F32)
          E12 = AR.alloc("e12", [12], F32)
          L12 = [AR.alloc(f"l12{i}", [12], F32) for i in range(2)]
          LA_ = AR.alloc("la", [6], F32)
          NACS = AR.alloc("nacs", [6], F32)
          EA = AR.alloc("ea", [6], F32)
          DEC = AR.alloc("dec", [6], F32)
          NEGC = AR.alloc("negc", [NKB, 6], F32)
          STA = AR.alloc("sta", [4, 128], BF16)
          CLT = AR.alloc("clt", [6], F32)
          CLTA = AR.alloc("clta", [max(NCB_S, 1), 6], F32)
          ZS = AR.alloc("zs", [2, 384], F32)
          ZE = AR.alloc("ze", [384], F32)
          UT = AR.alloc("uT", [2, 15 + 256], F32)
          PS2 = AR.alloc("ps2", [15 + 256], F32)
          PS4 = AR.alloc("ps4", [15 + 256], F32)
          DIFF = AR.alloc("diff", [2, 256], BF16)
          XBC = AR.alloc("xbcT", [5, 3 + 256], F32)
          ACC = AR.alloc("acc", [256], F32)
          CE = AR.alloc("ce", [256], F32)
          XS = AR.alloc("xsT", [3, 256], F32)
          BT = AR.alloc("bT", [256], BF16)
          CT = AR.alloc("cT", [256], BF16)
          XX = AR.alloc("xx", [6, 128], F32)
          LT = AR.alloc("lt", [6, 128], F32)
          MT = AR.alloc("mt", [6, 128], BF16)
          XSTOK = AR.alloc("xstok", [384], F32)
          XD = AR.alloc("xd", [6, 64], BF16)
          BW = AR.alloc("bw", [6, 128], BF16)
          Y1 = AR.alloc("y1", [384], F32)
          Y2 = AR.alloc("y2", [384], F32)
          YN = AR.alloc("yn", [384], BF16)
          SST = AR.alloc("sst", [192], F32)
          SBF = AR.alloc("sbf", [192], BF16)
          PT = [AR.alloc(f"pt{i}", [256], BF16) for i in range(4)]
          REC = [AR.alloc(f"rec{i}", [256], F32) for i in range(2)]
          MIXT = AR.alloc("mixT", [8, 256], BF16)
          CKS = [AR.alloc(f"cks{i}", [384], F32) for i in range(2)]
          CVS = [AR.alloc(f"cvs{i}", [384], F32) for i in range(2)]
          KB16 = AR.alloc("kb16", [384], BF16)
          STT = AR.alloc("stt", [256], F32)
          SIN = AR.alloc("sin", [3, 128], F32)

          DMA(PBC.v, I["pbc"][L], [uIN], [PBC.u], "c_pbc")
          DMA(PCOL.v, I["pcol"][L], [uIN], [PCOL.u], "c_pcol")
          DMA(PWF.v, I["poolw"][L], [uIN], [PWF.u], "c_pwf")
          P.op("pool", lambda e: e.tensor_copy(out=PWB.v.rearrange("p a b -> p (a b)"), in_=PWF.v), [PWF.u], [PWB.u])
          P.op("act", lambda e: e.activation(out=ABC.v, in_=PBC.v[:, B_AL:B_AL + 6], func=AF.Exp), [PBC.u], [ABC.u])
          P.op("act", lambda e: e.activation(out=ABC.v, in_=ABC.v, func=AF.Copy, scale=-1.0), [ABC.u], [ABC.u])
          for kc in range(8):
              for c0 in range(0, NIN, WCH):
                  c1 = min(NIN, c0 + WCH)
                  load_cast(WIN.v[:, kc, c0:c1], WIN.u, I["w_in"][L, kc * 128:(kc + 1) * 128, c0:c1], c1 - c0,
                            PCOL.v[:, C_GM + kc:C_GM + kc + 1])
          for kc in range(8):
              for c0 in range(0, D, WCH):
                  load_cast(WOUT.v[:, kc, c0:c0 + WCH], WOUT.u, I["w_out"][L, kc * 128:(kc + 1) * 128, c0:c0 + WCH], WCH)
          P.barrier()
          for b_ in [QT, KTC] + KVRk:
              P.op("pool", lambda e, b_=b_: e.memset(b_.v, 0.0), [], [b_.u])
          for b_ in [KTC] + KVRk:
              P.op("pool", lambda e, b_=b_: e.memset(b_.v[64:65, :, :], 1.0), [], [b_.u])
          P.op("pool", lambda e: e.memset(VCUR.v, 1.0), [], [VCUR.u])
          for b_ in KVRv:
              P.op("pool", lambda e, b_=b_: e.memset(b_.v, 1.0), [], [b_.u])
          P.op("pool", lambda e: e.memset(STA.v, 0.0), [], [STA.u])
          P.op("pool", lambda e: e.memset(BW.v, 0.0), [], [BW.u])
          mark_stage('A-weights')

          xin_ctr = [0]
          kvst_ctr = [0]
          ring_ctr = [0]
          pt_ctr = [0]
          st_ctr = [0]
          fm_ctr = [0]
          xo_ctr = [0]
          cks_ctr = [0]

          def do_step1(sq, t0, n):
              blocks = [(o, min(128, n - o)) for o in range(0, n, 128)]
              xins = []
              for bi, (o, nb) in enumerate(blocks):
                  xin = XIN[xin_ctr[0] % 4]
                  xin_ctr[0] += 1
                  xins.append(xin)
                  src, su = xsrc(L, sq, t0 + o, nb)
                  DMA(xin.v[0:nb, :], src, [su], [xin.u], xin.name)
                  hb = HBF[bi % 2]
                  ssq = SM.v[0:nb, 0:1]
                  rs = SM.v[0:nb, 1:2]
                  P.op("act", lambda e, xin=xin, nb=nb, ssq=ssq: e.activation(out=JUNK.v[0:nb, :], in_=xin.v[0:nb, :], func=AF.Square, accum_out=ssq),
                       [xin.u], [JUNK.u, smu[0]])
                  P.op("act", lambda e, nb=nb, ssq=ssq, rs=rs: e.activation(out=rs, in_=ssq, func=AF.Ln, bias=CF.v[0:nb, K_EPS:K_EPS + 1], scale=1.0 / D), [smu[0], CF.u], [smu[0]])
                  P.op("act", lambda e, rs=rs: e.activation(out=rs, in_=rs, func=AF.Exp, scale=-0.5), [smu[0]], [smu[0]])
                  P.op("dve", lambda e, hb=hb, xin=xin, nb=nb, rs=rs: e.tensor_scalar(out=hb.v[0:nb, :], in0=xin.v[0:nb, :], scalar1=rs, scalar2=CF.v[0:nb, K_Z:K_Z + 1], op0=ALU.mult, op1=ALU.add),
                       [xin.u, smu[0], CF.u], [hb.u])

                  def tr(e, hb=hb, nb=nb):
                      r = None
                      for kc in range(8):
                          r = e.transpose(out=pbs[0][:, kc * 128:kc * 128 + nb], in_=hb.v[0:nb, kc * 128:(kc + 1) * 128], identity=ident_b[0:nb, 0:nb])
                      return r
                  P.op("pe", tr, [hb.u, CB.u], pu[0])
                  P.op("act", lambda e, o=o, nb=nb: e.activation(out=HT.v[:, :, o:o + nb], in_=pbs[0][:, :].rearrange("p (a b) -> p a b", a=8)[:, :, 0:nb], func=AF.Copy),
                       pu[0], [HT.u])

              return xins

          s1cache = {}

          for sq in seqs:
              isP = sq.kind == "p"
              nkb = len(sq.kblocks)
              if isP:
                  P.op("pool", lambda e: e.memset(UT.v[:, :, 0:15], 0.0), [], [UT.u])
                  P.op("pool", lambda e: e.memset(XBC.v[:, :, 0:3], 0.0), [], [XBC.u])
                  P.op("pool", lambda e: e.memset(SST.v, 0.0), [], [SST.u])
                  P.op("pool", lambda e: e.memset(SBF.v, 0.0), [], [SBF.u])
              else:
                  s = sq.idx
                  DMA(STT.v[0:15, 0:256], I["spool"][L, s], [uIN], [STT.u], "c_stt")
                  for pt in range(2):
                      P.op("pe", lambda e, pt=pt: e.transpose(out=pbs[7][:, 256 + 16 * pt:256 + 16 * pt + 15], in_=STT.v[0:15, 128 * pt:128 * pt + 128], identity=ident_f[0:15, 0:15]),
                           [STT.u, CF.u], psu(7, 256, 512))
                  P.op("act", lambda e: e.activation(out=UT.v[:, 0, 0:15], in_=pbs[7][:, 256:271], func=AF.Copy), psu(7, 256, 512), [UT.u])
                  P.op("act", lambda e: e.activation(out=UT.v[:, 1, 0:15], in_=pbs[7][:, 272:287], func=AF.Copy), psu(7, 256, 512), [UT.u])
                  DMA(T1.v[0:3, 0:640], I["sconv"][L, s], [uIN], [T1.u], "c_t1")
                  for ct in range(5):
                      P.op("pe", lambda e, ct=ct: e.transpose(out=pbs[7][:, 4 * ct:4 * ct + 3], in_=T1.v[0:3, 128 * ct:128 * ct + 128], identity=ident_f[0:3, 0:3]),
                           [T1.u, CF.u], psu(7, 0, 256))
                  for ct in range(5):
                      P.op("act", lambda e, ct=ct: e.activation(out=XBC.v[:, ct, 0:3], in_=pbs[7][:, 4 * ct:4 * ct + 3], func=AF.Copy), psu(7, 0, 256), [XBC.u])
                  for j in range(3):
                      DMA(SIN.v[0:64, j, 0:64], I["sssd"][L, s, j], [uIN], [SIN.u], f"c_sin{j}a")
                      DMA(SIN.v[0:64, j, 64:128], I["sssd"][L, s, j + 3], [uIN], [SIN.u], f"c_sin{j}b")
                  for j in range(3):
                      P.op("pe", lambda e, j=j: e.transpose(out=pbs[6][:, 64 * j:64 * j + 64], in_=SIN.v[0:64, j, :], identity=ident_f[0:64, 0:64]),
                           [SIN.u, CF.u], psu(6, 0, 256))
                  P.op("act", lambda e: e.activation(out=SST.v, in_=pbs[6][:, 0:192], func=AF.Copy), psu(6, 0, 256), [SST.u])
                  P.op("dve", lambda e: e.tensor_copy(out=SBF.v, in_=SST.v), [SST.u], [SBF.u])

              if not isP:
                  DMA(CLTA.v, I["cl"][L, s].rearrange("(m p) h -> p m h", p=128), [uIN], [CLTA.u], "c_clta")
                  P.op("dve", lambda e: e.tensor_single_scalar(out=CLTA.v, in_=CLTA.v, scalar=-1.0, op=ALU.mult), [CLTA.u], [CLTA.u])

                  def cs_mm(e):
                      r = None
                      for m in range(NCB_S):
                          r = e.matmul(pbs[6][:, 6 * m:6 * m + 6], lhsT=tri_f, rhs=CLTA.v[:, m, :], start=True, stop=(m == 0))
                          for j in range(m):
                              r = e.matmul(pbs[6][:, 6 * m:6 * m + 6], lhsT=ones_f, rhs=CLTA.v[:, j, :], start=False, stop=(j == m - 1))
                      return r
                  P.op("pe", cs_mm, [CLTA.u, CF.u], pu[6])
                  P.op("act", lambda e: e.activation(out=NEGC.v[:, 0:NCB_S, :], in_=pbs[6][:, 0:6 * NCB_S].rearrange("p (m h) -> p m h", h=6), func=AF.Copy), pu[6], [NEGC.u])

              for ti, (t0, n) in enumerate(sq.tiles):
                  blocks = [(o, min(128, n - o)) for o in range(0, n, 128)]
                  kb_first = (sq.npast // 128) + (0 if (not isP) else (0 if ti == 0 else 1 + 2 * (ti - 1)))
                  if isP:
                      kbs = [0] if ti == 0 else [1 + 2 * (ti - 1), 2 + 2 * (ti - 1)]
                  else:
                      kbs = [NCB_S]
                  key_ = (sq.kind, sq.idx, ti)
                  xins = s1cache.pop(key_) if key_ in s1cache else do_step1(sq, t0, n)

                  mark_stage('step1')
                  l12s = []
                  def do_tm(bi, o, nb):
                      kb = kbs[bi]

                      def tm(e, o=o, nb=nb):
                          r = None
                          for (c0, c1, bank, pc) in ((0, 384, 3, 0), (384, 768, 4, 0), (768, 780, 5, 0), (780, 1164, 5, 128)):
                              for kc in range(8):
                                  r = e.matmul(pbs[bank][0:nb, pc:pc + c1 - c0], lhsT=HT.v[:, kc, o:o + nb], rhs=WIN.v[:, kc, TM0 + c0:TM0 + c1], start=(kc == 0), stop=(kc == 7))
                          return r
                      P.op("pe", tm, [HT.u, WIN.u], pu[3] + pu[4] + pu[5])
                      kv = KVST[kvst_ctr[0] % 2]
                      lf = LFST[kvst_ctr[0] % 2]
                      l12 = L12[kvst_ctr[0] % 2]
                      kvst_ctr[0] += 1
                      l12s.append(l12)
                      P.op("act", lambda e, kv=kv, nb=nb: e.activation(out=kv.v[0:nb, 0:384], in_=pbs[3][0:nb, 0:384], func=AF.Copy), pu[3], [kv.u])
                      P.op("dve", lambda e, kv=kv, nb=nb: e.tensor_copy(out=kv.v[0:nb, 384:768], in_=pbs[4][0:nb, 0:384]), pu[4], [kv.u])
                      vsrc = kv.v[0:nb, 384:768].rearrange("p (j t d) -> p j t d", j=3, t=2)
                      vdst = VCUR.v[0:nb, bi, :].rearrange("p (j x) -> p j x", j=3)
                      P.op("pool", lambda e, vsrc=vsrc, vdst=vdst: e.tensor_copy(out=vdst[:, :, 0:64], in_=vsrc[:, :, 0, :]), [kv.u], [VCUR.u])
                      P.op("pool", lambda e, vsrc=vsrc, vdst=vdst: e.tensor_copy(out=vdst[:, :, 128:192], in_=vsrc[:, :, 1, :]), [kv.u], [VCUR.u])
                      if isP:
                          ko, vo, lo = O["k_p"][L, t0 + o:t0 + o + nb, :], O["v_p"][L, t0 + o:t0 + o + nb, :], O["l_p"][L, t0 + o:t0 + o + nb, :]
                      else:
                          ko, vo, lo = O["k_s"][L, sq.idx, o:o + nb, :], O["v_s"][L, sq.idx, o:o + nb, :], O["l_s"][L, sq.idx, o:o + nb, :]
                      DMA(ko, kv.v[0:nb, 0:384], [kv.u], [uOUT], kv.name + "_k")
                      DMA(vo, kv.v[0:nb, 384:768], [kv.u], [uOUT], kv.name + "_v")
                      P.op("dve", lambda e, nb=nb: e.tensor_tensor(out=FD.v[0:nb, :], in0=pbs[5][0:nb, 0:12], in1=PBC.v[0:nb, B_FB:B_FB + 12], op=ALU.add), psu(5, 0, 12) + [PBC.u], [FD.u])
                      P.op("act", lambda e, nb=nb: e.activation(out=E12.v[0:nb, 0:6], in_=FD.v[0:nb, 0:6], func=AF.Exp, scale=-1.0), [FD.u], [E12.u])
                      P.op("act", lambda e, nb=nb: e.activation(out=E12.v[0:nb, 6:12], in_=FD.v[0:nb, 6:12], func=AF.Exp), [FD.u], [E12.u])
                      P.op("act", lambda e, nb=nb, l12=l12: e.activation(out=l12.v[0:nb, :], in_=E12.v[0:nb, :], func=AF.Ln, bias=1.0), [E12.u], [l12.u])
                      P.op("dve", lambda e, nb=nb, l12=l12, lf=lf: e.tensor_single_scalar(out=lf.v[0:nb, :], in_=l12.v[0:nb, 0:6], scalar=-1.0, op=ALU.mult), [l12.u], [lf.u])
                      DMA(lo, lf.v[0:nb, :], [lf.u], [uOUT], lf.name)
                      P.op("act", lambda e, nb=nb: e.activation(out=ZE.v[0:nb, :], in_=pbs[5][0:nb, 128:512], func=AF.Exp, scale=-1.0), pu[5], [ZE.u])
                      P.op("act", lambda e, nb=nb: e.activation(out=ZE.v[0:nb, :], in_=ZE.v[0:nb, :], func=AF.Ln, bias=1.0), [ZE.u], [ZE.u])
                      P.op("act", lambda e, nb=nb: e.activation(out=ZE.v[0:nb, :], in_=ZE.v[0:nb, :], func=AF.Exp, scale=-1.0), [ZE.u], [ZE.u])
                      P.op("dve", lambda e, nb=nb, bi=bi: e.tensor_tensor(out=ZS.v[0:nb, bi, :], in0=pbs[5][0:nb, 128:512], in1=ZE.v[0:nb, :], op=ALU.mult), pu[5] + [ZE.u], [ZS.u])

                      if isP:
                          prev = None if kb == 0 else (kb - 1, sq.kblocks[kb - 1][1])
                      else:
                          prev = (kb - 1, 128)

                      def cmm(e, nb=nb, l12=l12, prev=prev):
                          r = e.matmul(pbs[6][0:nb, 384:390], lhsT=tri_f[0:nb, 0:nb], rhs=l12.v[0:nb, 0:6], start=True, stop=(prev is None))
                          if prev is not None:
                              pk, pn = prev
                              r = e.matmul(pbs[6][0:nb, 384:390], lhsT=sel_f(pn)[0:pn, 0:nb], rhs=NEGC.v[0:pn, pk, :], start=False, stop=True)
                          return r
                      P.op("pe", cmm, [l12.u, CF.u, NEGC.u], psu(6, 256, 512))
                      P.op("act", lambda e, nb=nb, kb=kb: e.activation(out=NEGC.v[0:nb, kb, :], in_=pbs[6][0:nb, 384:390], func=AF.Copy), psu(6, 256, 512), [NEGC.u])
                      P.op("dve", lambda e, nb=nb, l12=l12: e.tensor_single_scalar(out=STA.v[0:nb, 0, 0:128:32], in_=l12.v[0:nb, 0:4], scalar=-1.0, op=ALU.mult), [l12.u], [STA.u])
                      P.op("dve", lambda e, nb=nb, l12=l12: e.tensor_single_scalar(out=STA.v[0:nb, 1, 0:64:32], in_=l12.v[0:nb, 4:6], scalar=-1.0, op=ALU.mult), [l12.u], [STA.u])
                      if prev is not None:
                          pk, pn = prev
                          P.op("dve", lambda e, pk=pk, pn=pn: e.tensor_single_scalar(out=STA.v[0:pn, 2, 0:128:32], in_=NEGC.v[0:pn, pk, 0:4], scalar=-1.0, op=ALU.mult), [NEGC.u], [STA.u])
                          P.op("dve", lambda e, pk=pk, pn=pn: e.tensor_single_scalar(out=STA.v[0:pn, 3, 0:64:32], in_=NEGC.v[0:pn, pk, 4:6], scalar=-1.0, op=ALU.mult), [NEGC.u], [STA.u])

                      def smm(e, nb=nb, prev=prev):
                          r = None
                          for a in range(2):
                              r = e.matmul(pbs[6][:, 128 * a:128 * a + nb], lhsT=STA.v[0:nb, a, :], rhs=tri_b[0:nb, 0:nb], start=True, stop=(prev is None))
                              if prev is not None:
                                  pn = prev[1]
                                  r = e.matmul(pbs[6][:, 128 * a:128 * a + nb], lhsT=STA.v[0:pn, 2 + a, :], rhs=sel_b(pn)[0:pn, 0:nb], start=False, stop=True)
                          return r
                      P.op("pe", smm, [STA.u, CB.u], psu(6, 0, 256))
                      for h in range(6):
                          a, i = (0, h) if h < 4 else (1, h - 4)
                          P.op("act", lambda e, h=h, a=a, i=i, o=o, nb=nb: e.activation(out=QT.v[64:65, h, o:o + nb], in_=pbs[6][32 * i:32 * i + 1, 128 * a:128 * a + nb], func=AF.Copy),
                               psu(6, 0, 256), [QT.u])


                  mark_stage('step2a+fox')
                  def fm(c0, ncols, evac):
                      slot = fm_ctr[0] % 3
                      fm_ctr[0] += 1
                      bank, cc = (1, 2, 7)[slot], 0
                      us = [pu[bank][0]]

                      def mm(e):
                          r = None
                          for kc in range(8):
                              r = e.matmul(pbs[bank][0:ncols, cc:cc + n], lhsT=WIN.v[:, kc, c0:c0 + ncols], rhs=HT.v[:, kc, 0:n], start=(kc == 0), stop=(kc == 7))
                          return r
                      P.op("pe", mm, [WIN.u, HT.u], us)
                      evac(pbs[bank][:, cc:cc + n], us)


                  do_tm(0, blocks[0][0], blocks[0][1])
                  for j in range(3):
                      def evq(ps, us, j=j):
                          P.op("act", lambda e: e.activation(out=QT.v[0:64, 2 * j, 0:n], in_=ps[0:64, :], func=AF.Copy, scale=0.125), us, [QT.u])
                          P.op("act", lambda e: e.activation(out=QT.v[0:64, 2 * j + 1, 0:n], in_=ps[64:128, :], func=AF.Copy, scale=0.125), us, [QT.u])
                      fm(OQ + 128 * j, 128, evq)
                  for j in range(3):
                      def evk(ps, us, j=j):
                          P.op("act", lambda e: e.activation(out=KTC.v[0:64, 2 * j, 0:n], in_=ps[0:64, :], func=AF.Copy), us, [KTC.u])
                          P.op("dve", lambda e: e.tensor_copy(out=KTC.v[0:64, 2 * j + 1, 0:n], in_=ps[64:128, :]), us, [KTC.u])
                      fm(OKK + 128 * j, 128, evk)
                  if len(blocks) > 1:
                      do_tm(1, blocks[1][0], blocks[1][1])
                  for pt in range(2):
                      def evu(ps, us, pt=pt):
                          P.op("dve", lambda e: e.tensor_copy(out=UT.v[:, pt, 15:15 + n], in_=ps), us, [UT.u])
                      fm(OU + 128 * pt, 128, evu)
                  for ct in range(5):
                      def evx(ps, us, ct=ct):
                          P.op("act", lambda e: e.activation(out=XBC.v[:, ct, 3:3 + n], in_=ps, func=AF.Copy), us, [XBC.u])
                      fm(OX + 128 * ct, 128, evx)

                  mark_stage('step2b')
                  if isP and ti < len(sq.tiles) - 1:
                      for bi, (o, nb) in enumerate(blocks):
                          kb = kbs[bi]
                          DMA(KSC[kb].rearrange("p (h k) -> p h k", h=6)[:, :, 0:nb], KTC.v[:, :, o:o + nb], [KTC.u], [uKSC[kb]], f"c_ktc{bi}")
                          DMA(VSC[kb][0:nb, :], VCUR.v[0:nb, bi, :], [VCUR.u], [uKSC[kb]], f"c_vcur{bi}")

                  def gen_mix():
                      ln_ = 15 + n
                      for pt in range(2):
                          E_ = UT.v[:, pt, :]
                          P.op("dve", lambda e, E_=E_: e.tensor_tensor(out=PS2.v[:, 1:ln_], in0=E_[:, 1:ln_], in1=E_[:, 0:ln_ - 1], op=ALU.add), [UT.u], [PS2.u])
                          P.op("dve", lambda e: e.tensor_tensor(out=PS4.v[:, 3:ln_], in0=PS2.v[:, 3:ln_], in1=PS2.v[:, 1:ln_ - 2], op=ALU.add), [PS2.u], [PS4.u])
                          if pt == 0:
                              lo, hi = PS2, PS4
                          else:
                              P.op("dve", lambda e: e.tensor_tensor(out=PS2.v[:, 7:ln_], in0=PS4.v[:, 7:ln_], in1=PS4.v[:, 3:ln_ - 4], op=ALU.add), [PS4.u], [PS2.u])
                              P.op("dve", lambda e: e.tensor_tensor(out=PS4.v[:, 15:ln_], in0=PS2.v[:, 15:ln_], in1=PS2.v[:, 7:ln_ - 8], op=ALU.add), [PS2.u], [PS4.u])
                              lo, hi = PS2, PS4
                          for (r0, sb) in ((0, lo), (64, hi)):
                              if isP and ti == 0:
                                  P.op("dve", lambda e, r0=r0, sb=sb, pt=pt: e.tensor_tensor(out=ACC.v[r0:r0 + 64, 0:n], in0=sb.v[r0:r0 + 64, 15:15 + n], in1=CF.v[r0:r0 + 64, K_RC + 16 * pt:K_RC + 16 * pt + n], op=ALU.mult),
                                       [sb.u, CF.u], [ACC.u])
                                  P.op("dve", lambda e, r0=r0, pt=pt, E_=E_: e.tensor_tensor(out=DIFF.v[r0:r0 + 64, pt, 0:n], in0=ACC.v[r0:r0 + 64, 0:n], in1=E_[r0:r0 + 64, 15:15 + n], op=ALU.subtract),
                                       [ACC.u, UT.u], [DIFF.u])
                              else:
                                  P.op("dve", lambda e, r0=r0, sb=sb, pt=pt, E_=E_: e.scalar_tensor_tensor(out=DIFF.v[r0:r0 + 64, pt, 0:n], in0=sb.v[r0:r0 + 64, 15:15 + n],
                                                                                                   scalar=CF.v[r0:r0 + 64, K_RC + 16 * pt + 15:K_RC + 16 * pt + 16], in1=E_[r0:r0 + 64, 15:15 + n], op0=ALU.mult, op1=ALU.subtract),
                                       [sb.u, CF.u, UT.u], [DIFF.u])
                          P.op("pe", lambda e, pt=pt: e.matmul(pbs[7][:, 0:n], lhsT=PWB.v[:, pt, :], rhs=DIFF.v[:, pt, 0:n], start=True, stop=True), [PWB.u, DIFF.u], psu(7, 0, 256))
                          yield
                          P.op("act", lambda e, pt=pt: e.activation(out=MIXT.v[:, 3 + pt, 0:n], in_=pbs[7][:, 0:n], func=AF.Copy, scale=PCOL.v[:, C_PS + pt:C_PS + pt + 1]), psu(7, 0, 256) + [PCOL.u], [MIXT.u])
                      P.op("pool", lambda e: e.tensor_copy(out=UT.v[:, :, 0:15], in_=UT.v[:, :, n:n + 15]), [UT.u], [UT.u])

                      for ct in range(5):
                          cw = PCOL.v[:, C_CW + 4 * ct:C_CW + 4 * ct + 4]
                          cb_ = PCOL.v[:, C_CB + ct:C_CB + ct + 1]
                          P.op("pool", lambda e, ct=ct, cw=cw, cb_=cb_: e.tensor_scalar(out=ACC.v[:, 0:n], in0=XBC.v[:, ct, 0:n], scalar1=cw[:, 0:1], scalar2=cb_, op0=ALU.mult, op1=ALU.add),
                               [XBC.u, PCOL.u], [ACC.u])
                          for j in range(1, 4):
                              P.op("dve", lambda e, ct=ct, cw=cw, j=j: e.scalar_tensor_tensor(out=ACC.v[:, 0:n], in0=XBC.v[:, ct, j:j + n], scalar=cw[:, j:j + 1], in1=ACC.v[:, 0:n], op0=ALU.mult, op1=ALU.add),
                                   [XBC.u, PCOL.u], [ACC.u])
                          P.op("act", lambda e: e.activation(out=CE.v[:, 0:n], in_=ACC.v[:, 0:n], func=AF.Exp, scale=-1.0), [ACC.u], [CE.u])
                          P.op("act", lambda e: e.activation(out=CE.v[:, 0:n], in_=CE.v[:, 0:n], func=AF.Ln, bias=1.0), [CE.u], [CE.u])
                          P.op("act", lambda e: e.activation(out=CE.v[:, 0:n], in_=CE.v[:, 0:n], func=AF.Exp, scale=-1.0), [CE.u], [CE.u])
                          if ct < 3:
                              dst, du = XS.v[:, ct, 0:n], XS.u
                          elif ct == 3:
                              dst, du = BT.v[:, 0:n], BT.u
                          else:
                              dst, du = CT.v[:, 0:n], CT.u
                          P.op("dve", lambda e, dst=dst: e.tensor_tensor(out=dst, in0=ACC.v[:, 0:n], in1=CE.v[:, 0:n], op=ALU.mult), [ACC.u, CE.u], [du])
                          yield
                      P.op("pool", lambda e: e.tensor_copy(out=XBC.v[:, :, 0:3], in_=XBC.v[:, :, n:n + 3]), [XBC.u], [XBC.u])

                      for bi, (o, nb) in enumerate(blocks):
                          l12 = l12s[bi]
                          P.op("dve", lambda e, nb=nb, l12=l12: e.tensor_tensor(out=LA_.v[0:nb, :], in0=l12.v[0:nb, 6:12], in1=ABC.v[0:nb, :], op=ALU.mult), [l12.u, ABC.u], [LA_.u])
                          P.op("pe", lambda e, nb=nb: e.matmul(pbs[6][0:nb, 400:406], lhsT=tri_f[0:nb, 0:nb], rhs=LA_.v[0:nb, :], start=True, stop=True), [LA_.u, CF.u], psu(6, 256, 512))
                          P.op("act", lambda e, nb=nb: e.activation(out=NACS.v[0:nb, :], in_=pbs[6][0:nb, 400:406], func=AF.Copy, scale=-1.0), psu(6, 256, 512), [NACS.u])
                          P.op("act", lambda e, nb=nb: e.activation(out=EA.v[0:nb, :], in_=pbs[6][0:nb, 400:406], func=AF.Exp), psu(6, 256, 512), [EA.u])
                          P.op("dve", lambda e, nb=nb: e.tensor_tensor(out=XX.v[0:nb, :, 0:nb], in0=tri_f[0:nb, 0:nb].unsqueeze(1).to_broadcast([nb, 6, nb]),
                                                                      in1=LA_.v[0:nb, :].unsqueeze(2).to_broadcast([nb, 6, nb]), op=ALU.mult), [LA_.u, CF.u], [XX.u])
                          abank = lambda h: ((5, 7)[h // 4], 128 * (h % 4))

                          def amm(e, nb=nb):
                              r = None
                              for h in range(6):
                                  b_, c_ = abank(h)
                                  e.matmul(pbs[b_][:, c_:c_ + nb], lhsT=ones_f[0:nb, :], rhs=XX.v[0:nb, h, 0:nb], start=True, stop=False)
                                  r = e.matmul(pbs[b_][:, c_:c_ + nb], lhsT=ident_f[0:nb, :], rhs=mask_f[0:nb, 0:nb], start=False, stop=True)
                              return r
                          P.op("pe", amm, [XX.u, CF.u], [pu[5][0], pu[7][0]])
                          yield
                          for h in range(6):
                              b_, c_ = abank(h)
                              P.op("act", lambda e, h=h, b_=b_, c_=c_, nb=nb: e.activation(out=LT.v[0:nb, h, 0:nb], in_=pbs[b_][0:nb, c_:c_ + nb], func=AF.Exp, bias=NACS.v[0:nb, h:h + 1]),
                                   [pu[b_][0], NACS.u], [LT.u])

                          P.op("act", lambda e, nb=nb: e.activation(out=DEC.v[:, 0:4], in_=pbs[5][:, nb - 1:512:128], func=AF.Exp), [pu[5][0]], [DEC.u])
                          P.op("act", lambda e, nb=nb: e.activation(out=DEC.v[:, 4:6], in_=pbs[7][:, nb - 1:256:128], func=AF.Exp), [pu[7][0]], [DEC.u])

                          def gmm(e, o=o, nb=nb):
                              r = None
                              for g in range(2):
                                  r = e.matmul(pbs[(5, 7)[g]][0:nb, 0:nb], lhsT=BT.v[64 * g:64 * g + 64, o:o + nb], rhs=CT.v[64 * g:64 * g + 64, o:o + nb], start=True, stop=True)
                              return r
                          P.op("pe", gmm, [BT.u, CT.u], [pu[5][0], pu[7][0]])
                          yield
                          for h in range(6):
                              g = h // 3
                              P.op("dve", lambda e, h=h, g=g, nb=nb: e.tensor_tensor(out=MT.v[0:nb, h, 0:nb], in0=pbs[(5, 7)[g]][0:nb, 0:nb], in1=LT.v[0:nb, h, 0:nb], op=ALU.mult),
                                   [pu[(5, 7)[g]][0], LT.u], [MT.u])

                          def xtr(e, o=o, nb=nb):
                              r = None
                              for j in range(3):
                                  r = e.transpose(out=pbs[7][0:nb, 128 * j:128 * j + 128], in_=XS.v[:, j, o:o + nb], identity=ident_f)
                              return r
                          P.op("pe", xtr, [XS.u, CF.u], pu[7])
                          yield
                          P.op("pe", lambda e, o=o, nb=nb: e.transpose(out=pbs[0][0:nb, 0:128], in_=BT.v[:, o:o + nb], identity=ident_b), [BT.u, CB.u], pu[0])
                          P.op("act", lambda e, nb=nb: e.activation(out=XSTOK.v[0:nb, :], in_=pbs[7][0:nb, 0:384], func=AF.Copy), pu[7], [XSTOK.u])
                          P.op("dve", lambda e, nb=nb, l12=l12: e.tensor_tensor(out=XD.v[0:nb, :, :], in0=XSTOK.v[0:nb, :].rearrange("p (h d) -> p h d", h=6),
                                                                               in1=l12.v[0:nb, 6:12].unsqueeze(2).to_broadcast([nb, 6, 64]), op=ALU.mult), [XSTOK.u, l12.u], [XD.u])
                          for h in range(6):
                              g = h // 3
                              P.op("act", lambda e, h=h, g=g, nb=nb: e.activation(out=BW.v[0:nb, h, 64 * g:64 * g + 64], in_=pbs[0][0:nb, 64 * g:64 * g + 64], func=AF.Copy, scale=LT.v[0:nb, h, nb - 1:nb]),
                                   pu[0] + [LT.u], [BW.u])

                          def ymm(e, o=o, nb=nb):
                              r = None
                              for h in range(6):
                                  r = e.matmul(pbs[6][0:nb, 64 * h:64 * h + 64], lhsT=MT.v[0:nb, h, 0:nb], rhs=XD.v[0:nb, h, :], start=True, stop=True)
                              for g in range(2):
                                  r = e.matmul(pbs[(5, 7)[g]][0:nb, 0:192], lhsT=CT.v[64 * g:64 * g + 64, o:o + nb], rhs=SBF.v[64 * g:64 * g + 64, :], start=True, stop=True)
                              for h in range(6):
                                  r = e.matmul(pb0f[:, 64 * h:64 * h + 64], lhsT=BW.v[0:nb, h, :], rhs=XD.v[0:nb, h, :], start=True, stop=True)
                              return r
                          P.op("pe", ymm, [MT.u, XD.u, CT.u, SBF.u, BW.u], [pu[6][0], pu[5][0], pu[7][0], pu[0][0]])
                          yield
                          for h in range(6):
                              g, j = h // 3, h % 3
                              P.op("dve", lambda e, h=h, g=g, j=j: e.scalar_tensor_tensor(out=SST.v[64 * g:64 * g + 64, 64 * j:64 * j + 64], in0=SST.v[64 * g:64 * g + 64, 64 * j:64 * j + 64],
                                                                                       scalar=DEC.v[64 * g:64 * g + 64, h:h + 1], in1=pb0f[64 * g:64 * g + 64, 64 * h:64 * h + 64], op0=ALU.mult, op1=ALU.add),
                                   [SST.u, DEC.u] + pu[0], [SST.u])
                          P.op("dve", lambda e: e.tensor_copy(out=SBF.v, in_=SST.v), [SST.u], [SBF.u])
                          yield
                          for g in range(2):
                              P.op("dve", lambda e, nb=nb, g=g: e.tensor_tensor(out=Y1.v[0:nb, 192 * g:192 * g + 192].rearrange("p (h d) -> p h d", h=3), in0=pbs[(5, 7)[g]][0:nb, 0:192].rearrange("p (h d) -> p h d", h=3),
                                                                               in1=EA.v[0:nb, 3 * g:3 * g + 3].unsqueeze(2).to_broadcast([nb, 3, 64]), op=ALU.mult), [pu[(5, 7)[g]][0], EA.u], [Y1.u])
                          P.op("dve", lambda e, nb=nb: e.tensor_tensor(out=Y1.v[0:nb, :], in0=pbs[6][0:nb, 0:384], in1=Y1.v[0:nb, :], op=ALU.add), pu[6] + [Y1.u], [Y1.u])
                          P.op("pool", lambda e, nb=nb: e.tensor_tensor(out=Y2.v[0:nb, :].rearrange("p (h d) -> p h d", h=6), in0=XSTOK.v[0:nb, :].rearrange("p (h d) -> p h d", h=6),
                                                                       in1=PBC.v[0:nb, B_DS:B_DS + 6].unsqueeze(2).to_broadcast([nb, 6, 64]), op=ALU.mult), [XSTOK.u, PBC.u], [Y2.u])
                          P.op("dve", lambda e, nb=nb: e.tensor_tensor(out=Y1.v[0:nb, :], in0=Y1.v[0:nb, :], in1=Y2.v[0:nb, :], op=ALU.add), [Y1.u, Y2.u], [Y1.u])
                          P.op("dve", lambda e, nb=nb, bi=bi: e.tensor_tensor(out=Y1.v[0:nb, :], in0=Y1.v[0:nb, :], in1=ZS.v[0:nb, bi, :], op=ALU.mult), [Y1.u, ZS.u], [Y1.u])
                          ssq = SM.v[0:nb, 2:3]
                          rs = SM.v[0:nb, 3:4]
                          P.op("act", lambda e, nb=nb, ssq=ssq: e.activation(out=Y2.v[0:nb, :], in_=Y1.v[0:nb, :], func=AF.Square, accum_out=ssq), [Y1.u], [Y2.u, smu[1]])
                          P.op("act", lambda e, nb=nb, ssq=ssq, rs=rs: e.activation(out=rs, in_=ssq, func=AF.Ln, bias=CF.v[0:nb, K_EPS:K_EPS + 1], scale=1.0 / 384), [smu[1], CF.u], [smu[1]])
                          P.op("act", lambda e, rs=rs: e.activation(out=rs, in_=rs, func=AF.Exp, scale=-0.5), [smu[1]], [smu[1]])
                          P.op("dve", lambda e, nb=nb, rs=rs: e.scalar_tensor_tensor(out=YN.v[0:nb, :], in0=Y1.v[0:nb, :], scalar=rs, in1=PBC.v[0:nb, B_SN:B_SN + 384], op0=ALU.mult, op1=ALU.mult),
                               [Y1.u, smu[1], PBC.u], [YN.u])
                          yield

                          def ytr(e, nb=nb):
                              r = None
                              for j in range(3):
                                  r = e.transpose(out=pbs[0][:, 512 + 128 * j:512 + 128 * j + nb], in_=YN.v[0:nb, 128 * j:128 * j + 128], identity=ident_b[0:nb, 0:nb])
                              return r
                          P.op("pe", ytr, [YN.u, CB.u], pu[0])
                          yield
                          P.op("act", lambda e, o=o, nb=nb: e.activation(out=MIXT.v[:, 5:8, o:o + nb], in_=pbs[0][:, 512:896].rearrange("p (a b) -> p a b", a=3)[:, :, 0:nb], func=AF.Copy), pu[0], [MIXT.u])


                  def gen_att():
                      q0 = sq.npast + t0
                      vis = []
                      for kb, (ks, kn) in enumerate(sq.kblocks):
                          if ks > q0 + n - 1:
                              break
                          i0 = max(0, ks - q0)
                          diag = ks + kn - 1 > q0
                          vis.append((kb, ks, kn, i0, diag))
                      otu = lambda h: (5 + h // 2, 256 * (h % 2))
                      pend = []
                      SKEW = 3
                      for hp in range(1):
                          heads = list(range(6))
                          for vi, (kb, ks, kn, i0, diag) in enumerate(vis):
                              nq = n - i0
                              cur = ks >= q0
                              if cur:
                                  bi = (ks - q0) // 128
                                  kt_ap = KTC.v[:, :, ks - q0:ks - q0 + kn]
                                  v_ap = VCUR.v[0:kn, bi, :]
                                  kvu = [KTC.u, VCUR.u]
                              else:
                                  r_ = ring_ctr[0] % NRING
                                  ring_ctr[0] += 1
                                  rk, rv = KVRk[r_], KVRv[r_]
                                  kt_ap = rk.v[:, :, 0:kn]
                                  v_ap = rv.v[0:kn, :]
                                  kvu = [rk.u, rv.u]
                                  if isP:
                                      DMA(rk.v[:, :, 0:kn], KSC[kb].rearrange("p (h k) -> p h k", h=6)[:, :, 0:kn], [uKSC[kb]], [rk.u], rk.name)
                                      DMA(rv.v[0:kn, :], VSC[kb][0:kn, :], [uKSC[kb]], [rv.u], rv.name)
                                  else:
                                      ck_, cv_ = CKS[cks_ctr[0] % 2], CVS[cks_ctr[0] % 2]
                                      cks_ctr[0] += 1
                                      DMA(ck_.v, I["ck"][L, sq.idx, ks:ks + 128, :], [uIN], [ck_.u], ck_.name)
                                      DMA(cv_.v, I["cv"][L, sq.idx, ks:ks + 128, :], [uIN], [cv_.u], cv_.name)
                                      P.op("dve", lambda e, ck_=ck_: e.tensor_copy(out=KB16.v, in_=ck_.v), [ck_.u], [KB16.u])

                                      def ktr(e):
                                          r = None
                                          for j in range(3):
                                              r = e.transpose(out=pbs[0][:, 128 * j:128 * j + 128], in_=KB16.v[:, 128 * j:128 * j + 128], identity=ident_b)
                                          return r
                                      P.op("pe", ktr, [KB16.u, CB.u], pu[0])
                                      for h in range(6):
                                          P.op("act", lambda e, h=h, rk=rk: e.activation(out=rk.v[0:64, h, :], in_=pbs[0][64 * (h % 2):64 * (h % 2) + 64, 128 * (h // 2):128 * (h // 2) + 128], func=AF.Copy),
                                               pu[0], [rk.u])
                                      vsrc = cv_.v.rearrange("p (j t d) -> p j t d", j=3, t=2)
                                      vdst = rv.v.rearrange("p (j x) -> p j x", j=3)
                                      P.op("dve", lambda e, vsrc=vsrc, vdst=vdst: e.tensor_copy(out=vdst[:, :, 0:64], in_=vsrc[:, :, 0, :]), [cv_.u], [rv.u])
                                      P.op("pool", lambda e, vsrc=vsrc, vdst=vdst: e.tensor_copy(out=vdst[:, :, 128:192], in_=vsrc[:, :, 1, :]), [cv_.u], [rv.u])
                              if vi == 0:
                                  for h in heads:
                                      ob, oc = otu(h)
                                      P.op("dve", lambda e, ob=ob, oc=oc: e.memset(pbs[ob][:, oc:oc + n], 0.0), [], [pu[ob][0]])
                              for h in heads:
                                  sl = st_ctr[0] % 4
                                  st_ctr[0] += 1
                                  sb_, sc_ = 1 + sl, 0
                                  sus = [pu[sb_][0]]
                                  mw = min(kn, nq)

                                  def smm2(e, h=h, kt_ap=kt_ap, sb_=sb_, sc_=sc_, kn=kn, nq=nq, i0=i0, diag=diag, mw=mw):
                                      r = e.matmul(pbs[sb_][0:kn, sc_:sc_ + nq], lhsT=kt_ap[:, h, :], rhs=QT.v[:, h, i0:n], start=True, stop=not diag)
                                      if diag:
                                          r = e.matmul(pbs[sb_][0:kn, sc_:sc_ + mw], lhsT=ident_b[0:kn, 0:kn], rhs=mask_b[0:kn, 0:mw], start=False, stop=True)
                                      return r
                                  P.op("pe", smm2, kvu + [QT.u, CB.u], sus)
                                  pt_ = PT[pt_ctr[0] % 4]
                                  pt_ctr[0] += 1
                                  P.op("act", lambda e, h=h, pt_=pt_, sb_=sb_, sc_=sc_, kn=kn, nq=nq, kb=kb: e.activation(out=pt_.v[0:kn, 0:nq], in_=pbs[sb_][0:kn, sc_:sc_ + nq], func=AF.Exp, bias=NEGC.v[0:kn, kb, h:h + 1]),
                                       sus + [NEGC.u], [pt_.u])
                                  ob, oc = otu(h)
                                  vc0 = 192 * (h // 2) + 64 * (h % 2)
                                  pend.append((lambda e, pt_=pt_, ob=ob, oc=oc, v_ap=v_ap, vc0=vc0, kn=kn, nq=nq, i0=i0: e.matmul(pbs[ob][:, oc + i0:oc + n], lhsT=v_ap[:, vc0:vc0 + 128], rhs=pt_.v[0:kn, 0:nq],
                                                                                                                      start=False, stop=False, skip_group_check=True),
                                               kvu + [pt_.u], [pu[ob][0]]))
                                  if len(pend) > SKEW:
                                      f_, r_s, w_s = pend.pop(0)
                                      P.op("pe", f_, r_s, w_s)
                                  yield
                          while pend:
                              f_, r_s, w_s = pend.pop(0)
                              P.op("pe", f_, r_s, w_s)
                          for h in heads:
                              ob, oc = otu(h)
                              rc_ = REC[h % 2]
                              nr, dr = (0, 64) if h % 2 == 0 else (64, 0)
                              P.op("act", lambda e, ob=ob, oc=oc, rc_=rc_, dr=dr: e.activation(out=rc_.v[dr:dr + 64, 0:n], in_=pbs[ob][dr:dr + 64, oc:oc + n], func=AF.Ln), [pu[ob][0]], [rc_.u])
                              P.op("act", lambda e, rc_=rc_, dr=dr: e.activation(out=rc_.v[dr:dr + 64, 0:n], in_=rc_.v[dr:dr + 64, 0:n], func=AF.Exp, scale=-1.0), [rc_.u], [rc_.u])
                              P.op("dve", lambda e, h=h, ob=ob, oc=oc, rc_=rc_, nr=nr, dr=dr: e.tensor_tensor(out=MIXT.v[nr:nr + 64, h // 2, 0:n], in0=pbs[ob][nr:nr + 64, oc:oc + n], in1=rc_.v[dr:dr + 64, 0:n], op=ALU.mult),
                                   [pu[ob][0], rc_.u], [MIXT.u])
                          yield

                  if ti + 1 < len(sq.tiles):
                      nsq, nti = sq, ti + 1
                  else:
                      k_ = seqs.index(sq)
                      nsq, nti = (seqs[k_ + 1], 0) if k_ + 1 < len(seqs) else (None, 0)
                  if nsq is not None:
                      s1cache[(nsq.kind, nsq.idx, nti)] = do_step1(nsq, nsq.tiles[nti][0], nsq.tiles[nti][1])
                  for _ in gen_mix():
                      pass
                  for _ in gen_att():
                      pass


                  mark_stage('attn')
                  for bi, (o, nb) in enumerate(blocks):
                      def omm(e, o=o, nb=nb):
                          r = None
                          for c in range(2):
                              for kt in range(8):
                                  r = e.matmul(pbs[6 + c][0:nb, :], lhsT=MIXT.v[:, kt, o:o + nb], rhs=WOUT.v[:, kt, 512 * c:512 * c + 512], start=(kt == 0), stop=(kt == 7))
                          return r
                      P.op("pe", omm, [MIXT.u, WOUT.u], pu[6] + pu[7])
                      xo = XOUT[xo_ctr[0] % 2]
                      xo_ctr[0] += 1
                      post_norm_res(P, pbs, pu, 6, 7, nb, xins[bi], xo, B_PM, JUNK, SM, smu, CF, PBC, T1)
                      r = sq.row0 + t0 + o
                      DMA(XA[r:r + nb, :], xo.v[0:nb, :], [xo.u], [uXA[r // 128]], xo.name)

              mark_stage('seq-tiles-done')
              s = sq.idx
              for pt in range(2):
                  P.op("pe", lambda e, pt=pt: e.transpose(out=pbs[7][0:15, 128 * pt:128 * pt + 128], in_=UT.v[:, pt, 0:15], identity=ident_f), [UT.u, CF.u], pu[7])
              P.op("act", lambda e: e.activation(out=STT.v[0:15, 0:256], in_=pbs[7][0:15, 0:256], func=AF.Copy), pu[7], [STT.u])
              DMA(O["pool_p"][L] if isP else O["pool_s"][L, s], STT.v[0:15, 0:256], [STT.u], [uOUT], "c_stt")
              for ct in range(5):
                  P.op("pe", lambda e, ct=ct: e.transpose(out=pbs[3 + ct // 4][0:3, 128 * (ct % 4):128 * (ct % 4) + 128], in_=XBC.v[:, ct, 0:3], identity=ident_f), [XBC.u, CF.u], pu[3] + pu[4])
              P.op("act", lambda e: e.activation(out=T1.v[0:3, 0:512], in_=pbs[3][0:3, 0:512], func=AF.Copy), pu[3], [T1.u])
              P.op("act", lambda e: e.activation(out=T1.v[0:3, 512:640], in_=pbs[4][0:3, 0:128], func=AF.Copy), pu[4], [T1.u])
              DMA(O["conv_p"][L] if isP else O["conv_s"][L, s], T1.v[0:3, 0:640], [T1.u], [uOUT], "c_t1")
              for j in range(3):
                  P.op("pe", lambda e, j=j: e.transpose(out=pbs[5][0:64, 128 * j:128 * j + 128], in_=SST.v[:, 64 * j:64 * j + 64], identity=ident_f), [SST.u, CF.u], pu[5])
              P.op("act", lambda e: e.activation(out=SIN.v[0:64, :, :], in_=pbs[5][0:64, 0:384].rearrange("p (a b) -> p a b", a=3), func=AF.Copy), pu[5], [SIN.u])
              od = O["ssd_p"][L] if isP else O["ssd_s"][L, s]
              for j in range(3):
                  DMA(od[j], SIN.v[0:64, j, 0:64], [SIN.u], [uOUT], f"c_so{j}a")
                  DMA(od[j + 3], SIN.v[0:64, j, 64:128], [SIN.u], [uOUT], f"c_so{j}b")

          mark_stage('phaseA-done')
          P.barrier()
          AR.off = mark
          WG = AR.alloc("wg", [8, DFF], BF16)
          WU = AR.alloc("wu", [8, DFF], BF16)
          WD = AR.alloc("wd", [NFC, D], BF16)
          H1 = AR.alloc("h1", [NFC, 256], BF16)
          GE = [AR.alloc(f"ge{i}", [256], F32) for i in range(3)]
          for kc in range(8):
              for c0 in range(0, DFF, WCH):
                  c1 = min(DFF, c0 + WCH)
                  load_cast(WG.v[:, kc, c0:c1], WG.u, I["w_gate"][L, kc * 128:(kc + 1) * 128, c0:c1], c1 - c0, PCOL.v[:, C_GF + kc:C_GF + kc + 1])
                  load_cast(WU.v[:, kc, c0:c1], WU.u, I["w_up"][L, kc * 128:(kc + 1) * 128, c0:c1], c1 - c0, PCOL.v[:, C_GF + kc:C_GF + kc + 1])
          for fc in range(NFC):
              for c0 in range(0, D, WCH):
                  load_cast(WD.v[:, fc, c0:c0 + WCH], WD.u, I["w_down"][L, fc * 128:(fc + 1) * 128, c0:c0 + WCH], WCH)

          P.barrier()
          btiles = [(0, META)] + [(META + 256 * j, 256) for j in range(NPT)] + [(T, NSEQ * DSEQ)]
          xin_ctr = [0]
          xo_ctr = [0]
          gu_ctr = [0]
          HT2 = AR.alloc("hT2", [8, 256], BF16)
          HTS = [HT, HT2]
          bstate = {}

          def b_front(ti):
              r0, n = btiles[ti]
              HTc = HTS[ti % 2]
              blocks = [(o, min(128, n - o)) for o in range(0, n, 128)]
              xins = []
              for bi, (o, nb) in enumerate(blocks):
                  xin = XIN[xin_ctr[0] % 4]
                  xin_ctr[0] += 1
                  xins.append(xin)
                  r = r0 + o
                  DMA(xin.v[0:nb, :], XA[r:r + nb, :], [uXA[r // 128]] + ([uXA[(r + nb - 1) // 128]] if (r + nb - 1) // 128 != r // 128 else []), [xin.u], xin.name)
                  hb = HBF[bi % 2]
                  ssq = SM.v[0:nb, 0:1]
                  rs = SM.v[0:nb, 1:2]
                  P.op("act", lambda e, xin=xin, nb=nb, ssq=ssq: e.activation(out=JUNK.v[0:nb, :], in_=xin.v[0:nb, :], func=AF.Square, accum_out=ssq), [xin.u], [JUNK.u, smu[0]])
                  P.op("act", lambda e, nb=nb, ssq=ssq, rs=rs: e.activation(out=rs, in_=ssq, func=AF.Ln, bias=CF.v[0:nb, K_EPS:K_EPS + 1], scale=1.0 / D), [smu[0], CF.u], [smu[0]])
                  P.op("act", lambda e, rs=rs: e.activation(out=rs, in_=rs, func=AF.Exp, scale=-0.5), [smu[0]], [smu[0]])
                  P.op("dve", lambda e, hb=hb, xin=xin, nb=nb, rs=rs: e.tensor_scalar(out=hb.v[0:nb, :], in0=xin.v[0:nb, :], scalar1=rs, scalar2=CF.v[0:nb, K_Z:K_Z + 1], op0=ALU.mult, op1=ALU.add),
                       [xin.u, smu[0], CF.u], [hb.u])

                  def tr(e, hb=hb, nb=nb):
                      r_ = None
                      for kc in range(8):
                          r_ = e.transpose(out=pbs[0][:, kc * 128:kc * 128 + nb], in_=hb.v[0:nb, kc * 128:(kc + 1) * 128], identity=ident_b[0:nb, 0:nb])
                      return r_
                  P.op("pe", tr, [hb.u, CB.u], pu[0])
                  P.op("act", lambda e, o=o, nb=nb: e.activation(out=HTc.v[:, :, o:o + nb], in_=pbs[0][:, :].rearrange("p (a b) -> p a b", a=8)[:, :, 0:nb], func=AF.Copy), pu[0], [HTc.u])
              bstate[ti] = (blocks, xins)

          def b_gu(ti):
              r0, n = btiles[ti]
              HTc = HTS[ti % 2]
              for fc in range(NFC):
                  sl = gu_ctr[0] % 3
                  gu_ctr[0] += 1
                  (gb_, gc_), (ub_, uc_) = (1 + sl, 0), (1 + sl, 256)
                  gus = [pu[gb_][0], pu[ub_][0]]

                  def gmm2(e, fc=fc, gb_=gb_, gc_=gc_, ub_=ub_, uc_=uc_):
                      r_ = None
                      for kc in range(8):
                          r_ = e.matmul(pbs[gb_][:, gc_:gc_ + n], lhsT=WG.v[:, kc, 128 * fc:128 * fc + 128], rhs=HTc.v[:, kc, 0:n], start=(kc == 0), stop=(kc == 7))
                      for kc in range(8):
                          r_ = e.matmul(pbs[ub_][:, uc_:uc_ + n], lhsT=WU.v[:, kc, 128 * fc:128 * fc + 128], rhs=HTc.v[:, kc, 0:n], start=(kc == 0), stop=(kc == 7))
                      return r_
                  P.op("pe", gmm2, [WG.u, WU.u, HTc.u], gus)
                  ge = GE[sl]
                  P.op("act", lambda e, ge=ge, gb_=gb_, gc_=gc_: e.activation(out=ge.v[:, 0:n], in_=pbs[gb_][:, gc_:gc_ + n], func=AF.Exp, scale=-1.0), [gus[0]], [ge.u])
                  P.op("act", lambda e, ge=ge: e.activation(out=ge.v[:, 0:n], in_=ge.v[:, 0:n], func=AF.Ln, bias=1.0), [ge.u], [ge.u])
                  P.op("act", lambda e, ge=ge: e.activation(out=ge.v[:, 0:n], in_=ge.v[:, 0:n], func=AF.Exp, scale=-1.0), [ge.u], [ge.u])
                  P.op("dve", lambda e, ge=ge, gb_=gb_, gc_=gc_: e.tensor_tensor(out=ge.v[:, 0:n], in0=pbs[gb_][:, gc_:gc_ + n], in1=ge.v[:, 0:n], op=ALU.mult), [gus[0], ge.u], [ge.u])
                  P.op("dve", lambda e, ge=ge, ub_=ub_, uc_=uc_, fc=fc: e.tensor_tensor(out=H1.v[:, fc, 0:n], in0=pbs[ub_][:, uc_:uc_ + n], in1=ge.v[:, 0:n], op=ALU.mult), [gus[1], ge.u], [H1.u])

          def b_down(ti):
              r0, n = btiles[ti]
              blocks, xins = bstate[ti]
              for bi, (o, nb) in enumerate(blocks):
                  ba, bb = (4, 5) if bi % 2 == 0 else (6, 7)

                  def dmm(e, o=o, nb=nb, ba=ba, bb=bb):
                      r_ = None
                      for c, bk in ((0, ba), (1, bb)):
                          for fc in range(NFC):
                              r_ = e.matmul(pbs[bk][0:nb, :], lhsT=H1.v[:, fc, o:o + nb], rhs=WD.v[:, fc, 512 * c:512 * c + 512], start=(fc == 0), stop=(fc == NFC - 1))
                      return r_
                  P.op("pe", dmm, [H1.u, WD.u], pu[ba] + pu[bb])
                  xo = XOUT[xo_ctr[0] % 2]
                  xo_ctr[0] += 1
                  post_norm_res(P, pbs, pu, ba, bb, nb, xins[bi], xo, B_PF, JUNK, SM, smu, CF, PBC, T1)
                  r = r0 + o
                  if L < DEPTH - 1:
                      DMA(XB[r:r + nb, :], xo.v[0:nb, :], [xo.u], [uXB[r // 128], uXB[(r + nb - 1) // 128]], xo.name)
                  else:
                      if r0 == 0:
                          pass
                      elif r0 < T:
                          DMA(O["y_p"][r - META:r - META + nb, :], xo.v[0:nb, :], [xo.u], [uOUT], xo.name)
                      else:
                          DMA(O["y_s"][r - T:r - T + nb, :], xo.v[0:nb, :], [xo.u], [uOUT], xo.name)


          b_front(0)
          for ti in range(len(btiles)):
              b_gu(ti)
              if ti + 1 < len(btiles):
                  b_front(ti + 1)
              b_down(ti)
    except _Stop:
        pass
    P.barrier()
    semkeys = list(ENGS) + sorted(P.dcnt.keys())
    sems = {k: es.enter_context(nc.semaphore(k)) for k in semkeys}

    def replay(qn, e):
        for waits, fn, ev, inc in P.q[qn]:
            for k, v in waits:
                e.wait_ge(sems[k], v)
            if fn is None:
                continue
            ins = None
            for m_, a_, kw_ in fn:
                ins = getattr(e, m_)(*a_, **kw_)
            ins.then_inc(sems[ev[0]], inc)

    with nc.Block() as block:
        @block.sync
        def _(e):
            replay("sp", e)

        @block.tensor
        def _(e):
            replay("pe", e)

        @block.scalar
        def _(e):
            replay("act", e)

        @block.vector
        def _(e):
            replay("dve", e)

        @block.gpsimd
        def _(e):
            replay("pool", e)
    es.close()
    return nc


def post_norm_res(P, pbs, pu, ba, bb, nb, xin, xo, goff, JUNK, SM, smu, CF, PBC, T1):
    s0 = SM.v[0:nb, 4:5]
    s1 = SM.v[0:nb, 5:6]
    rs = SM.v[0:nb, 6:7]
    P.op("act", lambda e: e.activation(out=JUNK.v[0:nb, 0:512], in_=pbs[ba][0:nb, :], func=AF.Square, accum_out=s0), pu[ba], [JUNK.u, smu[2]])
    P.op("act", lambda e: e.activation(out=JUNK.v[0:nb, 512:1024], in_=pbs[bb][0:nb, :], func=AF.Square, accum_out=s1), pu[bb], [JUNK.u, smu[2]])
    P.op("dve", lambda e: e.tensor_tensor(out=s0, in0=s0, in1=s1, op=ALU.add), [smu[2]], [smu[2]])
    P.op("act", lambda e: e.activation(out=rs, in_=s0, func=AF.Ln, bias=CF.v[0:nb, K_EPS:K_EPS + 1], scale=1.0 / D), [smu[2], CF.u], [smu[2]])
    P.op("act", lambda e: e.activation(out=rs, in_=rs, func=AF.Exp, scale=-0.5), [smu[2]], [smu[2]])
    P.op("dve", lambda e: e.scalar_tensor_tensor(out=T1.v[0:nb, 0:512], in0=pbs[ba][0:nb, :], scalar=rs, in1=PBC.v[0:nb, goff:goff + 512], op0=ALU.mult, op1=ALU.mult),
         pu[ba] + [smu[2], PBC.u], [T1.u])
    P.op("dve", lambda e: e.scalar_tensor_tensor(out=T1.v[0:nb, 512:1024], in0=pbs[bb][0:nb, :], scalar=rs, in1=PBC.v[0:nb, goff + 512:goff + 1024], op0=ALU.mult, op1=ALU.mult),
         pu[bb] + [smu[2], PBC.u], [T1.u])
    P.op("dve", lambda e: e.tensor_tensor(out=xo.v[0:nb, :], in0=T1.v[0:nb, :], in1=xin.v[0:nb, :], op=ALU.add), [T1.u, xin.u], [xo.u])


def _consts():
    cf = np.zeros((128, NCF), np.float32)
    p = np.arange(128)
    cf[:, K_ID:K_ID + 128] = np.eye(128, dtype=np.float32)
    cf[:, K_TRI:K_TRI + 128] = (p[:, None] <= p[None, :]).astype(np.float32)
    cf[:, K_ONE:K_ONE + 128] = 1.0
    cf[15, K_S16:K_S16 + 128] = 1.0
    cf[31, K_S32:K_S32 + 128] = 1.0
    cf[127, K_S128:K_S128 + 128] = 1.0
    cf[:, K_MSK:K_MSK + 128] = np.where(p[:, None] <= p[None, :], 0.0, NEG).astype(np.float32)
    for pt in range(2):
        for pp in range(128):
            g = 2 * pt + (1 if pp >= 64 else 0)
            w = 2 ** (g + 1)
            for pos in range(16):
                cf[pp, K_RC + 16 * pt + pos] = 1.0 / min(pos + 1, w)
    cf[:, K_Z] = 0.0
    cf[:, K_1] = 1.0
    cf[:, K_EPS] = EPS
    cb = np.zeros((128, NCB), np.float32)
    cb[:, KB_ID:KB_ID + 128] = cf[:, K_ID:K_ID + 128]
    cb[:, KB_MSK:KB_MSK + 128] = cf[:, K_MSK:K_MSK + 128]
    cb[:, KB_TRI:KB_TRI + 128] = cf[:, K_TRI:K_TRI + 128]
    cb[:, KB_S16:KB_S16 + 128] = cf[:, K_S16:K_S16 + 128]
    cb[:, KB_S32:KB_S32 + 128] = cf[:, K_S32:K_S32 + 128]
    cb[:, KB_S128:KB_S128 + 128] = cf[:, K_S128:K_S128 + 128]
    return cf, cb.astype(ml_dtypes.bfloat16)


_CACHE = {}


def kernel(x_prompt, x_sample, cache_fox_k, cache_fox_v, cache_fox_logf, state_pool, state_conv,
           state_ssd, meta_tokens, ln_pre_mix, ln_post_mix, ln_pre_ffn, ln_post_ffn, w_in,
           fox_f_bias, pool_w, pool_scale, conv_w, conv_b, dt_bias, a_log, d_skip, ssd_norm,
           w_out, w_gate, w_up, w_down):
    f = lambda a: np.ascontiguousarray(np.asarray(a, dtype=np.float32))
    x_prompt, x_sample = f(x_prompt), f(x_sample)
    BATCH, SEQ, _ = x_prompt.shape
    DEPTH, DB, PAST = cache_fox_k.shape[0], cache_fox_k.shape[1], cache_fox_k.shape[2]
    NCORE = 8
    assert DB == NCORE * NSEQ and x_sample.shape[1] == DSEQ
    T = META + SEQ
    key = (SEQ, PAST, DEPTH)
    if key not in _CACHE:
        _CACHE[key] = build(SEQ, PAST, DEPTH)
    nc = _CACHE[key]
    perm = np.concatenate([np.arange(0, 384), np.arange(384, 768), np.arange(768, 1152), np.arange(1152, 1158),
                           np.arange(2438, 2444), np.arange(1414, 1798), np.arange(1158, 1414), np.arange(1798, 2438)])
    w_in_r = np.ascontiguousarray(f(w_in)[:, :, perm])
    pbc = np.zeros((DEPTH, 128, NBC), np.float32)
    pcol = np.zeros((DEPTH, 128, NCOL), np.float32)
    poolw = np.zeros((DEPTH, 128, 256), np.float32)
    for l in range(DEPTH):
        pbc[l, :, B_PM:B_PM + 1024] = f(ln_post_mix)[l][None]
        pbc[l, :, B_PF:B_PF + 1024] = f(ln_post_ffn)[l][None]
        pbc[l, :, B_SN:B_SN + 384] = f(ssd_norm)[l][None]
        pbc[l, :, B_FB:B_FB + 6] = f(fox_f_bias)[l][None]
        pbc[l, :, B_FB + 6:B_FB + 12] = f(dt_bias)[l][None]
        pbc[l, :, B_AL:B_AL + 6] = f(a_log)[l][None]
        pbc[l, :, B_DS:B_DS + 6] = f(d_skip)[l][None]
        pcol[l, :, C_GM:C_GM + 8] = f(ln_pre_mix)[l].reshape(8, 128).T
        pcol[l, :, C_GF:C_GF + 8] = f(ln_pre_ffn)[l].reshape(8, 128).T
        pcol[l, :, C_PS:C_PS + 2] = f(pool_scale)[l].reshape(2, 128).T
        cw = f(conv_w)[l]
        for ct in range(5):
            pcol[l, :, C_CW + 4 * ct:C_CW + 4 * ct + 4] = cw[:, 128 * ct:128 * ct + 128].T
        pcol[l, :, C_CB:C_CB + 5] = f(conv_b)[l].reshape(5, 128).T
        pw = f(pool_w)[l]
        for g in range(4):
            pt, hf = g // 2, g % 2
            poolw[l, 64 * hf:64 * hf + 64, 128 * pt + 64 * hf:128 * pt + 64 * hf + 64] = pw[g]
    cf, cb = _consts()
    shared = dict(w_in=w_in_r, w_out=f(w_out), w_gate=f(w_gate), w_up=f(w_up), w_down=f(w_down),
                  pbc=pbc, pcol=pcol, poolw=poolw, cstf=cf, cstb=cb, meta=f(meta_tokens))
    ck, cv, cl = f(cache_fox_k), f(cache_fox_v), f(cache_fox_logf)
    sp_, sc_, ss_ = f(state_pool), f(state_conv), f(state_ssd)
    in_maps = []
    for c in range(NCORE):
        sl = slice(NSEQ * c, NSEQ * c + NSEQ)
        m = dict(shared)
        m["xp"] = x_prompt[c % BATCH]
        m["xs"] = np.ascontiguousarray(x_sample[sl].reshape(NSEQ * DSEQ, D))
        m["ck"] = np.ascontiguousarray(ck[:, sl].reshape(DEPTH, NSEQ, PAST, 384))
        m["cv"] = np.ascontiguousarray(cv[:, sl].reshape(DEPTH, NSEQ, PAST, 384))
        m["cl"] = np.ascontiguousarray(cl[:, sl])
        m["spool"] = np.ascontiguousarray(sp_[:, sl])
        m["sconv"] = np.ascontiguousarray(sc_[:, sl])
        m["sssd"] = np.ascontiguousarray(ss_[:, sl])
        in_maps.append(m)
    res = run_bass_kernel_spmd(nc, in_maps, core_ids=list(range(NCORE)))
    R = res.results
    y_prompt = np.stack([R[b]["y_p"] for b in range(BATCH)], 0)
    y_sample = np.concatenate([R[c]["y_s"].reshape(NSEQ, DSEQ, D) for c in range(NCORE)], 0)
    stp = lambda k, shp: np.stack([R[b][k].reshape(shp) for b in range(BATCH)], 1)
    sts = lambda k, shp: np.concatenate([R[c][k].reshape(shp) for c in range(NCORE)], 1)
    outs = (
        y_prompt, y_sample,
        stp("k_p", (DEPTH, T, 6, 64)), stp("v_p", (DEPTH, T, 6, 64)), stp("l_p", (DEPTH, T, 6)),
        stp("pool_p", (DEPTH, 15, 256)), stp("conv_p", (DEPTH, 3, 640)), stp("ssd_p", (DEPTH, 6, 64, 64)),
        sts("k_s", (DEPTH, NSEQ, DSEQ, 6, 64)), sts("v_s", (DEPTH, NSEQ, DSEQ, 6, 64)), sts("l_s", (DEPTH, NSEQ, DSEQ, 6)),
        sts("pool_s", (DEPTH, NSEQ, 15, 256)), sts("conv_s", (DEPTH, NSEQ, 3, 640)), sts("ssd_s", (DEPTH, NSEQ, 6, 64, 64)),
    )
    return tuple(np.ascontiguousarray(o, dtype=np.float32) for o in outs)
```

```python
import numpy as np
import ml_dtypes
from contextlib import ExitStack
import concourse.bass as bass
import concourse.mybir as mybir
from concourse.bass_utils import run_bass_kernel_spmd
from concourse.alu_op_type import AluOpType as ALU

F32 = mybir.dt.float32
BF16 = mybir.dt.bfloat16
AF = mybir.ActivationFunctionType
NEG = -30000.0
D = 1024
NIN = 2444
DFF = 2816
NFC = DFF // 128
EPS = 1e-6
META = 16
DSEQ = 32
NSEQ = 4
OQ, OKK, OV, OF, ODT, OZ, OU, OX = 0, 384, 768, 1152, 1158, 1164, 1548, 1804
TM0 = 384
B_PM, B_PF, B_SN, B_FB, B_AL, B_DS, NBC = 0, 1024, 2048, 2432, 2444, 2450, 2456
C_GM, C_GF, C_PS, C_CW, C_CB, NCOL = 0, 8, 16, 18, 38, 43
K_ID, K_TRI, K_ONE, K_S16, K_S32, K_S128, K_MSK, K_RC, K_Z, K_1, K_EPS, NCF = 0, 128, 256, 384, 512, 640, 768, 896, 928, 929, 930, 931
KB_ID, KB_MSK, KB_TRI, KB_S16, KB_S32, KB_S128, NCB = 0, 128, 256, 384, 512, 640, 768
ENGS = ("pe", "act", "dve", "pool")


class U:
    __slots__ = ("name", "w", "r", "excl")

    def __init__(self, name, excl=False):
        self.name = name
        self.w = None
        self.r = {}
        self.excl = excl


class _Rec:
    def __init__(self):
        self.calls = []

    def __getattr__(self, name):
        def f(*a, **kw):
            self.calls.append((name, a, kw))
            return None
        return f


class Prog:
    def __init__(self):
        self.q = {e: [] for e in ENGS + ("sp",)}
        self.cnt = {e: 0 for e in ENGS}
        self.waited = {e: {} for e in ENGS + ("sp",)}
        self.dcnt = {}

    def _waits(self, eng, reads, writes):
        waits = []
        wd = self.waited[eng]

        def need(ev):
            if ev is None:
                return
            k, v = ev
            if wd.get(k, 0) >= v:
                return
            wd[k] = v
            waits.append((k, v))

        for u in reads:
            need(u.w)
        for u in writes:
            need(u.w)
            for k, v in u.r.items():
                need((k, v))
        return waits

    def op(self, eng, fn, reads=(), writes=()):
        ex = [u for u in reads if u.excl]
        if ex:
            reads = [u for u in reads if not u.excl]
            writes = list(writes) + ex
        if callable(fn):
            rec = _Rec()
            fn(rec)
            fn = rec.calls
        assert isinstance(fn, list) and len(fn) > 0, fn
        waits = self._waits(eng, reads, writes)
        self.cnt[eng] += 1
        ev = (eng, self.cnt[eng])
        self.q[eng].append((waits, fn, ev, 1))
        for u in reads:
            if u.r.get(eng, 0) < ev[1]:
                u.r[eng] = ev[1]
        for u in writes:
            u.w = ev
            u.r = {}

    def dma(self, fn, reads, writes, sem):
        waits = self._waits("sp", reads, writes)
        prev = self.dcnt.get(sem, 0)
        if prev > self.waited["sp"].get(sem, 0):
            self.waited["sp"][sem] = prev
            waits.append((sem, prev))
        self.dcnt[sem] = prev + 16
        ev = (sem, self.dcnt[sem])
        self.q["sp"].append((waits, fn, ev, 16))
        for u in reads:
            if u.r.get(sem, 0) < ev[1]:
                u.r[sem] = ev[1]
        for u in writes:
            u.w = ev
            u.r = {}

    def barrier(self):
        for e in ENGS + ("sp",):
            waits = []
            wd = self.waited[e]
            for f in ENGS:
                if f != e and self.cnt[f] > wd.get(f, 0):
                    wd[f] = self.cnt[f]
                    waits.append((f, self.cnt[f]))
            for s, v in self.dcnt.items():
                if v > wd.get(s, 0):
                    wd[s] = v
                    waits.append((s, v))
            if waits:
                self.q[e].append((waits, None, None, 0))


class Buf:
    __slots__ = ("v", "u", "name")

    def __init__(self, v, name):
        self.v = v
        self.u = U(name)
        self.name = name


class Arena:
    def __init__(self, ap, nwords):
        self.ap = ap
        self.nwords = nwords
        self.off = 0
        self.peak = 0

    def alloc(self, name, shape, dt):
        n = 1
        for s in shape:
            n *= s
        words = n if dt == F32 else (n + 1) // 2
        assert self.off + words <= self.nwords, f"arena overflow at {name}: {self.off + words} > {self.nwords}"
        v = self.ap[:, self.off:self.off + words]
        self.off += words
        self.peak = max(self.peak, self.off)
        if dt == BF16:
            v = v.bitcast(BF16)
            if 2 * words != n:
                v = v[:, 0:n]
        if len(shape) == 2:
            v = v.rearrange("p (a b) -> p a b", a=shape[0])
        elif len(shape) == 3:
            v = v.rearrange("p (a b c) -> p a b c", a=shape[0], b=shape[1])
        return Buf(v, name)


class _Stop(Exception):
    pass


def build(SEQ, PAST, DEPTH, stop_at=None):
    T = META + SEQ
    stage = [0]

    def mark_stage(name):
        stage[0] += 1
        if stop_at is not None:
            print("stage", stage[0], name, flush=True)
            if stage[0] >= stop_at:
                raise _Stop()

    NPT = SEQ // 256
    NKB_P = 1 + SEQ // 128
    NCB_S = PAST // 128
    NKB = max(NKB_P, NCB_S + 1)
    NROW = T + NSEQ * DSEQ
    nc = bass.Bass("TRN2", target_bir_lowering=False)

    def din(name, shape, dt=F32):
        return nc.dram_tensor(name, list(shape), dt, kind="ExternalInput").ap()

    def dout(name, shape):
        return nc.dram_tensor(name, list(shape), F32, kind="ExternalOutput").ap()

    def dscr(name, shape, dt=F32):
        return nc.dram_tensor(name, list(shape), dt, kind="Internal").ap()

    I = dict(
        xp=din("xp", [SEQ, D]), meta=din("meta", [META, D]), xs=din("xs", [NSEQ * DSEQ, D]),
        ck=din("ck", [DEPTH, NSEQ, PAST, 384]), cv=din("cv", [DEPTH, NSEQ, PAST, 384]),
        cl=din("cl", [DEPTH, NSEQ, PAST, 6]), spool=din("spool", [DEPTH, NSEQ, 15, 256]),
        sconv=din("sconv", [DEPTH, NSEQ, 3, 640]), sssd=din("sssd", [DEPTH, NSEQ, 6, 64, 64]),
        w_in=din("w_in", [DEPTH, D, NIN]), w_out=din("w_out", [DEPTH, D, D]),
        w_gate=din("w_gate", [DEPTH, D, DFF]), w_up=din("w_up", [DEPTH, D, DFF]),
        w_down=din("w_down", [DEPTH, DFF, D]),
        pbc=din("pbc", [DEPTH, 128, NBC]), pcol=din("pcol", [DEPTH, 128, NCOL]),
        poolw=din("poolw", [DEPTH, 128, 256]),
        cstf=din("cstf", [128, NCF]), cstb=din("cstb", [128, NCB], BF16),
    )
    O = dict(
        y_p=dout("y_p", [SEQ, D]), y_s=dout("y_s", [NSEQ * DSEQ, D]),
        k_p=dout("k_p", [DEPTH, T, 384]), v_p=dout("v_p", [DEPTH, T, 384]), l_p=dout("l_p", [DEPTH, T, 6]),
        pool_p=dout("pool_p", [DEPTH, 15, 256]), conv_p=dout("conv_p", [DEPTH, 3, 640]),
        ssd_p=dout("ssd_p", [DEPTH, 6, 64, 64]),
        k_s=dout("k_s", [DEPTH, NSEQ, DSEQ, 384]), v_s=dout("v_s", [DEPTH, NSEQ, DSEQ, 384]),
        l_s=dout("l_s", [DEPTH, NSEQ, DSEQ, 6]), pool_s=dout("pool_s", [DEPTH, NSEQ, 15, 256]),
        conv_s=dout("conv_s", [DEPTH, NSEQ, 3, 640]), ssd_s=dout("ssd_s", [DEPTH, NSEQ, 6, 64, 64]),
    )
    XA = dscr("xa", [NROW, D])
    XB = dscr("xb", [NROW, D])
    KSC = dscr("ksc", [NKB_P, 128, 768], BF16)
    VSC = dscr("vsc", [NKB_P, 128, 576], BF16)
    uXA = [U(f"xa{i}") for i in range((NROW + 127) // 128 + 1)]
    uXB = [U(f"xb{i}") for i in range((NROW + 127) // 128 + 1)]
    uKSC = [U(f"ksc{i}") for i in range(NKB_P)]
    uIN = U("inputs")
    uOUT = U("outputs")

    P = Prog()
    es = ExitStack()
    ARW = 53100
    arena_t = es.enter_context(nc.sbuf_tensor("arena", [128, ARW], F32))
    AR = Arena(arena_t[:, :], ARW)
    pb0 = es.enter_context(nc.psum_tensor("pb0", [128, 1024], BF16))
    pbs = [pb0] + [es.enter_context(nc.psum_tensor(f"pb{i}", [128, 512], F32)) for i in range(1, 8)]
    pb0f = pb0[:, :].bitcast(F32)
    pu = []
    for i in range(8):
        _u = U(f"ps{i}", excl=True)
        pu.append([_u, _u])

    def psu(bank, c0, c1):
        half = 512 if bank == 0 else 256
        us = []
        if c0 < half:
            us.append(pu[bank][0])
        if c1 > half:
            us.append(pu[bank][1])
        return us

    CF = AR.alloc("cstf", [NCF], F32)
    CB = AR.alloc("cstb", [NCB], BF16)
    PBC = AR.alloc("pbc", [NBC], F32)
    PCOL = AR.alloc("pcol", [NCOL], F32)
    PWF = AR.alloc("pwf", [256], F32)
    PWB = AR.alloc("pwb", [2, 128], BF16)
    ABC = AR.alloc("abc", [6], F32)
    NST = 2
    WCH = 512
    WST = [AR.alloc(f"wst{i}", [WCH], F32) for i in range(NST)]
    XIN = [AR.alloc(f"xin{i}", [D], F32) for i in range(4)]
    XOUT = [AR.alloc(f"xout{i}", [D], F32) for i in range(2)]
    HBF = [AR.alloc(f"hbf{i}", [D], BF16) for i in range(2)]
    HT = AR.alloc("hT", [8, 256], BF16)
    _t1off = AR.off
    T1 = AR.alloc("t1", [D], F32)
    JUNK = Buf(AR.ap[:, _t1off:_t1off + D // 2].bitcast(BF16), "junk")
    JUNK.u = T1.u
    SM = AR.alloc("small", [64], F32)
    smu = [U(f"sm{i}") for i in range(8)]
    mark = AR.off

    ident_f = CF.v[:, K_ID:K_ID + 128]
    tri_f = CF.v[:, K_TRI:K_TRI + 128]
    ones_f = CF.v[:, K_ONE:K_ONE + 128]
    mask_f = CF.v[:, K_MSK:K_MSK + 128]
    zcol = CF.v[:, K_Z:K_Z + 1]
    ecol = CF.v[:, K_EPS:K_EPS + 1]
    ident_b = CB.v[:, KB_ID:KB_ID + 128]
    mask_b = CB.v[:, KB_MSK:KB_MSK + 128]
    tri_b = CB.v[:, KB_TRI:KB_TRI + 128]

    def sel_f(n):
        o = {16: K_S16, 32: K_S32, 128: K_S128}[n]
        return CF.v[:, o:o + 128]

    def sel_b(n):
        o = {16: KB_S16, 32: KB_S32, 128: KB_S128}[n]
        return CB.v[:, o:o + 128]

    dma_q = [0]

    def DMA(out, in_, reads, writes, sem):
        P.dma([("dma_start", (), dict(out=out, in_=in_))], reads, writes, sem)

    DMA(CF.v, I["cstf"][:, :], [uIN], [CF.u], "c_cf")
    DMA(CB.v, I["cstb"][:, :], [uIN], [CB.u], "c_cb")

    wst_ctr = [0]
    STG = list(WST)
    for i_ in range(4):
        STG.append(Buf(XIN[i_].v[:, 0:WCH], f"xstg{i_}a"))
        STG.append(Buf(XIN[i_].v[:, WCH:2 * WCH], f"xstg{i_}b"))
    CAST_ENG = ("pool", "dve", "act")

    def load_cast(dst_view, dst_u, src_ap, ncols, gcol=None):
        st = STG[wst_ctr[0] % len(STG)]
        eng = CAST_ENG[wst_ctr[0] % 3]
        dst_u = U("wchunk")
        wst_ctr[0] += 1
        DMA(st.v[:, 0:ncols], src_ap, [uIN], [st.u], st.name)
        if eng == "act":
            if gcol is None:
                P.op("act", lambda e: e.activation(out=dst_view, in_=st.v[:, 0:ncols], func=AF.Copy), [st.u], [dst_u])
            else:
                P.op("act", lambda e: e.activation(out=dst_view, in_=st.v[:, 0:ncols], func=AF.Copy, scale=gcol), [st.u, PCOL.u], [dst_u])
        elif gcol is None:
            P.op(eng, lambda e: e.tensor_copy(out=dst_view, in_=st.v[:, 0:ncols]), [st.u], [dst_u])
        else:
            P.op(eng, lambda e: e.tensor_scalar(out=dst_view, in0=st.v[:, 0:ncols], scalar1=gcol, scalar2=zcol, op0=ALU.mult, op1=ALU.add),
                 [st.u, PCOL.u, CF.u], [dst_u])

    class Seq:
        pass

    seqs = []
    sp = Seq()
    sp.kind = "p"
    sp.idx = 0
    sp.tiles = [(0, META)] + [(META + 256 * j, 256) for j in range(NPT)]
    sp.npast = 0
    sp.row0 = 0
    sp.kblocks = [(0, META)] + [(META + 128 * m, 128) for m in range(SEQ // 128)]
    seqs.append(sp)
    for s in range(NSEQ):
        q = Seq()
        q.kind = "s"
        q.idx = s
        q.tiles = [(0, DSEQ)]
        q.npast = PAST
        q.row0 = T + DSEQ * s
        q.kblocks = [(128 * m, 128) for m in range(NCB_S)] + [(PAST, DSEQ)]
        seqs.append(q)

    def xsrc(L, sq, t, nb):
        if L == 0:
            if sq.kind == "p":
                if t < META:
                    return I["meta"][t:t + nb, :], uIN
                return I["xp"][t - META:t - META + nb, :], uIN
            return I["xs"][DSEQ * sq.idx + t:DSEQ * sq.idx + t + nb, :], uIN
        r = sq.row0 + t
        return XB[r:r + nb, :], uXB[r // 128]

    try:
      for L in range(DEPTH):
          P.barrier()
          AR.off = mark
          WIN = AR.alloc("win", [8, NIN], BF16)
          WOUT = AR.alloc("wout", [8, D], BF16)
          NRING = 3
          KVRk = [AR.alloc(f"kvrk{i}", [6, 128], BF16) for i in range(NRING)]
          KVRv = [AR.alloc(f"kvrv{i}", [576], BF16) for i in range(NRING)]
          QT = AR.alloc("qT", [6, 256], BF16)
          KTC = AR.alloc("kTc", [6, 256], BF16)
          VCUR = AR.alloc("vcur", [2, 576], BF16)
          KVST = [AR.alloc(f"kvst{i}", [768], F32) for i in range(2)]
          LFST = [AR.alloc(f"lfst{i}", [6], F32) for i in range(2)]
          FD = AR.alloc("fd", [12], F32)
          E12 = AR.alloc("e12", [12], F32)
          L12 = [AR.alloc(f"l12{i}", [12], F32) for i in range(2)]
          LA_ = AR.alloc("la", [6], F32)
          NACS = AR.alloc("nacs", [6], F32)
          EA = AR.alloc("ea", [6], F32)
          DEC = AR.alloc("dec", [6], F32)
          NEGC = AR.alloc("negc", [NKB, 6], F32)
          STA = AR.alloc("sta", [4, 128], BF16)
          CLT = AR.alloc("clt", [6], F32)
          CLTA = AR.alloc("clta", [max(NCB_S, 1), 6], F32)
          ZS = AR.alloc("zs", [2, 384], F32)
          ZE = AR.alloc("ze", [384], F32)
          UT = AR.alloc("uT", [2, 15 + 256], F32)
          PS2 = AR.alloc("ps2", [15 + 256], F32)
          PS4 = AR.alloc("ps4", [15 + 256], F32)
          DIFF = AR.alloc("diff", [2, 256], BF16)
          XBC = AR.alloc("xbcT", [5, 3 + 256], F32)
          ACC = AR.alloc("acc", [256], F32)
          CE = AR.alloc("ce", [256], F32)
          XS = AR.alloc("xsT", [3, 256], F32)
          BT = AR.alloc("bT", [256], BF16)
          CT = AR.alloc("cT", [256], BF16)
          XX = AR.alloc("xx", [6, 128], F32)
          LT = AR.alloc("lt", [6, 128], F32)
          MT = AR.alloc("mt", [6, 128], BF16)
          XSTOK = AR.alloc("xstok", [384], F32)
          XD = AR.alloc("xd", [6, 64], BF16)
          BW = AR.alloc("bw", [6, 128], BF16)
          Y1 = AR.alloc("y1", [384], F32)
          Y2 = AR.alloc("y2", [384], F32)
          YN = AR.alloc("yn", [384], BF16)
          SST = AR.alloc("sst", [192], F32)
          SBF = AR.alloc("sbf", [192], BF16)
          PT = [AR.alloc(f"pt{i}", [256], BF16) for i in range(4)]
          REC = [AR.alloc(f"rec{i}", [256], F32) for i in range(2)]
          MIXT = AR.alloc("mixT", [8, 256], BF16)
          CKS = [AR.alloc(f"cks{i}", [384], F32) for i in range(2)]
          CVS = [AR.alloc(f"cvs{i}", [384], F32) for i in range(2)]
          KB16 = AR.alloc("kb16", [384], BF16)
          STT = AR.alloc("stt", [256], F32)
          SIN = AR.alloc("sin", [3, 128], F32)

          DMA(PBC.v, I["pbc"][L], [uIN], [PBC.u], "c_pbc")
          DMA(PCOL.v, I["pcol"][L], [uIN], [PCOL.u], "c_pcol")
          DMA(PWF.v, I["poolw"][L], [uIN], [PWF.u], "c_pwf")
          P.op("pool", lambda e: e.tensor_copy(out=PWB.v.rearrange("p a b -> p (a b)"), in_=PWF.v), [PWF.u], [PWB.u])
          P.op("act", lambda e: e.activation(out=ABC.v, in_=PBC.v[:, B_AL:B_AL + 6], func=AF.Exp), [PBC.u], [ABC.u])
          P.op("act", lambda e: e.activation(out=ABC.v, in_=ABC.v, func=AF.Copy, scale=-1.0), [ABC.u], [ABC.u])
          for kc in range(8):
              for c0 in range(0, NIN, WCH):
                  c1 = min(NIN, c0 + WCH)
                  load_cast(WIN.v[:, kc, c0:c1], WIN.u, I["w_in"][L, kc * 128:(kc + 1) * 128, c0:c1], c1 - c0,
                            PCOL.v[:, C_GM + kc:C_GM + kc + 1])
          for kc in range(8):
              for c0 in range(0, D, WCH):
                  load_cast(WOUT.v[:, kc, c0:c0 + WCH], WOUT.u, I["w_out"][L, kc * 128:(kc + 1) * 128, c0:c0 + WCH], WCH)
          P.barrier()
          for b_ in [QT, KTC] + KVRk:
              P.op("pool", lambda e, b_=b_: e.memset(b_.v, 0.0), [], [b_.u])
          for b_ in [KTC] + KVRk:
              P.op("pool", lambda e, b_=b_: e.memset(b_.v[64:65, :, :], 1.0), [], [b_.u])
          P.op("pool", lambda e: e.memset(VCUR.v, 1.0), [], [VCUR.u])
          for b_ in KVRv:
              P.op("pool", lambda e, b_=b_: e.memset(b_.v, 1.0), [], [b_.u])
          P.op("pool", lambda e: e.memset(STA.v, 0.0), [], [STA.u])
          P.op("pool", lambda e: e.memset(BW.v, 0.0), [], [BW.u])
          mark_stage('A-weights')

          xin_ctr = [0]
          kvst_ctr = [0]
          ring_ctr = [0]
          pt_ctr = [0]
          st_ctr = [0]
          fm_ctr = [0]
          xo_ctr = [0]
          cks_ctr = [0]

          def do_step1(sq, t0, n):
              blocks = [(o, min(128, n - o)) for o in range(0, n, 128)]
              xins = []
              for bi, (o, nb) in enumerate(blocks):
                  xin = XIN[xin_ctr[0] % 4]
                  xin_ctr[0] += 1
                  xins.append(xin)
                  src, su = xsrc(L, sq, t0 + o, nb)
                  DMA(xin.v[0:nb, :], src, [su], [xin.u], xin.name)
                  hb = HBF[bi % 2]
                  ssq = SM.v[0:nb, 0:1]
                  rs = SM.v[0:nb, 1:2]
                  P.op("act", lambda e, xin=xin, nb=nb, ssq=ssq: e.activation(out=JUNK.v[0:nb, :], in_=xin.v[0:nb, :], func=AF.Square, accum_out=ssq),
                       [xin.u], [JUNK.u, smu[0]])
                  P.op("act", lambda e, nb=nb, ssq=ssq, rs=rs: e.activation(out=rs, in_=ssq, func=AF.Ln, bias=CF.v[0:nb, K_EPS:K_EPS + 1], scale=1.0 / D), [smu[0], CF.u], [smu[0]])
                  P.op("act", lambda e, rs=rs: e.activation(out=rs, in_=rs, func=AF.Exp, scale=-0.5), [smu[0]], [smu[0]])
                  P.op("dve", lambda e, hb=hb, xin=xin, nb=nb, rs=rs: e.tensor_scalar(out=hb.v[0:nb, :], in0=xin.v[0:nb, :], scalar1=rs, scalar2=CF.v[0:nb, K_Z:K_Z + 1], op0=ALU.mult, op1=ALU.add),
                       [xin.u, smu[0], CF.u], [hb.u])

                  def tr(e, hb=hb, nb=nb):
                      r = None
                      for kc in range(8):
                          r = e.transpose(out=pbs[0][:, kc * 128:kc * 128 + nb], in_=hb.v[0:nb, kc * 128:(kc + 1) * 128], identity=ident_b[0:nb, 0:nb])
                      return r
                  P.op("pe", tr, [hb.u, CB.u], pu[0])
                  P.op("act", lambda e, o=o, nb=nb: e.activation(out=HT.v[:, :, o:o + nb], in_=pbs[0][:, :].rearrange("p (a b) -> p a b", a=8)[:, :, 0:nb], func=AF.Copy),
                       pu[0], [HT.u])

              return xins

          s1cache = {}

          for sq in seqs:
              isP = sq.kind == "p"
              nkb = len(sq.kblocks)
              if isP:
                  P.op("pool", lambda e: e.memset(UT.v[:, :, 0:15], 0.0), [], [UT.u])
                  P.op("pool", lambda e: e.memset(XBC.v[:, :, 0:3], 0.0), [], [XBC.u])
                  P.op("pool", lambda e: e.memset(SST.v, 0.0), [], [SST.u])
                  P.op("pool", lambda e: e.memset(SBF.v, 0.0), [], [SBF.u])
              else:
                  s = sq.idx
                  DMA(STT.v[0:15, 0:256], I["spool"][L, s], [uIN], [STT.u], "c_stt")
                  for pt in range(2):
                      P.op("pe", lambda e, pt=pt: e.transpose(out=pbs[7][:, 256 + 16 * pt:256 + 16 * pt + 15], in_=STT.v[0:15, 128 * pt:128 * pt + 128], identity=ident_f[0:15, 0:15]),
                           [STT.u, CF.u], psu(7, 256, 512))
                  P.op("act", lambda e: e.activation(out=UT.v[:, 0, 0:15], in_=pbs[7][:, 256:271], func=AF.Copy), psu(7, 256, 512), [UT.u])
                  P.op("act", lambda e: e.activation(out=UT.v[:, 1, 0:15], in_=pbs[7][:, 272:287], func=AF.Copy), psu(7, 256, 512), [UT.u])
                  DMA(T1.v[0:3, 0:640], I["sconv"][L, s], [uIN], [T1.u], "c_t1")
                  for ct in range(5):
                      P.op("pe", lambda e, ct=ct: e.transpose(out=pbs[7][:, 4 * ct:4 * ct + 3], in_=T1.v[0:3, 128 * ct:128 * ct + 128], identity=ident_f[0:3, 0:3]),
                           [T1.u, CF.u], psu(7, 0, 256))
                  for ct in range(5):
                      P.op("act", lambda e, ct=ct: e.activation(out=XBC.v[:, ct, 0:3], in_=pbs[7][:, 4 * ct:4 * ct + 3], func=AF.Copy), psu(7, 0, 256), [XBC.u])
                  for j in range(3):
                      DMA(SIN.v[0:64, j, 0:64], I["sssd"][L, s, j], [uIN], [SIN.u], f"c_sin{j}a")
                      DMA(SIN.v[0:64, j, 64:128], I["sssd"][L, s, j + 3], [uIN], [SIN.u], f"c_sin{j}b")
                  for j in range(3):
                      P.op("pe", lambda e, j=j: e.transpose(out=pbs[6][:, 64 * j:64 * j + 64], in_=SIN.v[0:64, j, :], identity=ident_f[0:64, 0:64]),
                           [SIN.u, CF.u], psu(6, 0, 256))
                  P.op("act", lambda e: e.activation(out=SST.v, in_=pbs[6][:, 0:192], func=AF.Copy), psu(6, 0, 256), [SST.u])
                  P.op("dve", lambda e: e.tensor_copy(out=SBF.v, in_=SST.v), [SST.u], [SBF.u])

              if not isP:
                  DMA(CLTA.v, I["cl"][L, s].rearrange("(m p) h -> p m h", p=128), [uIN], [CLTA.u], "c_clta")
                  P.op("dve", lambda e: e.tensor_single_scalar(out=CLTA.v, in_=CLTA.v, scalar=-1.0, op=ALU.mult), [CLTA.u], [CLTA.u])

                  def cs_mm(e):
                      r = None
                      for m in range(NCB_S):
                          r = e.matmul(pbs[6][:, 6 * m:6 * m + 6], lhsT=tri_f, rhs=CLTA.v[:, m, :], start=True, stop=(m == 0))
                          for j in range(m):
                              r = e.matmul(pbs[6][:, 6 * m:6 * m + 6], lhsT=ones_f, rhs=CLTA.v[:, j, :], start=False, stop=(j == m - 1))
                      return r
                  P.op("pe", cs_mm, [CLTA.u, CF.u], pu[6])
                  P.op("act", lambda e: e.activation(out=NEGC.v[:, 0:NCB_S, :], in_=pbs[6][:, 0:6 * NCB_S].rearrange("p (m h) -> p m h", h=6), func=AF.Copy), pu[6], [NEGC.u])

              for ti, (t0, n) in enumerate(sq.tiles):
                  blocks = [(o, min(128, n - o)) for o in range(0, n, 128)]
                  kb_first = (sq.npast // 128) + (0 if (not isP) else (0 if ti == 0 else 1 + 2 * (ti - 1)))
                  if isP:
                      kbs = [0] if ti == 0 else [1 + 2 * (ti - 1), 2 + 2 * (ti - 1)]
                  else:
                      kbs = [NCB_S]
                  key_ = (sq.kind, sq.idx, ti)
                  xins = s1cache.pop(key_) if key_ in s1cache else do_step1(sq, t0, n)

                  mark_stage('step1')
                  l12s = []
                  def do_tm(bi, o, nb):
                      kb = kbs[bi]

                      def tm(e, o=o, nb=nb):
                          r = None
                          for (c0, c1, bank, pc) in ((0, 384, 3, 0), (384, 768, 4, 0), (768, 780, 5, 0), (780, 1164, 5, 128)):
                              for kc in range(8):
                                  r = e.matmul(pbs[bank][0:nb, pc:pc + c1 - c0], lhsT=HT.v[:, kc, o:o + nb], rhs=WIN.v[:, kc, TM0 + c0:TM0 + c1], start=(kc == 0), stop=(kc == 7))
                          return r
                      P.op("pe", tm, [HT.u, WIN.u], pu[3] + pu[4] + pu[5])
                      kv = KVST[kvst_ctr[0] % 2]
                      lf = LFST[kvst_ctr[0] % 2]
                      l12 = L12[kvst_ctr[0] % 2]
                      kvst_ctr[0] += 1
                      l12s.append(l12)
                      P.op("act", lambda e, kv=kv, nb=nb: e.activation(out=kv.v[0:nb, 0:384], in_=pbs[3][0:nb, 0:384], func=AF.Copy), pu[3], [kv.u])
                      P.op("dve", lambda e, kv=kv, nb=nb: e.tensor_copy(out=kv.v[0:nb, 384:768], in_=pbs[4][0:nb, 0:384]), pu[4], [kv.u])
                      vsrc = kv.v[0:nb, 384:768].rearrange("p (j t d) -> p j t d", j=3, t=2)
                      vdst = VCUR.v[0:nb, bi, :].rearrange("p (j x) -> p j x", j=3)
                      P.op("pool", lambda e, vsrc=vsrc, vdst=vdst: e.tensor_copy(out=vdst[:, :, 0:64], in_=vsrc[:, :, 0, :]), [kv.u], [VCUR.u])
                      P.op("pool", lambda e, vsrc=vsrc, vdst=vdst: e.tensor_copy(out=vdst[:, :, 128:192], in_=vsrc[:, :, 1, :]), [kv.u], [VCUR.u])
                      if isP:
                          ko, vo, lo = O["k_p"][L, t0 + o:t0 + o + nb, :], O["v_p"][L, t0 + o:t0 + o + nb, :], O["l_p"][L, t0 + o:t0 + o + nb, :]
                      else:
                          ko, vo, lo = O["k_s"][L, sq.idx, o:o + nb, :], O["v_s"][L, sq.idx, o:o + nb, :], O["l_s"][L, sq.idx, o:o + nb, :]
                      DMA(ko, kv.v[0:nb, 0:384], [kv.u], [uOUT], kv.name + "_k")
                      DMA(vo, kv.v[0:nb, 384:768], [kv.u], [uOUT], kv.name + "_v")
                      P.op("dve", lambda e, nb=nb: e.tensor_tensor(out=FD.v[0:nb, :], in0=pbs[5][0:nb, 0:12], in1=PBC.v[0:nb, B_FB:B_FB + 12], op=ALU.add), psu(5, 0, 12) + [PBC.u], [FD.u])
                      P.op("act", lambda e, nb=nb: e.activation(out=E12.v[0:nb, 0:6], in_=FD.v[0:nb, 0:6], func=AF.Exp, scale=-1.0), [FD.u], [E12.u])
                      P.op("act", lambda e, nb=nb: e.activation(out=E12.v[0:nb, 6:12], in_=FD.v[0:nb, 6:12], func=AF.Exp), [FD.u], [E12.u])
                      P.op("act", lambda e, nb=nb, l12=l12: e.activation(out=l12.v[0:nb, :], in_=E12.v[0:nb, :], func=AF.Ln, bias=1.0), [E12.u], [l12.u])
                      P.op("dve", lambda e, nb=nb, l12=l12, lf=lf: e.tensor_single_scalar(out=lf.v[0:nb, :], in_=l12.v[0:nb, 0:6], scalar=-1.0, op=ALU.mult), [l12.u], [lf.u])
                      DMA(lo, lf.v[0:nb, :], [lf.u], [uOUT], lf.name)
                      P.op("act", lambda e, nb=nb: e.activation(out=ZE.v[0:nb, :], in_=pbs[5][0:nb, 128:512], func=AF.Exp, scale=-1.0), pu[5], [ZE.u])
                      P.op("act", lambda e, nb=nb: e.activation(out=ZE.v[0:nb, :], in_=ZE.v[0:nb, :], func=AF.Ln, bias=1.0), [ZE.u], [ZE.u])
                      P.op("act", lambda e, nb=nb: e.activation(out=ZE.v[0:nb, :], in_=ZE.v[0:nb, :], func=AF.Exp, scale=-1.0), [ZE.u], [ZE.u])
                      P.op("dve", lambda e, nb=nb, bi=bi: e.tensor_tensor(out=ZS.v[0:nb, bi, :], in0=pbs[5][0:nb, 128:512], in1=ZE.v[0:nb, :], op=ALU.mult), pu[5] + [ZE.u], [ZS.u])

                      if isP:
                          prev = None if kb == 0 else (kb - 1, sq.kblocks[kb - 1][1])
                      else:
                          prev = (kb - 1, 128)

                      def cmm(e, nb=nb, l12=l12, prev=prev):
                          r = e.matmul(pbs[6][0:nb, 384:390], lhsT=tri_f[0:nb, 0:nb], rhs=l12.v[0:nb, 0:6], start=True, stop=(prev is None))
                          if prev is not None:
                              pk, pn = prev
                              r = e.matmul(pbs[6][0:nb, 384:390], lhsT=sel_f(pn)[0:pn, 0:nb], rhs=NEGC.v[0:pn, pk, :], start=False, stop=True)
                          return r
                      P.op("pe", cmm, [l12.u, CF.u, NEGC.u], psu(6, 256, 512))
                      P.op("act", lambda e, nb=nb, kb=kb: e.activation(out=NEGC.v[0:nb, kb, :], in_=pbs[6][0:nb, 384:390], func=AF.Copy), psu(6, 256, 512), [NEGC.u])
                      P.op("dve", lambda e, nb=nb, l12=l12: e.tensor_single_scalar(out=STA.v[0:nb, 0, 0:128:32], in_=l12.v[0:nb, 0:4], scalar=-1.0, op=ALU.mult), [l12.u], [STA.u])
                      P.op("dve", lambda e, nb=nb, l12=l12: e.tensor_single_scalar(out=STA.v[0:nb, 1, 0:64:32], in_=l12.v[0:nb, 4:6], scalar=-1.0, op=ALU.mult), [l12.u], [STA.u])
                      if prev is not None:
                          pk, pn = prev
                          P.op("dve", lambda e, pk=pk, pn=pn: e.tensor_single_scalar(out=STA.v[0:pn, 2, 0:128:32], in_=NEGC.v[0:pn, pk, 0:4], scalar=-1.0, op=ALU.mult), [NEGC.u], [STA.u])
                          P.op("dve", lambda e, pk=pk, pn=pn: e.tensor_single_scalar(out=STA.v[0:pn, 3, 0:64:32], in_=NEGC.v[0:pn, pk, 4:6], scalar=-1.0, op=ALU.mult), [NEGC.u], [STA.u])

                      def smm(e, nb=nb, prev=prev):
                          r = None
                          for a in range(2):
                              r = e.matmul(pbs[6][:, 128 * a:128 * a + nb], lhsT=STA.v[0:nb, a, :], rhs=tri_b[0:nb, 0:nb], start=True, stop=(prev is None))
                              if prev is not None:
                                  pn = prev[1]
                                  r = e.matmul(pbs[6][:, 128 * a:128 * a + nb], lhsT=STA.v[0:pn, 2 + a, :], rhs=sel_b(pn)[0:pn, 0:nb], start=False, stop=True)
                          return r
                      P.op("pe", smm, [STA.u, CB.u], psu(6, 0, 256))
                      for h in range(6):
                          a, i = (0, h) if h < 4 else (1, h - 4)
                          P.op("act", lambda e, h=h, a=a, i=i, o=o, nb=nb: e.activation(out=QT.v[64:65, h, o:o + nb], in_=pbs[6][32 * i:32 * i + 1, 128 * a:128 * a + nb], func=AF.Copy),
                               psu(6, 0, 256), [QT.u])


                  mark_stage('step2a+fox')
                  def fm(c0, ncols, evac):
                      slot = fm_ctr[0] % 3
                      fm_ctr[0] += 1
                      bank, cc = (1, 2, 7)[slot], 0
                      us = [pu[bank][0]]

                      def mm(e):
                          r = None
                          for kc in range(8):
                              r = e.matmul(pbs[bank][0:ncols, cc:cc + n], lhsT=WIN.v[:, kc, c0:c0 + ncols], rhs=HT.v[:, kc, 0:n], start=(kc == 0), stop=(kc == 7))
                          return r
                      P.op("pe", mm, [WIN.u, HT.u], us)
                      evac(pbs[bank][:, cc:cc + n], us)


                  do_tm(0, blocks[0][0], blocks[0][1])
                  for j in range(3):
                      def evq(ps, us, j=j):
                          P.op("act", lambda e: e.activation(out=QT.v[0:64, 2 * j, 0:n], in_=ps[0:64, :], func=AF.Copy, scale=0.125), us, [QT.u])
                          P.op("act", lambda e: e.activation(out=QT.v[0:64, 2 * j + 1, 0:n], in_=ps[64:128, :], func=AF.Copy, scale=0.125), us, [QT.u])
                      fm(OQ + 128 * j, 128, evq)
                  for j in range(3):
                      def evk(ps, us, j=j):
                          P.op("act", lambda e: e.activation(out=KTC.v[0:64, 2 * j, 0:n], in_=ps[0:64, :], func=AF.Copy), us, [KTC.u])
                          P.op("dve", lambda e: e.tensor_copy(out=KTC.v[0:64, 2 * j + 1, 0:n], in_=ps[64:128, :]), us, [KTC.u])
                      fm(OKK + 128 * j, 128, evk)
                  if len(blocks) > 1:
                      do_tm(1, blocks[1][0], blocks[1][1])
                  for pt in range(2):
                      def evu(ps, us, pt=pt):
                          P.op("dve", lambda e: e.tensor_copy(out=UT.v[:, pt, 15:15 + n], in_=ps), us, [UT.u])
                      fm(OU + 128 * pt, 128, evu)
                  for ct in range(5):
                      def evx(ps, us, ct=ct):
                          P.op("act", lambda e: e.activation(out=XBC.v[:, ct, 3:3 + n], in_=ps, func=AF.Copy), us, [XBC.u])
                      fm(OX + 128 * ct, 128, evx)

                  mark_stage('step2b')
                  if isP and ti < len(sq.tiles) - 1:
                      for bi, (o, nb) in enumerate(blocks):
                          kb = kbs[bi]
                          DMA(KSC[kb].rearrange("p (h k) -> p h k", h=6)[:, :, 0:nb], KTC.v[:, :, o:o + nb], [KTC.u], [uKSC[kb]], f"c_ktc{bi}")
                          DMA(VSC[kb][0:nb, :], VCUR.v[0:nb, bi, :], [VCUR.u], [uKSC[kb]], f"c_vcur{bi}")

                  def gen_mix():
                      ln_ = 15 + n
                      for pt in range(2):
                          E_ = UT.v[:, pt, :]
                          P.op("dve", lambda e, E_=E_: e.tensor_tensor(out=PS2.v[:, 1:ln_], in0=E_[:, 1:ln_], in1=E_[:, 0:ln_ - 1], op=ALU.add), [UT.u], [PS2.u])
                          P.op("dve", lambda e: e.tensor_tensor(out=PS4.v[:, 3:ln_], in0=PS2.v[:, 3:ln_], in1=PS2.v[:, 1:ln_ - 2], op=ALU.add), [PS2.u], [PS4.u])
                          if pt == 0:
                              lo, hi = PS2, PS4
                          else:
                              P.op("dve", lambda e: e.tensor_tensor(out=PS2.v[:, 7:ln_], in0=PS4.v[:, 7:ln_], in1=PS4.v[:, 3:ln_ - 4], op=ALU.add), [PS4.u], [PS2.u])
                              P.op("dve", lambda e: e.tensor_tensor(out=PS4.v[:, 15:ln_], in0=PS2.v[:, 15:ln_], in1=PS2.v[:, 7:ln_ - 8], op=ALU.add), [PS2.u], [PS4.u])
                              lo, hi = PS2, PS4
                          for (r0, sb) in ((0, lo), (64, hi)):
                              if isP and ti == 0:
                                  P.op("dve", lambda e, r0=r0, sb=sb, pt=pt: e.tensor_tensor(out=ACC.v[r0:r0 + 64, 0:n], in0=sb.v[r0:r0 + 64, 15:15 + n], in1=CF.v[r0:r0 + 64, K_RC + 16 * pt:K_RC + 16 * pt + n], op=ALU.mult),
                                       [sb.u, CF.u], [ACC.u])
                                  P.op("dve", lambda e, r0=r0, pt=pt, E_=E_: e.tensor_tensor(out=DIFF.v[r0:r0 + 64, pt, 0:n], in0=ACC.v[r0:r0 + 64, 0:n], in1=E_[r0:r0 + 64, 15:15 + n], op=ALU.subtract),
                                       [ACC.u, UT.u], [DIFF.u])
                              else:
                                  P.op("dve", lambda e, r0=r0, sb=sb, pt=pt, E_=E_: e.scalar_tensor_tensor(out=DIFF.v[r0:r0 + 64, pt, 0:n], in0=sb.v[r0:r0 + 64, 15:15 + n],
                                                                                                   scalar=CF.v[r0:r0 + 64, K_RC + 16 * pt + 15:K_RC + 16 * pt + 16], in1=E_[r0:r0 + 64, 15:15 + n], op0=ALU.mult, op1=ALU.subtract),
                                       [sb.u, CF.u, UT.u], [DIFF.u])
                          P.op("pe", lambda e, pt=pt: e.matmul(pbs[7][:, 0:n], lhsT=PWB.v[:, pt, :], rhs=DIFF.v[:, pt, 0:n], start=True, stop=True), [PWB.u, DIFF.u], psu(7, 0, 256))
                          yield
                          P.op("act", lambda e, pt=pt: e.activation(out=MIXT.v[:, 3 + pt, 0:n], in_=pbs[7][:, 0:n], func=AF.Copy, scale=PCOL.v[:, C_PS + pt:C_PS + pt + 1]), psu(7, 0, 256) + [PCOL.u], [MIXT.u])
                      P.op("pool", lambda e: e.tensor_copy(out=UT.v[:, :, 0:15], in_=UT.v[:, :, n:n + 15]), [UT.u], [UT.u])

                      for ct in range(5):
                          cw = PCOL.v[:, C_CW + 4 * ct:C_CW + 4 * ct + 4]
                          cb_ = PCOL.v[:, C_CB + ct:C_CB + ct + 1]
                          P.op("pool", lambda e, ct=ct, cw=cw, cb_=cb_: e.tensor_scalar(out=ACC.v[:, 0:n], in0=XBC.v[:, ct, 0:n], scalar1=cw[:, 0:1], scalar2=cb_, op0=ALU.mult, op1=ALU.add),
                               [XBC.u, PCOL.u], [ACC.u])
                          for j in range(1, 4):
                              P.op("dve", lambda e, ct=ct, cw=cw, j=j: e.scalar_tensor_tensor(out=ACC.v[:, 0:n], in0=XBC.v[:, ct, j:j + n], scalar=cw[:, j:j + 1], in1=ACC.v[:, 0:n], op0=ALU.mult, op1=ALU.add),
                                   [XBC.u, PCOL.u], [ACC.u])
                          P.op("act", lambda e: e.activation(out=CE.v[:, 0:n], in_=ACC.v[:, 0:n], func=AF.Exp, scale=-1.0), [ACC.u], [CE.u])
                          P.op("act", lambda e: e.activation(out=CE.v[:, 0:n], in_=CE.v[:, 0:n], func=AF.Ln, bias=1.0), [CE.u], [CE.u])
                          P.op("act", lambda e: e.activation(out=CE.v[:, 0:n], in_=CE.v[:, 0:n], func=AF.Exp, scale=-1.0), [CE.u], [CE.u])
                          if ct < 3:
                              dst, du = XS.v[:, ct, 0:n], XS.u
                          elif ct == 3:
                              dst, du = BT.v[:, 0:n], BT.u
                          else:
                              dst, du = CT.v[:, 0:n], CT.u
                          P.op("dve", lambda e, dst=dst: e.tensor_tensor(out=dst, in0=ACC.v[:, 0:n], in1=CE.v[:, 0:n], op=ALU.mult), [ACC.u, CE.u], [du])
                          yield
                      P.op("pool", lambda e: e.tensor_copy(out=XBC.v[:, :, 0:3], in_=XBC.v[:, :, n:n + 3]), [XBC.u], [XBC.u])

                      for bi, (o, nb) in enumerate(blocks):
                          l12 = l12s[bi]
                          P.op("dve", lambda e, nb=nb, l12=l12: e.tensor_tensor(out=LA_.v[0:nb, :], in0=l12.v[0:nb, 6:12], in1=ABC.v[0:nb, :], op=ALU.mult), [l12.u, ABC.u], [LA_.u])
                          P.op("pe", lambda e, nb=nb: e.matmul(pbs[6][0:nb, 400:406], lhsT=tri_f[0:nb, 0:nb], rhs=LA_.v[0:nb, :], start=True, stop=True), [LA_.u, CF.u], psu(6, 256, 512))
                          P.op("act", lambda e, nb=nb: e.activation(out=NACS.v[0:nb, :], in_=pbs[6][0:nb, 400:406], func=AF.Copy, scale=-1.0), psu(6, 256, 512), [NACS.u])
                          P.op("act", lambda e, nb=nb: e.activation(out=EA.v[0:nb, :], in_=pbs[6][0:nb, 400:406], func=AF.Exp), psu(6, 256, 512), [EA.u])
                          P.op("dve", lambda e, nb=nb: e.tensor_tensor(out=XX.v[0:nb, :, 0:nb], in0=tri_f[0:nb, 0:nb].unsqueeze(1).to_broadcast([nb, 6, nb]),
                                                                      in1=LA_.v[0:nb, :].unsqueeze(2).to_broadcast([nb, 6, nb]), op=ALU.mult), [LA_.u, CF.u], [XX.u])
                          abank = lambda h: ((5, 7)[h // 4], 128 * (h % 4))

                          def amm(e, nb=nb):
                              r = None
                              for h in range(6):
                                  b_, c_ = abank(h)
                                  e.matmul(pbs[b_][:, c_:c_ + nb], lhsT=ones_f[0:nb, :], rhs=XX.v[0:nb, h, 0:nb], start=True, stop=False)
                                  r = e.matmul(pbs[b_][:, c_:c_ + nb], lhsT=ident_f[0:nb, :], rhs=mask_f[0:nb, 0:nb], start=False, stop=True)
                              return r
                          P.op("pe", amm, [XX.u, CF.u], [pu[5][0], pu[7][0]])
                          yield
                          for h in range(6):
                              b_, c_ = abank(h)
                              P.op("act", lambda e, h=h, b_=b_, c_=c_, nb=nb: e.activation(out=LT.v[0:nb, h, 0:nb], in_=pbs[b_][0:nb, c_:c_ + nb], func=AF.Exp, bias=NACS.v[0:nb, h:h + 1]),
                                   [pu[b_][0], NACS.u], [LT.u])

                          P.op("act", lambda e, nb=nb: e.activation(out=DEC.v[:, 0:4], in_=pbs[5][:, nb - 1:512:128], func=AF.Exp), [pu[5][0]], [DEC.u])
                          P.op("act", lambda e, nb=nb: e.activation(out=DEC.v[:, 4:6], in_=pbs[7][:, nb - 1:256:128], func=AF.Exp), [pu[7][0]], [DEC.u])

                          def gmm(e, o=o, nb=nb):
                              r = None
                              for g in range(2):
                                  r = e.matmul(pbs[(5, 7)[g]][0:nb, 0:nb], lhsT=BT.v[64 * g:64 * g + 64, o:o + nb], rhs=CT.v[64 * g:64 * g + 64, o:o + nb], start=True, stop=True)
                              return r
                          P.op("pe", gmm, [BT.u, CT.u], [pu[5][0], pu[7][0]])
                          yield
                          for h in range(6):
                              g = h // 3
                              P.op("dve", lambda e, h=h, g=g, nb=nb: e.tensor_tensor(out=MT.v[0:nb, h, 0:nb], in0=pbs[(5, 7)[g]][0:nb, 0:nb], in1=LT.v[0:nb, h, 0:nb], op=ALU.mult),
                                   [pu[(5, 7)[g]][0], LT.u], [MT.u])

                          def xtr(e, o=o, nb=nb):
                              r = None
                              for j in range(3):
                                  r = e.transpose(out=pbs[7][0:nb, 128 * j:128 * j + 128], in_=XS.v[:, j, o:o + nb], identity=ident_f)
                              return r
                          P.op("pe", xtr, [XS.u, CF.u], pu[7])
                          yield
                          P.op("pe", lambda e, o=o, nb=nb: e.transpose(out=pbs[0][0:nb, 0:128], in_=BT.v[:, o:o + nb], identity=ident_b), [BT.u, CB.u], pu[0])
                          P.op("act", lambda e, nb=nb: e.activation(out=XSTOK.v[0:nb, :], in_=pbs[7][0:nb, 0:384], func=AF.Copy), pu[7], [XSTOK.u])
                          P.op("dve", lambda e, nb=nb, l12=l12: e.tensor_tensor(out=XD.v[0:nb, :, :], in0=XSTOK.v[0:nb, :].rearrange("p (h d) -> p h d", h=6),
                                                                               in1=l12.v[0:nb, 6:12].unsqueeze(2).to_broadcast([nb, 6, 64]), op=ALU.mult), [XSTOK.u, l12.u], [XD.u])
                          for h in range(6):
                              g = h // 3
                              P.op("act", lambda e, h=h, g=g, nb=nb: e.activation(out=BW.v[0:nb, h, 64 * g:64 * g + 64], in_=pbs[0][0:nb, 64 * g:64 * g + 64], func=AF.Copy, scale=LT.v[0:nb, h, nb - 1:nb]),
                                   pu[0] + [LT.u], [BW.u])

                          def ymm(e, o=o, nb=nb):
                              r = None
                              for h in range(6):
                                  r = e.matmul(pbs[6][0:nb, 64 * h:64 * h + 64], lhsT=MT.v[0:nb, h, 0:nb], rhs=XD.v[0:nb, h, :], start=True, stop=True)
                              for g in range(2):
                                  r = e.matmul(pbs[(5, 7)[g]][0:nb, 0:192], lhsT=CT.v[64 * g:64 * g + 64, o:o + nb], rhs=SBF.v[64 * g:64 * g + 64, :], start=True, stop=True)
                              for h in range(6):
                                  r = e.matmul(pb0f[:, 64 * h:64 * h + 64], lhsT=BW.v[0:nb, h, :], rhs=XD.v[0:nb, h, :], start=True, stop=True)
                              return r
                          P.op("pe", ymm, [MT.u, XD.u, CT.u, SBF.u, BW.u], [pu[6][0], pu[5][0], pu[7][0], pu[0][0]])
                          yield
                          for h in range(6):
                              g, j = h // 3, h % 3
                              P.op("dve", lambda e, h=h, g=g, j=j: e.scalar_tensor_tensor(out=SST.v[64 * g:64 * g + 64, 64 * j:64 * j + 64], in0=SST.v[64 * g:64 * g + 64, 64 * j:64 * j + 64],
                                                                                       scalar=DEC.v[64 * g:64 * g + 64, h:h + 1], in1=pb0f[64 * g:64 * g + 64, 64 * h:64 * h + 64], op0=ALU.mult, op1=ALU.add),
                                   [SST.u, DEC.u] + pu[0], [SST.u])
                          P.op("dve", lambda e: e.tensor_copy(out=SBF.v, in_=SST.v), [SST.u], [SBF.u])
                          yield
                          for g in range(2):
                              P.op("dve", lambda e, nb=nb, g=g: e.tensor_tensor(out=Y1.v[0:nb, 192 * g:192 * g + 192].rearrange("p (h d) -> p h d", h=3), in0=pbs[(5, 7)[g]][0:nb, 0:192].rearrange("p (h d) -> p h d", h=3),
                                                                               in1=EA.v[0:nb, 3 * g:3 * g + 3].unsqueeze(2).to_broadcast([nb, 3, 64]), op=ALU.mult), [pu[(5, 7)[g]][0], EA.u], [Y1.u])
                          P.op("dve", lambda e, nb=nb: e.tensor_tensor(out=Y1.v[0:nb, :], in0=pbs[6][0:nb, 0:384], in1=Y1.v[0:nb, :], op=ALU.add), pu[6] + [Y1.u], [Y1.u])
                          P.op("pool", lambda e, nb=nb: e.tensor_tensor(out=Y2.v[0:nb, :].rearrange("p (h d) -> p h d", h=6), in0=XSTOK.v[0:nb, :].rearrange("p (h d) -> p h d", h=6),
                                                                       in1=PBC.v[0:nb, B_DS:B_DS + 6].unsqueeze(2).to_broadcast([nb, 6, 64]), op=ALU.mult), [XSTOK.u, PBC.u], [Y2.u])
                          P.op("dve", lambda e, nb=nb: e.tensor_tensor(out=Y1.v[0:nb, :], in0=Y1.v[0:nb, :], in1=Y2.v[0:nb, :], op=ALU.add), [Y1.u, Y2.u], [Y1.u])
                          P.op("dve", lambda e, nb=nb, bi=bi: e.tensor_tensor(out=Y1.v[0:nb, :], in0=Y1.v[0:nb, :], in1=ZS.v[0:nb, bi, :], op=ALU.mult), [Y1.u, ZS.u], [Y1.u])
                          ssq = SM.v[0:nb, 2:3]
                          rs = SM.v[0:nb, 3:4]
                          P.op("act", lambda e, nb=nb, ssq=ssq: e.activation(out=Y2.v[0:nb, :], in_=Y1.v[0:nb, :], func=AF.Square, accum_out=ssq), [Y1.u], [Y2.u, smu[1]])
                          P.op("act", lambda e, nb=nb, ssq=ssq, rs=rs: e.activation(out=rs, in_=ssq, func=AF.Ln, bias=CF.v[0:nb, K_EPS:K_EPS + 1], scale=1.0 / 384), [smu[1], CF.u], [smu[1]])
                          P.op("act", lambda e, rs=rs: e.activation(out=rs, in_=rs, func=AF.Exp, scale=-0.5), [smu[1]], [smu[1]])
                          P.op("dve", lambda e, nb=nb, rs=rs: e.scalar_tensor_tensor(out=YN.v[0:nb, :], in0=Y1.v[0:nb, :], scalar=rs, in1=PBC.v[0:nb, B_SN:B_SN + 384], op0=ALU.mult, op1=ALU.mult),
                               [Y1.u, smu[1], PBC.u], [YN.u])
                          yield

                          def ytr(e, nb=nb):
                              r = None
                              for j in range(3):
                                  r = e.transpose(out=pbs[0][:, 512 + 128 * j:512 + 128 * j + nb], in_=YN.v[0:nb, 128 * j:128 * j + 128], identity=ident_b[0:nb, 0:nb])
                              return r
                          P.op("pe", ytr, [YN.u, CB.u], pu[0])
                          yield
                          P.op("act", lambda e, o=o, nb=nb: e.activation(out=MIXT.v[:, 5:8, o:o + nb], in_=pbs[0][:, 512:896].rearrange("p (a b) -> p a b", a=3)[:, :, 0:nb], func=AF.Copy), pu[0], [MIXT.u])


                  def gen_att():
                      q0 = sq.npast + t0
                      vis = []
                      for kb, (ks, kn) in enumerate(sq.kblocks):
                          if ks > q0 + n - 1:
                              break
                          i0 = max(0, ks - q0)
                          diag = ks + kn - 1 > q0
                          vis.append((kb, ks, kn, i0, diag))
                      otu = lambda h: (5 + h // 2, 256 * (h % 2))
                      pend = []
                      SKEW = 3
                      for hp in range(1):
                          heads = list(range(6))
                          for vi, (kb, ks, kn, i0, diag) in enumerate(vis):
                              nq = n - i0
                              cur = ks >= q0
                              if cur:
                                  bi = (ks - q0) // 128
                                  kt_ap = KTC.v[:, :, ks - q0:ks - q0 + kn]
                                  v_ap = VCUR.v[0:kn, bi, :]
                                  kvu = [KTC.u, VCUR.u]
                              else:
                                  r_ = ring_ctr[0] % NRING
                                  ring_ctr[0] += 1
                                  rk, rv = KVRk[r_], KVRv[r_]
                                  kt_ap = rk.v[:, :, 0:kn]
                                  v_ap = rv.v[0:kn, :]
                                  kvu = [rk.u, rv.u]
                                  if isP:
                                      DMA(rk.v[:, :, 0:kn], KSC[kb].rearrange("p (h k) -> p h k", h=6)[:, :, 0:kn], [uKSC[kb]], [rk.u], rk.name)
                                      DMA(rv.v[0:kn, :], VSC[kb][0:kn, :], [uKSC[kb]], [rv.u], rv.name)
                                  else:
                                      ck_, cv_ = CKS[cks_ctr[0] % 2], CVS[cks_ctr[0] % 2]
                                      cks_ctr[0] += 1
                                      DMA(ck_.v, I["ck"][L, sq.idx, ks:ks + 128, :], [uIN], [ck_.u], ck_.name)
                                      DMA(cv_.v, I["cv"][L, sq.idx, ks:ks + 128, :], [uIN], [cv_.u], cv_.name)
                                      P.op("dve", lambda e, ck_=ck_: e.tensor_copy(out=KB16.v, in_=ck_.v), [ck_.u], [KB16.u])

                                      def ktr(e):
                                          r = None
                                          for j in range(3):
                                              r = e.transpose(out=pbs[0][:, 128 * j:128 * j + 128], in_=KB16.v[:, 128 * j:128 * j + 128], identity=ident_b)
                                          return r
                                      P.op("pe", ktr, [KB16.u, CB.u], pu[0])
                                      for h in range(6):
                                          P.op("act", lambda e, h=h, rk=rk: e.activation(out=rk.v[0:64, h, :], in_=pbs[0][64 * (h % 2):64 * (h % 2) + 64, 128 * (h // 2):128 * (h // 2) + 128], func=AF.Copy),
                                               pu[0], [rk.u])
                                      vsrc = cv_.v.rearrange("p (j t d) -> p j t d", j=3, t=2)
                                      vdst = rv.v.rearrange("p (j x) -> p j x", j=3)
                                      P.op("dve", lambda e, vsrc=vsrc, vdst=vdst: e.tensor_copy(out=vdst[:, :, 0:64], in_=vsrc[:, :, 0, :]), [cv_.u], [rv.u])
                                      P.op("pool", lambda e, vsrc=vsrc, vdst=vdst: e.tensor_copy(out=vdst[:, :, 128:192], in_=vsrc[:, :, 1, :]), [cv_.u], [rv.u])
                              if vi == 0:
                                  for h in heads:
                                      ob, oc = otu(h)
                                      P.op("dve", lambda e, ob=ob, oc=oc: e.memset(pbs[ob][:, oc:oc + n], 0.0), [], [pu[ob][0]])
                              for h in heads:
                                  sl = st_ctr[0] % 4
                                  st_ctr[0] += 1
                                  sb_, sc_ = 1 + sl, 0
                                  sus = [pu[sb_][0]]
                                  mw = min(kn, nq)

                                  def smm2(e, h=h, kt_ap=kt_ap, sb_=sb_, sc_=sc_, kn=kn, nq=nq, i0=i0, diag=diag, mw=mw):
                                      r = e.matmul(pbs[sb_][0:kn, sc_:sc_ + nq], lhsT=kt_ap[:, h, :], rhs=QT.v[:, h, i0:n], start=True, stop=not diag)
                                      if diag:
                                          r = e.matmul(pbs[sb_][0:kn, sc_:sc_ + mw], lhsT=ident_b[0:kn, 0:kn], rhs=mask_b[0:kn, 0:mw], start=False, stop=True)
                                      return r
                                  P.op("pe", smm2, kvu + [QT.u, CB.u], sus)
                                  pt_ = PT[pt_ctr[0] % 4]
                                  pt_ctr[0] += 1
                                  P.op("act", lambda e, h=h, pt_=pt_, sb_=sb_, sc_=sc_, kn=kn, nq=nq, kb=kb: e.activation(out=pt_.v[0:kn, 0:nq], in_=pbs[sb_][0:kn, sc_:sc_ + nq], func=AF.Exp, bias=NEGC.v[0:kn, kb, h:h + 1]),
                                       sus + [NEGC.u], [pt_.u])
                                  ob, oc = otu(h)
                                  vc0 = 192 * (h // 2) + 64 * (h % 2)
                                  pend.append((lambda e, pt_=pt_, ob=ob, oc=oc, v_ap=v_ap, vc0=vc0, kn=kn, nq=nq, i0=i0: e.matmul(pbs[ob][:, oc + i0:oc + n], lhsT=v_ap[:, vc0:vc0 + 128], rhs=pt_.v[0:kn, 0:nq],
                                                                                                                      start=False, stop=False, skip_group_check=True),
                                               kvu + [pt_.u], [pu[ob][0]]))
                                  if len(pend) > SKEW:
                                      f_, r_s, w_s = pend.pop(0)
                                      P.op("pe", f_, r_s, w_s)
                                  yield
                          while pend:
                              f_, r_s, w_s = pend.pop(0)
                              P.op("pe", f_, r_s, w_s)
                          for h in heads:
                              ob, oc = otu(h)
                              rc_ = REC[h % 2]
                              nr, dr = (0, 64) if h % 2 == 0 else (64, 0)
                              P.op("act", lambda e, ob=ob, oc=oc, rc_=rc_, dr=dr: e.activation(out=rc_.v[dr:dr + 64, 0:n], in_=pbs[ob][dr:dr + 64, oc:oc + n], func=AF.Ln), [pu[ob][0]], [rc_.u])
                              P.op("act", lambda e, rc_=rc_, dr=dr: e.activation(out=rc_.v[dr:dr + 64, 0:n], in_=rc_.v[dr:dr + 64, 0:n], func=AF.Exp, scale=-1.0), [rc_.u], [rc_.u])
                              P.op("dve", lambda e, h=h, ob=ob, oc=oc, rc_=rc_, nr=nr, dr=dr: e.tensor_tensor(out=MIXT.v[nr:nr + 64, h // 2, 0:n], in0=pbs[ob][nr:nr + 64, oc:oc + n], in1=rc_.v[dr:dr + 64, 0:n], op=ALU.mult),
                                   [pu[ob][0], rc_.u], [MIXT.u])
                          yield

                  if ti + 1 < len(sq.tiles):
                      nsq, nti = sq, ti + 1
                  else:
                      k_ = seqs.index(sq)
                      nsq, nti = (seqs[k_ + 1], 0) if k_ + 1 < len(seqs) else (None, 0)
                  if nsq is not None:
                      s1cache[(nsq.kind, nsq.idx, nti)] = do_step1(nsq, nsq.tiles[nti][0], nsq.tiles[nti][1])
                  for _ in gen_mix():
                      pass
                  for _ in gen_att():
                      pass


                  mark_stage('attn')
                  for bi, (o, nb) in enumerate(blocks):
                      obk = (6, 7) if bi % 2 == 0 else (1, 2)

                      def omm(e, o=o, nb=nb, obk=obk):
                          r = None
                          for c in range(2):
                              for kt in range(8):
                                  r = e.matmul(pbs[obk[c]][0:nb, :], lhsT=MIXT.v[:, kt, o:o + nb], rhs=WOUT.v[:, kt, 512 * c:512 * c + 512], start=(kt == 0), stop=(kt == 7))
                          return r
                      P.op("pe", omm, [MIXT.u, WOUT.u], pu[obk[0]] + pu[obk[1]])
                      xo = XOUT[xo_ctr[0] % 2]
                      xo_ctr[0] += 1
                      post_norm_res(P, pbs, pu, obk[0], obk[1], nb, xins[bi], xo, B_PM, JUNK, SM, smu, CF, PBC, T1)
                      r = sq.row0 + t0 + o
                      DMA(XA[r:r + nb, :], xo.v[0:nb, :], [xo.u], [uXA[r // 128]], xo.name)

              mark_stage('seq-tiles-done')
              s = sq.idx
              for pt in range(2):
                  P.op("pe", lambda e, pt=pt: e.transpose(out=pbs[7][0:15, 128 * pt:128 * pt + 128], in_=UT.v[:, pt, 0:15], identity=ident_f), [UT.u, CF.u], pu[7])
              P.op("act", lambda e: e.activation(out=STT.v[0:15, 0:256], in_=pbs[7][0:15, 0:256], func=AF.Copy), pu[7], [STT.u])
              DMA(O["pool_p"][L] if isP else O["pool_s"][L, s], STT.v[0:15, 0:256], [STT.u], [uOUT], "c_stt")
              for ct in range(5):
                  P.op("pe", lambda e, ct=ct: e.transpose(out=pbs[3 + ct // 4][0:3, 128 * (ct % 4):128 * (ct % 4) + 128], in_=XBC.v[:, ct, 0:3], identity=ident_f), [XBC.u, CF.u], pu[3] + pu[4])
              P.op("act", lambda e: e.activation(out=T1.v[0:3, 0:512], in_=pbs[3][0:3, 0:512], func=AF.Copy), pu[3], [T1.u])
              P.op("act", lambda e: e.activation(out=T1.v[0:3, 512:640], in_=pbs[4][0:3, 0:128], func=AF.Copy), pu[4], [T1.u])
              DMA(O["conv_p"][L] if isP else O["conv_s"][L, s], T1.v[0:3, 0:640], [T1.u], [uOUT], "c_t1")
              for j in range(3):
                  P.op("pe", lambda e, j=j: e.transpose(out=pbs[5][0:64, 128 * j:128 * j + 128], in_=SST.v[:, 64 * j:64 * j + 64], identity=ident_f), [SST.u, CF.u], pu[5])
              P.op("act", lambda e: e.activation(out=SIN.v[0:64, :, :], in_=pbs[5][0:64, 0:384].rearrange("p (a b) -> p a b", a=3), func=AF.Copy), pu[5], [SIN.u])
              od = O["ssd_p"][L] if isP else O["ssd_s"][L, s]
              for j in range(3):
                  DMA(od[j], SIN.v[0:64, j, 0:64], [SIN.u], [uOUT], f"c_so{j}a")
                  DMA(od[j + 3], SIN.v[0:64, j, 64:128], [SIN.u], [uOUT], f"c_so{j}b")

          mark_stage('phaseA-done')
          P.barrier()
          AR.off = mark
          WG = AR.alloc("wg", [8, DFF], BF16)
          WU = AR.alloc("wu", [8, DFF], BF16)
          WD = AR.alloc("wd", [NFC, D], BF16)
          H1 = AR.alloc("h1", [NFC, 256], BF16)
          GE = [AR.alloc(f"ge{i}", [256], F32) for i in range(3)]
          for kc in range(8):
              for c0 in range(0, DFF, WCH):
                  c1 = min(DFF, c0 + WCH)
                  load_cast(WG.v[:, kc, c0:c1], WG.u, I["w_gate"][L, kc * 128:(kc + 1) * 128, c0:c1], c1 - c0, PCOL.v[:, C_GF + kc:C_GF + kc + 1])
                  load_cast(WU.v[:, kc, c0:c1], WU.u, I["w_up"][L, kc * 128:(kc + 1) * 128, c0:c1], c1 - c0, PCOL.v[:, C_GF + kc:C_GF + kc + 1])
          for fc in range(NFC):
              for c0 in range(0, D, WCH):
                  load_cast(WD.v[:, fc, c0:c0 + WCH], WD.u, I["w_down"][L, fc * 128:(fc + 1) * 128, c0:c0 + WCH], WCH)

          P.barrier()
          btiles = [(0, META)] + [(META + 256 * j, 256) for j in range(NPT)] + [(T, NSEQ * DSEQ)]
          xin_ctr = [0]
          xo_ctr = [0]
          gu_ctr = [0]
          HT2 = AR.alloc("hT2", [8, 256], BF16)
          HTS = [HT, HT2]
          bstate = {}

          def b_front(ti):
              r0, n = btiles[ti]
              HTc = HTS[ti % 2]
              blocks = [(o, min(128, n - o)) for o in range(0, n, 128)]
              xins = []
              for bi, (o, nb) in enumerate(blocks):
                  xin = XIN[xin_ctr[0] % 4]
                  xin_ctr[0] += 1
                  xins.append(xin)
                  r = r0 + o
                  DMA(xin.v[0:nb, :], XA[r:r + nb, :], [uXA[r // 128]] + ([uXA[(r + nb - 1) // 128]] if (r + nb - 1) // 128 != r // 128 else []), [xin.u], xin.name)
                  hb = HBF[bi % 2]
                  ssq = SM.v[0:nb, 0:1]
                  rs = SM.v[0:nb, 1:2]
                  P.op("act", lambda e, xin=xin, nb=nb, ssq=ssq: e.activation(out=JUNK.v[0:nb, :], in_=xin.v[0:nb, :], func=AF.Square, accum_out=ssq), [xin.u], [JUNK.u, smu[0]])
                  P.op("act", lambda e, nb=nb, ssq=ssq, rs=rs: e.activation(out=rs, in_=ssq, func=AF.Ln, bias=CF.v[0:nb, K_EPS:K_EPS + 1], scale=1.0 / D), [smu[0], CF.u], [smu[0]])
                  P.op("act", lambda e, rs=rs: e.activation(out=rs, in_=rs, func=AF.Exp, scale=-0.5), [smu[0]], [smu[0]])
                  P.op("dve", lambda e, hb=hb, xin=xin, nb=nb, rs=rs: e.tensor_scalar(out=hb.v[0:nb, :], in0=xin.v[0:nb, :], scalar1=rs, scalar2=CF.v[0:nb, K_Z:K_Z + 1], op0=ALU.mult, op1=ALU.add),
                       [xin.u, smu[0], CF.u], [hb.u])

                  def tr(e, hb=hb, nb=nb):
                      r_ = None
                      for kc in range(8):
                          r_ = e.transpose(out=pbs[0][:, kc * 128:kc * 128 + nb], in_=hb.v[0:nb, kc * 128:(kc + 1) * 128], identity=ident_b[0:nb, 0:nb])
                      return r_
                  P.op("pe", tr, [hb.u, CB.u], pu[0])
                  P.op("act", lambda e, o=o, nb=nb: e.activation(out=HTc.v[:, :, o:o + nb], in_=pbs[0][:, :].rearrange("p (a b) -> p a b", a=8)[:, :, 0:nb], func=AF.Copy), pu[0], [HTc.u])
              bstate[ti] = (blocks, xins)

          def b_gu(ti):
              r0, n = btiles[ti]
              HTc = HTS[ti % 2]
              for fc in range(NFC):
                  sl = gu_ctr[0] % 3
                  gu_ctr[0] += 1
                  (gb_, gc_), (ub_, uc_) = (1 + sl, 0), (1 + sl, 256)
                  gus = [pu[gb_][0], pu[ub_][0]]

                  def gmm2(e, fc=fc, gb_=gb_, gc_=gc_, ub_=ub_, uc_=uc_):
                      r_ = None
                      for kc in range(8):
                          r_ = e.matmul(pbs[gb_][:, gc_:gc_ + n], lhsT=WG.v[:, kc, 128 * fc:128 * fc + 128], rhs=HTc.v[:, kc, 0:n], start=(kc == 0), stop=(kc == 7))
                      for kc in range(8):
                          r_ = e.matmul(pbs[ub_][:, uc_:uc_ + n], lhsT=WU.v[:, kc, 128 * fc:128 * fc + 128], rhs=HTc.v[:, kc, 0:n], start=(kc == 0), stop=(kc == 7))
                      return r_
                  P.op("pe", gmm2, [WG.u, WU.u, HTc.u], gus)
                  ge = GE[sl]
                  P.op("act", lambda e, ge=ge, gb_=gb_, gc_=gc_: e.activation(out=ge.v[:, 0:n], in_=pbs[gb_][:, gc_:gc_ + n], func=AF.Exp, scale=-1.0), [gus[0]], [ge.u])
                  P.op("act", lambda e, ge=ge: e.activation(out=ge.v[:, 0:n], in_=ge.v[:, 0:n], func=AF.Ln, bias=1.0), [ge.u], [ge.u])
                  P.op("act", lambda e, ge=ge: e.activation(out=ge.v[:, 0:n], in_=ge.v[:, 0:n], func=AF.Exp, scale=-1.0), [ge.u], [ge.u])
                  P.op("dve", lambda e, ge=ge, gb_=gb_, gc_=gc_: e.tensor_tensor(out=ge.v[:, 0:n], in0=pbs[gb_][:, gc_:gc_ + n], in1=ge.v[:, 0:n], op=ALU.mult), [gus[0], ge.u], [ge.u])
                  P.op("dve", lambda e, ge=ge, ub_=ub_, uc_=uc_, fc=fc: e.tensor_tensor(out=H1.v[:, fc, 0:n], in0=pbs[ub_][:, uc_:uc_ + n], in1=ge.v[:, 0:n], op=ALU.mult), [gus[1], ge.u], [H1.u])

          def b_down(ti):
              r0, n = btiles[ti]
              blocks, xins = bstate[ti]
              for bi, (o, nb) in enumerate(blocks):
                  ba, bb = (4, 5) if bi % 2 == 0 else (6, 7)

                  def dmm(e, o=o, nb=nb, ba=ba, bb=bb):
                      r_ = None
                      for c, bk in ((0, ba), (1, bb)):
                          for fc in range(NFC):
                              r_ = e.matmul(pbs[bk][0:nb, :], lhsT=H1.v[:, fc, o:o + nb], rhs=WD.v[:, fc, 512 * c:512 * c + 512], start=(fc == 0), stop=(fc == NFC - 1))
                      return r_
                  P.op("pe", dmm, [H1.u, WD.u], pu[ba] + pu[bb])
                  xo = XOUT[xo_ctr[0] % 2]
                  xo_ctr[0] += 1
                  post_norm_res(P, pbs, pu, ba, bb, nb, xins[bi], xo, B_PF, JUNK, SM, smu, CF, PBC, T1)
                  r = r0 + o
                  if L < DEPTH - 1:
                      DMA(XB[r:r + nb, :], xo.v[0:nb, :], [xo.u], [uXB[r // 128], uXB[(r + nb - 1) // 128]], xo.name)
                  else:
                      if r0 == 0:
                          pass
                      elif r0 < T:
                          DMA(O["y_p"][r - META:r - META + nb, :], xo.v[0:nb, :], [xo.u], [uOUT], xo.name)
                      else:
                          DMA(O["y_s"][r - T:r - T + nb, :], xo.v[0:nb, :], [xo.u], [uOUT], xo.name)


          b_front(0)
          for ti in range(len(btiles)):
              b_gu(ti)
              if ti + 1 < len(btiles):
                  b_front(ti + 1)
              b_down(ti)
    except _Stop:
        pass
    P.barrier()
    semkeys = list(ENGS) + sorted(P.dcnt.keys())
    sems = {k: es.enter_context(nc.semaphore(k)) for k in semkeys}

    def replay(qn, e):
        for waits, fn, ev, inc in P.q[qn]:
            for k, v in waits:
                e.wait_ge(sems[k], v)
            if fn is None:
                continue
            ins = None
            for m_, a_, kw_ in fn:
                ins = getattr(e, m_)(*a_, **kw_)
            ins.then_inc(sems[ev[0]], inc)

    with nc.Block() as block:
        @block.sync
        def _(e):
            replay("sp", e)

        @block.tensor
        def _(e):
            replay("pe", e)

        @block.scalar
        def _(e):
            replay("act", e)

        @block.vector
        def _(e):
            replay("dve", e)

        @block.gpsimd
        def _(e):
            replay("pool", e)
    es.close()
    return nc


def post_norm_res(P, pbs, pu, ba, bb, nb, xin, xo, goff, JUNK, SM, smu, CF, PBC, T1):
    s0 = SM.v[0:nb, 4:5]
    s1 = SM.v[0:nb, 5:6]
    rs = SM.v[0:nb, 6:7]
    P.op("act", lambda e: e.activation(out=JUNK.v[0:nb, 0:512], in_=pbs[ba][0:nb, :], func=AF.Square, accum_out=s0), pu[ba], [JUNK.u, smu[2]])
    P.op("act", lambda e: e.activation(out=JUNK.v[0:nb, 512:1024], in_=pbs[bb][0:nb, :], func=AF.Square, accum_out=s1), pu[bb], [JUNK.u, smu[2]])
    P.op("dve", lambda e: e.tensor_tensor(out=s0, in0=s0, in1=s1, op=ALU.add), [smu[2]], [smu[2]])
    P.op("act", lambda e: e.activation(out=rs, in_=s0, func=AF.Ln, bias=CF.v[0:nb, K_EPS:K_EPS + 1], scale=1.0 / D), [smu[2], CF.u], [smu[2]])
    P.op("act", lambda e: e.activation(out=rs, in_=rs, func=AF.Exp, scale=-0.5), [smu[2]], [smu[2]])
    P.op("dve", lambda e: e.scalar_tensor_tensor(out=T1.v[0:nb, 0:512], in0=pbs[ba][0:nb, :], scalar=rs, in1=PBC.v[0:nb, goff:goff + 512], op0=ALU.mult, op1=ALU.mult),
         pu[ba] + [smu[2], PBC.u], [T1.u])
    P.op("dve", lambda e: e.scalar_tensor_tensor(out=T1.v[0:nb, 512:1024], in0=pbs[bb][0:nb, :], scalar=rs, in1=PBC.v[0:nb, goff + 512:goff + 1024], op0=ALU.mult, op1=ALU.mult),
         pu[bb] + [smu[2], PBC.u], [T1.u])
    P.op("dve", lambda e: e.tensor_tensor(out=xo.v[0:nb, :], in0=T1.v[0:nb, :], in1=xin.v[0:nb, :], op=ALU.add), [T1.u, xin.u], [xo.u])


def _consts():
    cf = np.zeros((128, NCF), np.float32)
    p = np.arange(128)
    cf[:, K_ID:K_ID + 128] = np.eye(128, dtype=np.float32)
    cf[:, K_TRI:K_TRI + 128] = (p[:, None] <= p[None, :]).astype(np.float32)
    cf[:, K_ONE:K_ONE + 128] = 1.0
    cf[15, K_S16:K_S16 + 128] = 1.0
    cf[31, K_S32:K_S32 + 128] = 1.0
    cf[127, K_S128:K_S128 + 128] = 1.0
    cf[:, K_MSK:K_MSK + 128] = np.where(p[:, None] <= p[None, :], 0.0, NEG).astype(np.float32)
    for pt in range(2):
        for pp in range(128):
            g = 2 * pt + (1 if pp >= 64 else 0)
            w = 2 ** (g + 1)
            for pos in range(16):
                cf[pp, K_RC + 16 * pt + pos] = 1.0 / min(pos + 1, w)
    cf[:, K_Z] = 0.0
    cf[:, K_1] = 1.0
    cf[:, K_EPS] = EPS
    cb = np.zeros((128, NCB), np.float32)
    cb[:, KB_ID:KB_ID + 128] = cf[:, K_ID:K_ID + 128]
    cb[:, KB_MSK:KB_MSK + 128] = cf[:, K_MSK:K_MSK + 128]
    cb[:, KB_TRI:KB_TRI + 128] = cf[:, K_TRI:K_TRI + 128]
    cb[:, KB_S16:KB_S16 + 128] = cf[:, K_S16:K_S16 + 128]
    cb[:, KB_S32:KB_S32 + 128] = cf[:, K_S32:K_S32 + 128]
    cb[:, KB_S128:KB_S128 + 128] = cf[:, K_S128:K_S128 + 128]
    return cf, cb.astype(ml_dtypes.bfloat16)


_CACHE = {}


def kernel(x_prompt, x_sample, cache_fox_k, cache_fox_v, cache_fox_logf, state_pool, state_conv,
           state_ssd, meta_tokens, ln_pre_mix, ln_post_mix, ln_pre_ffn, ln_post_ffn, w_in,
           fox_f_bias, pool_w, pool_scale, conv_w, conv_b, dt_bias, a_log, d_skip, ssd_norm,
           w_out, w_gate, w_up, w_down):
    f = lambda a: np.ascontiguousarray(np.asarray(a, dtype=np.float32))
    x_prompt, x_sample = f(x_prompt), f(x_sample)
    BATCH, SEQ, _ = x_prompt.shape
    DEPTH, DB, PAST = cache_fox_k.shape[0], cache_fox_k.shape[1], cache_fox_k.shape[2]
    NCORE = 8
    assert DB == NCORE * NSEQ and x_sample.shape[1] == DSEQ
    T = META + SEQ
    key = (SEQ, PAST, DEPTH)
    if key not in _CACHE:
        _CACHE[key] = build(SEQ, PAST, DEPTH)
    nc = _CACHE[key]
    perm = np.concatenate([np.arange(0, 384), np.arange(384, 768), np.arange(768, 1152), np.arange(1152, 1158),
                           np.arange(2438, 2444), np.arange(1414, 1798), np.arange(1158, 1414), np.arange(1798, 2438)])
    w_in_r = np.ascontiguousarray(f(w_in)[:, :, perm])
    pbc = np.zeros((DEPTH, 128, NBC), np.float32)
    pcol = np.zeros((DEPTH, 128, NCOL), np.float32)
    poolw = np.zeros((DEPTH, 128, 256), np.float32)
    for l in range(DEPTH):
        pbc[l, :, B_PM:B_PM + 1024] = f(ln_post_mix)[l][None]
        pbc[l, :, B_PF:B_PF + 1024] = f(ln_post_ffn)[l][None]
        pbc[l, :, B_SN:B_SN + 384] = f(ssd_norm)[l][None]
        pbc[l, :, B_FB:B_FB + 6] = f(fox_f_bias)[l][None]
        pbc[l, :, B_FB + 6:B_FB + 12] = f(dt_bias)[l][None]
        pbc[l, :, B_AL:B_AL + 6] = f(a_log)[l][None]
        pbc[l, :, B_DS:B_DS + 6] = f(d_skip)[l][None]
        pcol[l, :, C_GM:C_GM + 8] = f(ln_pre_mix)[l].reshape(8, 128).T
        pcol[l, :, C_GF:C_GF + 8] = f(ln_pre_ffn)[l].reshape(8, 128).T
        pcol[l, :, C_PS:C_PS + 2] = f(pool_scale)[l].reshape(2, 128).T
        cw = f(conv_w)[l]
        for ct in range(5):
            pcol[l, :, C_CW + 4 * ct:C_CW + 4 * ct + 4] = cw[:, 128 * ct:128 * ct + 128].T
        pcol[l, :, C_CB:C_CB + 5] = f(conv_b)[l].reshape(5, 128).T
        pw = f(pool_w)[l]
        for g in range(4):
            pt, hf = g // 2, g % 2
            poolw[l, 64 * hf:64 * hf + 64, 128 * pt + 64 * hf:128 * pt + 64 * hf + 64] = pw[g]
    cf, cb = _consts()
    shared = dict(w_in=w_in_r, w_out=f(w_out), w_gate=f(w_gate), w_up=f(w_up), w_down=f(w_down),
                  pbc=pbc, pcol=pcol, poolw=poolw, cstf=cf, cstb=cb, meta=f(meta_tokens))
    ck, cv, cl = f(cache_fox_k), f(cache_fox_v), f(cache_fox_logf)
    sp_, sc_, ss_ = f(state_pool), f(state_conv), f(state_ssd)
    in_maps = []
    for c in range(NCORE):
        sl = slice(NSEQ * c, NSEQ * c + NSEQ)
        m = dict(shared)
        m["xp"] = x_prompt[c % BATCH]
        m["xs"] = np.ascontiguousarray(x_sample[sl].reshape(NSEQ * DSEQ, D))
        m["ck"] = np.ascontiguousarray(ck[:, sl].reshape(DEPTH, NSEQ, PAST, 384))
        m["cv"] = np.ascontiguousarray(cv[:, sl].reshape(DEPTH, NSEQ, PAST, 384))
        m["cl"] = np.ascontiguousarray(cl[:, sl])
        m["spool"] = np.ascontiguousarray(sp_[:, sl])
        m["sconv"] = np.ascontiguousarray(sc_[:, sl])
        m["sssd"] = np.ascontiguousarray(ss_[:, sl])
        in_maps.append(m)
    res = run_bass_kernel_spmd(nc, in_maps, core_ids=list(range(NCORE)))
    R = res.results
    y_prompt = np.stack([R[b]["y_p"] for b in range(BATCH)], 0)
    y_sample = np.concatenate([R[c]["y_s"].reshape(NSEQ, DSEQ, D) for c in range(NCORE)], 0)
    stp = lambda k, shp: np.stack([R[b][k].reshape(shp) for b in range(BATCH)], 1)
    sts = lambda k, shp: np.concatenate([R[c][k].reshape(shp) for c in range(NCORE)], 1)
    outs = (
        y_prompt, y_sample,
        stp("k_p", (DEPTH, T, 6, 64)), stp("v_p", (DEPTH, T, 6, 64)), stp("l_p", (DEPTH, T, 6)),
        stp("pool_p", (DEPTH, 15, 256)), stp("conv_p", (DEPTH, 3, 640)), stp("ssd_p", (DEPTH, 6, 64, 64)),
        sts("k_s", (DEPTH, NSEQ, DSEQ, 6, 64)), sts("v_s", (DEPTH, NSEQ, DSEQ, 6, 64)), sts("l_s", (DEPTH, NSEQ, DSEQ, 6)),
        sts("pool_s", (DEPTH, NSEQ, 15, 256)), sts("conv_s", (DEPTH, NSEQ, 3, 640)), sts("ssd_s", (DEPTH, NSEQ, 6, 64, 64)),
    )
    return tuple(np.ascontiguousarray(o, dtype=np.float32) for o in outs)
```
